# Optimizing a Trainium2 kernel written in Bass

```python
import jax
import jax.numpy as jnp
from jax import lax
import numpy as np

D_MODEL = 1024
BATCH = 4
SEQ = 4096
DEPTH = 4

GRID_W = 64
CTX_LEN = 256
HEAD_DIM = 64
GROUP_WIDTH = D_MODEL // 4
N_HEADS_A = GROUP_WIDTH // HEAD_DIM
N_HEADS_B = GROUP_WIDTH // HEAD_DIM
N_HEADS_C = GROUP_WIDTH // HEAD_DIM
N_KV_C = N_HEADS_C // 2
KV_WIDTH_C = N_KV_C * HEAD_DIM
DECAY_RANK = GROUP_WIDTH // 8
ICLR_RANK = GROUP_WIDTH // 8
GATE_RANK = GROUP_WIDTH // 4
RWKV_GN_EPS = 1e-5 * HEAD_DIM
NA_ROWS = 8
NA_COLS = 16
Q_BLOCK = 128
ROPE_THETA = 10000.0
POOL_WINDOWS = (2, 4, 8, 16)
POOL_GROUP = GROUP_WIDTH // 4
D_FF = 4 * D_MODEL
NORM_EPS = 1e-6
N_MOD = 6
PROJ_SPLITS = (GROUP_WIDTH, GROUP_WIDTH, GROUP_WIDTH, DECAY_RANK, DECAY_RANK, ICLR_RANK, ICLR_RANK, GATE_RANK,
               GROUP_WIDTH, GROUP_WIDTH, GROUP_WIDTH,
               GROUP_WIDTH, KV_WIDTH_C, KV_WIDTH_C,
               GROUP_WIDTH)
D_IN = sum(PROJ_SPLITS)
SLICE_A = slice(0, 8)
SLICE_B = slice(8, 11)
SLICE_C = slice(11, 14)
INDEX_D = 14

kernel_name = 'hybrid_rwkv7_natten_gqa_pool_dit_trunk'


def rms_norm(x, gain):
    xf = x.astype(jnp.float32)
    y = xf * lax.rsqrt(jnp.mean(xf * xf, axis=-1, keepdims=True) + NORM_EPS)
    return (y * gain.astype(jnp.float32)).astype(x.dtype)


def modulate(h, shift, scale):
    return h * (1.0 + scale) + shift


def split_projection(p):
    points = [int(s) for s in np.cumsum(PROJ_SPLITS)[:-1]]
    return jnp.split(p, points, axis=-1)


def heads_first(t, n_heads):
    b, l, _ = t.shape
    return t.reshape(b, l, n_heads, HEAD_DIM).transpose(0, 2, 1, 3)


def heads_last(t):
    b, h, l, d = t.shape
    return t.transpose(0, 2, 1, 3).reshape(b, l, h * d)


def axial_rope_tables(n_tokens):
    t = jnp.arange(n_tokens)
    row = (t // GRID_W).astype(jnp.float32)
    col = (t % GRID_W).astype(jnp.float32)
    n_freq = HEAD_DIM // 4
    inv_freq = ROPE_THETA ** (-jnp.arange(n_freq, dtype=jnp.float32) / n_freq)
    ang = jnp.concatenate([row[:, None] * inv_freq, col[:, None] * inv_freq], axis=-1)
    return jnp.cos(ang), jnp.sin(ang)


def apply_rope(x, cos, sin):
    xf = x.astype(jnp.float32)
    x1, x2 = jnp.split(xf, 2, axis=-1)
    return jnp.concatenate([x1 * cos - x2 * sin, x1 * sin + x2 * cos], axis=-1).astype(x.dtype)


def softmax_attention(q, k, v):
    s = jnp.einsum('bhqd,bhkd->bhqk', q, k).astype(jnp.float32) * HEAD_DIM ** -0.5
    p = jax.nn.softmax(s, axis=-1).astype(v.dtype)
    return jnp.einsum('bhqk,bhkd->bhqd', p, v)


def gqa_block_sweep(q, k, v):
    b, hq, lq, d = q.shape
    hkv = k.shape[1]
    groups = hq // hkv
    n_blocks = lq // Q_BLOCK
    qb = jnp.moveaxis(q.reshape(b, hkv, groups, n_blocks, Q_BLOCK, d), 3, 0)

    def one_block(q_blk):
        s = jnp.einsum('bkgqd,bksd->bkgqs', q_blk, k).astype(jnp.float32) * d ** -0.5
        p = jax.nn.softmax(s, axis=-1).astype(v.dtype)
        return jnp.einsum('bkgqs,bksd->bkgqd', p, v)

    out = lax.map(one_block, qb)
    return jnp.moveaxis(out, 0, 3).reshape(b, hq, lq, d)


def rwkv_heads(t):
    b, l, _ = t.shape
    return t.astype(jnp.float32).reshape(b, l, N_HEADS_A, HEAD_DIM)


def rwkv_direction_terms(k, w_lo, a_lo, w0, w2, a0, a2, k_k, k_a):
    kf = k.astype(jnp.float32)
    w = -jax.nn.softplus(-(w0 + jnp.tanh(w_lo) @ w2).astype(jnp.float32)) - 0.5
    decay = jnp.exp(-jnp.exp(w))
    a = jax.nn.sigmoid((a0 + a_lo @ a2).astype(jnp.float32))
    kk = rwkv_heads(kf * k_k)
    kk = kk / jnp.maximum(jnp.sqrt(jnp.sum(kk * kk, axis=-1, keepdims=True)), 1e-12)
    k_mod = kf * (1.0 + (a - 1.0) * k_a)
    return rwkv_heads(decay), kk, kk * rwkv_heads(a), rwkv_heads(k_mod)


def wkv7_scan(state0, r, v, decay, kk, b, k, reverse, with_output):
    to_time = lambda t: jnp.moveaxis(t, 1, 0)

    def update(s, w_t, kk_t, b_t, k_t, v_t):
        sa = jnp.einsum('bhvk,bhk->bhv', s, kk_t)
        return s * w_t[:, :, None, :] - sa[..., None] * b_t[:, :, None, :] + v_t[..., None] * k_t[:, :, None, :]

    if with_output:
        def step(s, xs):
            r_t, w_t, kk_t, b_t, k_t, v_t = xs
            s = update(s, w_t, kk_t, b_t, k_t, v_t)
            return s, jnp.einsum('bhvk,bhk->bhv', s, r_t)
        s_final, ys = lax.scan(step, state0, tuple(map(to_time, (r, decay, kk, b, k, v))), reverse=reverse)
        return s_final, jnp.moveaxis(ys, 0, 1)

    def step_state(s, xs):
        return update(s, *xs), None
    s_final, _ = lax.scan(step_state, state0, tuple(map(to_time, (decay, kk, b, k, v))), reverse=reverse)
    return s_final, None


def rwkv_readout(y, r, v, k_f, k_b, g_lo, r_k, g2, gn_w, gn_b):
    mu = jnp.mean(y, axis=-1, keepdims=True)
    var = jnp.mean(jnp.square(y - mu), axis=-1, keepdims=True)
    yn = ((y - mu) * lax.rsqrt(var + RWKV_GN_EPS)).reshape(y.shape[0], y.shape[1], GROUP_WIDTH) * gn_w + gn_b
    bonus = jnp.sum(r * (k_f + k_b) * r_k, axis=-1, keepdims=True) * v
    gate = jax.nn.sigmoid(g_lo) @ g2
    return (yn + bonus.reshape(yn.shape)) * gate


def rwkv_mixer(p_lat, p_ctx, w0, w2, a0, a2, k_k, k_a, r_k, g2, gn_w, gn_b, need_ctx_out):
    def terms(p):
        r, k, v, w_f, w_b, a_f, a_b, g = p
        fwd = rwkv_direction_terms(k, w_f, a_f, w0[0], w2[0], a0[0], a2[0], k_k, k_a)
        bwd = rwkv_direction_terms(k, w_b, a_b, w0[1], w2[1], a0[1], a2[1], k_k, k_a)
        return rwkv_heads(r), rwkv_heads(v), fwd, bwd, g

    r_c, v_c, fwd_c, bwd_c, g_c = terms(p_ctx)
    zero = jnp.zeros((r_c.shape[0], N_HEADS_A, HEAD_DIM, HEAD_DIM), jnp.float32)
    s_f, y_cf = wkv7_scan(zero, r_c, v_c, *fwd_c, reverse=False, with_output=need_ctx_out)
    s_b, y_cb = wkv7_scan(zero, r_c, v_c, *bwd_c, reverse=True, with_output=need_ctx_out)
    r_l, v_l, fwd_l, bwd_l, g_l = terms(p_lat)
    _, y_lf = wkv7_scan(s_f, r_l, v_l, *fwd_l, reverse=False, with_output=True)
    _, y_lb = wkv7_scan(s_b, r_l, v_l, *bwd_l, reverse=True, with_output=True)
    out_l = rwkv_readout(y_lf + y_lb, r_l, v_l, fwd_l[3], bwd_l[3], g_l, r_k, g2, gn_w, gn_b)
    out_c = None
    if need_ctx_out:
        out_c = rwkv_readout(y_cf + y_cb, r_c, v_c, fwd_c[3], bwd_c[3], g_c, r_k, g2, gn_w, gn_b)
    return out_l, out_c


def neighbourhood_mixer(p_lat, p_ctx, rpb, need_ctx_out):
    q, k, v = p_lat
    q_c, k_c, v_c = p_ctx
    bsz, n_lat, _ = q.shape
    rows = n_lat // GRID_W
    kh = min(NA_ROWS, rows)

    def grid(t):
        return t.reshape(bsz, rows, GRID_W, N_HEADS_B, HEAD_DIM).transpose(0, 3, 1, 2, 4)

    qg, kg, vg = grid(q), grid(k), grid(v)
    r_idx = jnp.arange(rows)
    row_start = jnp.clip(r_idx - kh // 2, 0, rows - kh)
    row_idx = row_start[:, None] + jnp.arange(kh)
    k_band = kg[:, :, row_idx]
    v_band = vg[:, :, row_idx]
    col = jnp.arange(GRID_W)
    col_start = jnp.clip(col - NA_COLS // 2, 0, GRID_W - NA_COLS)
    in_win = (col[None, :] >= col_start[:, None]) & (col[None, :] < col_start[:, None] + NA_COLS)
    col_off = jnp.clip(col[None, :] - col[:, None], -(NA_COLS - 1), NA_COLS - 1) + NA_COLS - 1
    row_off = row_idx - r_idx[:, None] + NA_ROWS - 1
    bias = rpb[:, row_off[:, None, :, None], col_off[None, :, None, :]].astype(jnp.float32)
    bias = jnp.where(in_win[None, None, :, None, :], bias, -jnp.inf)
    scale = HEAD_DIM ** -0.5
    s_loc = jnp.einsum('bhrqd,bhrikd->bhrqik', qg, k_band).astype(jnp.float32) * scale + bias[None]
    kcg, vcg = heads_first(k_c, N_HEADS_B), heads_first(v_c, N_HEADS_B)
    s_ctx = jnp.einsum('bhrqd,bhcd->bhrqc', qg, kcg).astype(jnp.float32) * scale
    n_loc = kh * GRID_W
    s = jnp.concatenate([s_loc.reshape(bsz, N_HEADS_B, rows, GRID_W, n_loc), s_ctx], axis=-1)
    p = jax.nn.softmax(s, axis=-1).astype(v.dtype)
    p_loc = p[..., :n_loc].reshape(bsz, N_HEADS_B, rows, GRID_W, kh, GRID_W)
    o = (jnp.einsum('bhrqik,bhrikd->bhrqd', p_loc, v_band)
         + jnp.einsum('bhrqc,bhcd->bhrqd', p[..., n_loc:], vcg))
    out_l = o.transpose(0, 2, 3, 1, 4).reshape(bsz, n_lat, GROUP_WIDTH)
    out_c = None
    if need_ctx_out:
        out_c = heads_last(softmax_attention(heads_first(q_c, N_HEADS_B), kcg, vcg))
    return out_l, out_c


def gqa_mixer(p_lat, p_ctx, q_gain, k_gain, cos, sin, need_ctx_out):
    q_l, k_l, v_l = p_lat
    q_c, k_c, v_c = p_ctx
    ql = apply_rope(rms_norm(heads_first(q_l, N_HEADS_C), q_gain), cos, sin)
    kl = apply_rope(rms_norm(heads_first(k_l, N_KV_C), k_gain), cos, sin)
    kc = rms_norm(heads_first(k_c, N_KV_C), k_gain)
    vc = heads_first(v_c, N_KV_C)
    k_all = jnp.concatenate([kc, kl], axis=2)
    v_all = jnp.concatenate([vc, heads_first(v_l, N_KV_C)], axis=2)
    out_l = heads_last(gqa_block_sweep(ql, k_all, v_all))
    out_c = None
    if need_ctx_out:
        qc = rms_norm(heads_first(q_c, N_HEADS_C), q_gain)
        out_c = heads_last(gqa_block_sweep(qc, kc, vc))
    return out_l, out_c


def centred_window_mean(x, window):
    n = x.shape[1]
    cs = jnp.pad(jnp.cumsum(x.astype(jnp.float32), axis=1), ((0, 0), (1, 0), (0, 0)))
    t = jnp.arange(n)
    lo = jnp.clip(t - window // 2, 0, n)
    hi = jnp.clip(t - window // 2 + window, 0, n)
    return (cs[:, hi] - cs[:, lo]) / (hi - lo).astype(jnp.float32)[None, :, None]


def pool_mixer(p, pool_w, pool_scale):
    groups = jnp.split(p, len(POOL_WINDOWS), axis=-1)
    outs = [jnp.einsum('blc,cd->bld', centred_window_mean(g, w) - g.astype(jnp.float32), pool_w[j])
            for j, (g, w) in enumerate(zip(groups, POOL_WINDOWS))]
    return jnp.concatenate(outs, axis=-1) * pool_scale


def squared_relu_mlp(h, w1, w2):
    return jnp.square(jax.nn.relu(h @ w1)) @ w2


def setup_inputs(seed: int = 0) -> dict:
    key = jax.random.key(seed)
    ks = jax.random.split(key, 32)
    nrm = jax.random.normal
    f32 = jnp.float32
    d = D_MODEL
    return {
        'x': nrm(ks[0], (BATCH, SEQ, d), f32),
        'c': nrm(ks[1], (BATCH, d), f32),
        'ctx': nrm(ks[2], (BATCH, CTX_LEN, d), f32),
        'c_ctx': nrm(ks[3], (d,), f32),
        'ada_w': nrm(ks[4], (DEPTH, d, N_MOD * d), f32) * (0.5 * d ** -0.5),
        'ada_b': nrm(ks[5], (DEPTH, N_MOD * d), f32) * 0.01,
        'norm1_g': 1.0 + 0.02 * nrm(ks[6], (DEPTH, d), f32),
        'norm2_g': 1.0 + 0.02 * nrm(ks[7], (DEPTH, d), f32),
        'w_in': nrm(ks[8], (DEPTH, d, D_IN), f32) * d ** -0.5,
        'w_out': nrm(ks[9], (DEPTH, d, d), f32) * d ** -0.5,
        'rwkv_w0': jax.random.uniform(ks[10], (DEPTH, 2, GROUP_WIDTH), f32, -5.0, 1.0),
        'rwkv_w2': nrm(ks[11], (DEPTH, 2, DECAY_RANK, GROUP_WIDTH), f32) * 0.1,
        'rwkv_a0': nrm(ks[12], (DEPTH, 2, GROUP_WIDTH), f32) * 0.1,
        'rwkv_a2': nrm(ks[13], (DEPTH, 2, ICLR_RANK, GROUP_WIDTH), f32) * 0.1,
        'rwkv_k_k': 0.85 + 0.02 * nrm(ks[14], (DEPTH, GROUP_WIDTH), f32),
        'rwkv_k_a': 1.0 + 0.02 * nrm(ks[15], (DEPTH, GROUP_WIDTH), f32),
        'rwkv_r_k': nrm(ks[16], (DEPTH, N_HEADS_A, HEAD_DIM), f32) * 0.1,
        'rwkv_g2': nrm(ks[17], (DEPTH, GATE_RANK, GROUP_WIDTH), f32) * GATE_RANK ** -0.5,
        'rwkv_gn_w': 1.0 + 0.02 * nrm(ks[18], (DEPTH, GROUP_WIDTH), f32),
        'rwkv_gn_b': 0.01 * nrm(ks[19], (DEPTH, GROUP_WIDTH), f32),
        'na_rpb': 0.02 * nrm(ks[20], (DEPTH, N_HEADS_B, 2 * NA_ROWS - 1, 2 * NA_COLS - 1), f32),
        'gqa_q_gain': 1.0 + 0.02 * nrm(ks[21], (DEPTH, HEAD_DIM), f32),
        'gqa_k_gain': 1.0 + 0.02 * nrm(ks[22], (DEPTH, HEAD_DIM), f32),
        'pool_w': nrm(ks[23], (DEPTH, len(POOL_WINDOWS), POOL_GROUP, POOL_GROUP), f32) * POOL_GROUP ** -0.5,
        'pool_scale': 1.0 + 0.02 * nrm(ks[24], (DEPTH, GROUP_WIDTH), f32),
        'mlp_w1': nrm(ks[25], (DEPTH, d, D_FF), f32) * d ** -0.5,
        'mlp_w2': nrm(ks[26], (DEPTH, D_FF, d), f32) * D_FF ** -0.5,
        'final_g': 1.0 + 0.02 * nrm(ks[27], (d,), f32),
    }


def reference(x, c, ctx, c_ctx, ada_w, ada_b, norm1_g, norm2_g, w_in, w_out,
              rwkv_w0, rwkv_w2, rwkv_a0, rwkv_a2, rwkv_k_k, rwkv_k_a, rwkv_r_k, rwkv_g2,
              rwkv_gn_w, rwkv_gn_b, na_rpb, gqa_q_gain, gqa_k_gain, pool_w, pool_scale,
              mlp_w1, mlp_w2, final_g):
    bsz, n_lat, _ = x.shape
    cos, sin = axial_rope_tables(n_lat)
    silu_c = jax.nn.silu(c.astype(jnp.float32))
    silu_c_ctx = jax.nn.silu(c_ctx.astype(jnp.float32))
    xc = ctx
    for i in range(DEPTH):
        need_ctx_out = i < DEPTH - 1
        mod_l = (silu_c @ ada_w[i] + ada_b[i]).reshape(bsz, N_MOD, 1, D_MODEL).astype(x.dtype)
        mod_c = (silu_c_ctx @ ada_w[i] + ada_b[i]).reshape(N_MOD, D_MODEL).astype(x.dtype)

        h = modulate(rms_norm(x, norm1_g[i]), mod_l[:, 0], mod_l[:, 1])
        hc = modulate(rms_norm(xc, norm1_g[i]), mod_c[0], mod_c[1])
        p_lat = split_projection(h @ w_in[i])
        p_ctx = split_projection(hc @ w_in[i])
        a_l, a_c = rwkv_mixer(p_lat[SLICE_A], p_ctx[SLICE_A], rwkv_w0[i], rwkv_w2[i], rwkv_a0[i], rwkv_a2[i],
                              rwkv_k_k[i], rwkv_k_a[i], rwkv_r_k[i], rwkv_g2[i], rwkv_gn_w[i], rwkv_gn_b[i],
                              need_ctx_out)
        b_l, b_c = neighbourhood_mixer(p_lat[SLICE_B], p_ctx[SLICE_B], na_rpb[i], need_ctx_out)
        c_l, c_c = gqa_mixer(p_lat[SLICE_C], p_ctx[SLICE_C], gqa_q_gain[i], gqa_k_gain[i], cos, sin, need_ctx_out)
        d_l = pool_mixer(p_lat[INDEX_D], pool_w[i], pool_scale[i])
        mix_l = jnp.concatenate([a_l.astype(x.dtype), b_l.astype(x.dtype), c_l.astype(x.dtype),
                                 d_l.astype(x.dtype)], axis=-1)
        x = x + mod_l[:, 2] * (mix_l @ w_out[i])

        h2 = modulate(rms_norm(x, norm2_g[i]), mod_l[:, 3], mod_l[:, 4])
        x = x + mod_l[:, 5] * squared_relu_mlp(h2, mlp_w1[i], mlp_w2[i])

        if need_ctx_out:
            d_c = pool_mixer(p_ctx[INDEX_D], pool_w[i], pool_scale[i])
            mix_c = jnp.concatenate([a_c.astype(xc.dtype), b_c.astype(xc.dtype), c_c.astype(xc.dtype),
                                     d_c.astype(xc.dtype)], axis=-1)
            xc = xc + mod_c[2] * (mix_c @ w_out[i])
            h2c = modulate(rms_norm(xc, norm2_g[i]), mod_c[3], mod_c[4])
            xc = xc + mod_c[5] * squared_relu_mlp(h2c, mlp_w1[i], mlp_w2[i])
    return rms_norm(x, final_g)
```

```python
import contextlib
import numpy as np
import concourse.bass as bass
import concourse.mybir as mybir
from concourse.bass_utils import run_bass_kernel_spmd

F32 = mybir.dt.float32
AF = mybir.ActivationFunctionType
ALU = mybir.AluOpType
AX = mybir.AxisListType

D = 1024
B = 4
SEQ = 4096
DEPTH = 4
CTX = 256
NT = 2176
TA = 4352
D_IN = 2496
EPS = 1e-6
NCORES = 8


class Buf:
    __slots__ = ("t", "lw", "rd", "name")

    def __init__(self, t, name=""):
        self.t = t
        self.lw = None
        self.rd = {}
        self.name = name

    def __getitem__(self, idx):
        return self.t[idx]


class Prog:
    NDMA = 16

    def __init__(self, nc):
        self.nc = nc
        self.es = contextlib.ExitStack()
        self.E = {"pe": nc.tensor, "dve": nc.vector, "act": nc.scalar, "pool": nc.gpsimd, "sp": nc.sync}
        self.sem = {}
        self.cnt = {}
        for e in self.E:
            self.sem[e] = self.es.enter_context(nc.semaphore("c_" + e))
            self.cnt[e] = 0
        self.dsem = []
        for i in range(self.NDMA):
            k = "d%d" % i
            self.sem[k] = self.es.enter_context(nc.semaphore(k))
            self.cnt[k] = 0
            self.dsem.append(k)
        self.dnext = 0
        self.seen = {e: {} for e in self.E}
        self.nbuf = 0
        self.ninst = 0
        self.dq = 0
        self.pes = None
        self.bind = {}
        self.standalone = True

    def begin_phase(self, bind):
        self.pes = contextlib.ExitStack()
        self.bind = dict(bind or {})

    def barrier(self):
        for e, E in self.E.items():
            for k, c in self.cnt.items():
                if k == e or c == 0:
                    continue
                if self.seen[e].get(k, 0) < c:
                    E.wait_ge(self.sem[k], c)
                    self.seen[e][k] = c

    def core_barrier(self):
        self.barrier()
        self.nc.all_core_barrier()

    def end_phase(self):
        self.barrier()
        self.pes.close()
        self.pes = None
        self.bind = {}

    def _stack(self):
        return self.pes if self.pes is not None else self.es

    def sb(self, shape, dt=F32, name=None):
        self.nbuf += 1
        name = ("sb%d" % self.nbuf) if name is None else ("%s_%d" % (name, self.nbuf))
        return Buf(self._stack().enter_context(self.nc.sbuf_tensor(name, list(shape), dt)), name)

    def ps(self, shape, dt=F32, name=None):
        self.nbuf += 1
        name = ("ps%d" % self.nbuf) if name is None else ("%s_%d" % (name, self.nbuf))
        return Buf(self._stack().enter_context(self.nc.psum_tensor(name, list(shape), dt)), name)

    def din(self, name, shape, dt=F32):
        if name in self.bind:
            return self.bind[name]
        return Buf(self.nc.dram_tensor(name, list(shape), dt, kind="ExternalInput").ap(), name)

    def dout(self, name, shape, dt=F32):
        if name in self.bind:
            return self.bind[name]
        return Buf(self.nc.dram_tensor(name, list(shape), dt, kind="ExternalOutput").ap(), name)

    def _waits(self, eng, reads, writes):
        w = {}

        def need(tok):
            if tok is None:
                return
            k, v = tok
            if k == eng and eng == "pe":
                return
            if w.get(k, 0) < v:
                w[k] = v

        for b in reads:
            need(b.lw)
        for b in writes:
            need(b.lw)
            for k, v in b.rd.items():
                need((k, v))
        E = self.E[eng]
        for k, v in w.items():
            if self.seen[eng].get(k, 0) < v:
                E.wait_ge(self.sem[k], v)
                self.seen[eng][k] = v

    def op(self, eng, fn, reads=(), writes=()):
        self._waits(eng, reads, writes)
        inst = fn(self.E[eng])
        self.cnt[eng] += 1
        inst.then_inc(self.sem[eng], 1)
        c = self.cnt[eng]
        for b in reads:
            b.rd[eng] = c
        for b in writes:
            b.lw = (eng, c)
            b.rd = {}
        self.ninst += 1
        return inst

    def dma(self, out, in_, reads=(), writes=(), eng=None):
        eng = "sp"
        k = self.dsem[self.dnext]
        self.dnext = (self.dnext + 1) % self.NDMA
        E = self.E[eng]
        if self.cnt[k] > 0 and self.seen[eng].get(k, 0) < self.cnt[k]:
            E.wait_ge(self.sem[k], self.cnt[k])
            self.seen[eng][k] = self.cnt[k]
        self._waits(eng, reads, writes)
        inst = E.dma_start(out=out, in_=in_)
        self.cnt[k] += 16
        inst.then_inc(self.sem[k], 16)
        c = self.cnt[k]
        for b in reads:
            b.rd[k] = c
        for b in writes:
            b.lw = (k, c)
            b.rd = {}
        self.ninst += 1

    def finish(self, bufs):
        if not self.standalone:
            self.end_phase()
            return
        self._waits("sp", bufs, ())
        self.es.close()

    def scratch(self, name, shape, dt=F32):
        return Buf(self.nc.dram_tensor(name, list(shape), dt, kind="Internal").ap(), name)

    def mm(self, ps, out_ap, lhsT_b, lhsT_ap, rhs_b, rhs_ap, start, stop, skip=False):
        self.op("pe", lambda e: e.matmul(out_ap, lhsT=lhsT_ap, rhs=rhs_ap, start=start, stop=stop,
                                         skip_group_check=skip),
                reads=[lhsT_b, rhs_b], writes=[ps])

    def act(self, out_b, out_ap, in_b, in_ap, func, bias=None, scale=1.0, extra_reads=()):
        kw = {}
        if bias is not None:
            kw["bias"] = bias
        self.op("act", lambda e: e.activation(out=out_ap, in_=in_ap, func=func, scale=scale, **kw),
                reads=[in_b] + list(extra_reads), writes=[out_b])

    def tt(self, eng, out_b, out_ap, a_b, a_ap, b_b, b_ap, op):
        self.op(eng, lambda e: e.tensor_tensor(out=out_ap, in0=a_ap, in1=b_ap, op=op),
                reads=[a_b, b_b], writes=[out_b])

    def ts(self, eng, out_b, out_ap, a_b, a_ap, s1, s2, op0, op1=None, extra_reads=()):
        if op1 is None:
            self.op(eng, lambda e: e.tensor_scalar(out=out_ap, in0=a_ap, scalar1=s1, scalar2=None, op0=op0),
                    reads=[a_b] + list(extra_reads), writes=[out_b])
        else:
            self.op(eng, lambda e: e.tensor_scalar(out=out_ap, in0=a_ap, scalar1=s1, scalar2=s2, op0=op0, op1=op1),
                    reads=[a_b] + list(extra_reads), writes=[out_b])

    def stt(self, out_b, out_ap, a_b, a_ap, scalar, b_b, b_ap, op0, op1, extra_reads=()):
        self.op("dve", lambda e: e.scalar_tensor_tensor(out=out_ap, in0=a_ap, scalar=scalar, in1=b_ap, op0=op0, op1=op1),
                reads=[a_b, b_b] + list(extra_reads), writes=[out_b])

    def copy(self, eng, out_b, out_ap, in_b, in_ap):
        if eng == "act":
            self.op("act", lambda e: e.activation(out=out_ap, in_=in_ap, func=AF.Copy), reads=[in_b], writes=[out_b])
        else:
            self.op(eng, lambda e: e.tensor_copy(out=out_ap, in_=in_ap), reads=[in_b], writes=[out_b])

    def memset(self, eng, b, ap, val):
        self.op(eng, lambda e: e.memset(ap, val), reads=[], writes=[b])


def _begin(P, bind):
    if P is None:
        nc = bass.Bass("TRN2", target_bir_lowering=False)
        return Prog(nc), nc
    P.standalone = False
    P.begin_phase(bind)
    return P, P.nc


def _store_mix(P, opt, od, o):
    if "ms" not in opt:
        P.dma(od[:], o[:], reads=[o], writes=[od])
        return
    ML, r0 = opt["ms"], opt["ms_row"]
    for h in range(2):
        P.dma(ML.t[h, r0:r0 + 128, 0:2048], o[:, CTX + h * 2048:CTX + (h + 1) * 2048], reads=[o], writes=[ML])
        P.dma(ML.t[h, r0:r0 + 128, 2048:2176], o[:, h * 128:(h + 1) * 128], reads=[o], writes=[ML])


_PROGS = {}


def _run(key, builder, in_maps):
    if key not in _PROGS:
        _PROGS[key] = builder()
    nc = _PROGS[key]
    res = run_bass_kernel_spmd(nc, in_maps, core_ids=list(range(NCORES)))
    return res.results


class Ring:
    def __init__(self, items):
        self.items = items
        self.i = 0

    def next(self):
        x = self.items[self.i]
        self.i = (self.i + 1) % len(self.items)
        return x


def build_L0(P=None, bind=None, opt=None):
    P, nc = _begin(P, bind)
    opt = opt or {}
    cT = P.din("cT", [128, 8, 8])
    aw = P.din("aw", [1024, 3072])
    ab = P.din("ab", [8, 3072])
    out = P.dout("mod", [8, 3072])
    c_sb = P.sb([128, 8, 8])
    sg = P.sb([128, 8, 8])
    w_sb = P.sb([128, 8, 3072])
    b_sb = P.sb([8, 3072])
    o_sb = P.sb([8, 3072])
    P.dma(c_sb[:], cT[:], reads=[cT], writes=[c_sb])
    P.dma(b_sb[:], ab[:], reads=[ab], writes=[b_sb])
    awv = aw.t.rearrange("(kc p) n -> p kc n", p=128)
    for kc in range(8):
        P.dma(w_sb[:, kc, :], awv[:, kc, :], reads=[aw], writes=[w_sb], eng=("sp" if kc % 2 == 0 else "pool"))
    P.act(sg, sg[:], c_sb, c_sb[:], AF.Sigmoid)
    P.tt("dve", sg, sg[:], sg, sg[:], c_sb, c_sb[:], ALU.mult)
    pss = [P.ps([8, 512]) for _ in range(2)]
    for j in range(6):
        ps = pss[j % 2]
        for kc in range(8):
            P.mm(ps, ps[:], sg, sg[:, kc, :], w_sb, w_sb[:, kc, j * 512:(j + 1) * 512], kc == 0, kc == 7)
        P.tt("dve", o_sb, o_sb[:, j * 512:(j + 1) * 512], ps, ps[:], b_sb, b_sb[:, j * 512:(j + 1) * 512], ALU.add)
    P.dma(out[:], o_sb[:], reads=[o_sb], writes=[out])
    P.finish([out])
    return nc


def _rmsnorm_tile(P, x, Tn, ones, epsb, sq, ps_stat, rs):
    for kc in range(8):
        P.act(sq, sq[:, kc, :Tn], x, x[:, kc, :Tn], AF.Square)
    for kc in range(8):
        P.mm(ps_stat, ps_stat[:, :Tn], ones, ones[:], sq, sq[:, kc, :Tn], kc == 0, kc == 7)
    P.act(rs, rs[:, :Tn], ps_stat, ps_stat[:, :Tn], AF.Sqrt, bias=epsb[:, 0:1], extra_reads=[epsb])
    P.op("dve", lambda e: e.reciprocal(out=rs[:, :Tn], in_=rs[:, :Tn]), reads=[rs], writes=[rs])


PCH = [(0, 128), (128, 128), (256, 128), (384, 128), (512, 128), (640, 128),
       (768, 128), (896, 64),
       (960, 128), (1088, 128), (1216, 128), (1344, 128),
       (1728, 128), (1856, 128), (1984, 128),
       (2240, 128), (2368, 128)]
VB0, VC0 = 1472, 2112


def build_LA(P=None, bind=None, opt=None):
    P, nc = _begin(P, bind)
    opt = opt or {}
    NTK = opt.get("ntok", NT)
    WC = opt.get("wcols", D_IN)
    pch = opt.get("pch", [(r0, nr, r0) for (r0, nr) in PCH])
    vch = opt.get("vch", [(VB0, 256, 0), (VC0, 128, 256)])
    VW = sum(n for (_, n, _) in vch)
    xT = P.din("xT", [1024, NTK])
    w = P.din("w", [1024, WC])
    vec = P.din("vec", [128, 5, 8])
    pT = P.dout("pT", [D_IN, NTK])
    vtok = P.dout("vtok", [NTK, VW])
    w_sb = P.sb([128, 8, WC])
    v_sb = P.sb([128, 5, 8])
    gs = P.sb([128, 2, 8])
    ones = P.sb([128, 128])
    epsb = P.sb([128, 1])
    xs = Ring([P.sb([128, 8, 512]) for _ in range(2)])
    sq = P.sb([128, 8, 512])
    h = P.sb([128, 8, 512])
    rs = P.sb([128, 512])
    stg = Ring([P.sb([128, 512]) for _ in range(4)])
    ps_stat = P.ps([128, 512])
    pso = Ring([P.ps([128, 512]) for _ in range(4)])
    P.memset("pool", ones, ones[:], 1.0 / D)
    P.memset("pool", epsb, epsb[:], EPS)
    P.dma(v_sb[:], vec[:], reads=[vec], writes=[v_sb])
    wv = w.t.rearrange("(kc p) n -> p kc n", p=128)
    for kc in range(8):
        P.dma(w_sb[:, kc, :], wv[:, kc, :], reads=[w], writes=[w_sb], eng=("sp" if kc % 2 == 0 else "pool"))
    P.stt(gs, gs[:, 0, :], v_sb, v_sb[:, 2, :], 1.0, v_sb, v_sb[:, 0, :], ALU.add, ALU.mult)
    P.stt(gs, gs[:, 1, :], v_sb, v_sb[:, 4, :], 1.0, v_sb, v_sb[:, 0, :], ALU.add, ALU.mult)
    if "xslots" in opt:
        xvs = [b_.t.rearrange("(kc p) t -> p kc t", p=128) for b_ in opt["xslots"]]
        xbs = opt["xslots"]
    else:
        xvs = [xT.t.rearrange("(kc p) t -> p kc t", p=128)]
        xbs = [xT]
    tiles = opt.get("tiles", [(i * 512, 512, 0) for i in range(4)] + [(2048, 128, 1)])
    ei = 0
    hr = Ring([h, P.sb([128, 8, 512])])
    eic = [0]

    def front(tl):
        (s0, Tn, isctx) = tl[:3]
        sl = tl[4] if len(tl) > 4 else 0
        x = xs.next()
        h_ = hr.next()
        P.dma(x[:, :, :Tn], xvs[sl][:, :, s0:s0 + Tn], reads=[xbs[sl]], writes=[x])
        _rmsnorm_tile(P, x, Tn, ones, epsb, sq, ps_stat, rs)
        sh = 3 if isctx else 1
        for kc in range(8):
            P.stt(h_, h_[:, kc, :Tn], x, x[:, kc, :Tn], gs[:, isctx, kc:kc + 1], rs, rs[:, :Tn], ALU.mult, ALU.mult,
                  extra_reads=[gs])
            P.act(h_, h_[:, kc, :Tn], h_, h_[:, kc, :Tn], AF.Identity, bias=v_sb[:, sh, kc:kc + 1], extra_reads=[v_sb])
        return h_

    def back(tl, h_, mid):
        (s0, Tn, isctx) = tl[:3]
        t0 = tl[3] if len(tl) > 3 else s0
        for ci, (wc0, nr, r0) in enumerate(pch):
            ps = pso.next()
            for kc in range(8):
                P.mm(ps, ps[:nr, :Tn], w_sb, w_sb[:, kc, wc0:wc0 + nr], h_, h_[:, kc, :Tn], kc == 0, kc == 7)
            st = stg.next()
            P.copy("act" if eic[0] % 2 == 0 else "dve", st, st[:nr, :Tn], ps, ps[:nr, :Tn])
            eic[0] += 1
            P.dma(pT[r0:r0 + nr, t0:t0 + Tn], st[:nr, :Tn], reads=[st], writes=[pT])
            if ci == 5:
                mid()
        for sub in range(Tn // 128):
            ps = pso.next()
            for (wc0, ncol, oc0) in vch:
                for kc in range(8):
                    P.mm(ps, ps[:, oc0:oc0 + ncol], h_, h_[:, kc, sub * 128:(sub + 1) * 128], w_sb, w_sb[:, kc, wc0:wc0 + ncol],
                         kc == 0, kc == 7)
            st = stg.next()
            P.copy("act" if eic[0] % 2 == 0 else "dve", st, st[:, :VW], ps, ps[:, :VW])
            eic[0] += 1
            P.dma(vtok[t0 + sub * 128:t0 + (sub + 1) * 128, :], st[:, :VW], reads=[st], writes=[vtok])

    nxt = {"h": front(tiles[0])}
    for ti, tl in enumerate(tiles):
        cur_h = nxt["h"]

        def mid(ti=ti):
            if ti + 1 < len(tiles):
                nxt["h"] = front(tiles[ti + 1])

        back(tl, cur_h, mid)
    P.finish([pT, vtok])
    return nc


def build_LD(final, P=None, bind=None, opt=None):
    P, nc = _begin(P, bind)
    opt = opt or {}
    NTK = opt.get("ntok", NT)
    xT = P.din("xT", [1024, NTK])
    mT = P.din("mT", [1024, NTK])
    wo = P.din("wo", [1024, 1024])
    w1 = P.din("w1", [32, 128, 8, 128])
    w2 = P.din("w2", [4096, 1024])
    vec = P.din("vec", [128, 10, 8])
    oT = P.dout("oT", [1024, NTK])
    TT = 256
    wo_sb = P.sb([128, 8, 1024])
    v_sb = P.sb([128, 10, 8])
    gs = P.sb([128, 2, 8])
    ones = P.sb([128, 128])
    epsb = P.sb([128, 1])
    xs = Ring([P.sb([128, 8, TT]) for _ in range(2)])
    ms = Ring([P.sb([128, 8, TT]) for _ in range(2)])
    x1 = P.sb([128, 8, TT])
    sq = P.sb([128, 8, TT])
    h = P.sb([128, 8, TT])
    rs = P.sb([128, TT])
    w1r = Ring([P.sb([128, 8, 128]) for _ in range(3)])
    w2r = Ring([P.sb([128, 1024]) for _ in range(3)])
    rl = Ring([P.sb([128, TT]) for _ in range(2)])
    hid = Ring([P.sb([128, TT]) for _ in range(3)])
    osb = Ring([P.sb([128, 8, TT]) for _ in range(2)])
    pso = Ring([P.ps([128, TT]) for _ in range(2)])
    psh = Ring([P.ps([128, TT]) for _ in range(2)])
    acc = P.ps([128, 8, TT])
    P.memset("pool", ones, ones[:], 1.0 / D)
    P.memset("pool", epsb, epsb[:], EPS)
    P.dma(v_sb[:], vec[:], reads=[vec], writes=[v_sb])
    wov = wo.t.rearrange("(kc p) n -> p kc n", p=128)
    for kc in range(8):
        P.dma(wo_sb[:, kc, :], wov[:, kc, :], reads=[wo], writes=[wo_sb], eng=("sp" if kc % 2 == 0 else "pool"))
    P.stt(gs, gs[:, 0, :], v_sb, v_sb[:, 3, :], 1.0, v_sb, v_sb[:, 0, :], ALU.add, ALU.mult)
    P.stt(gs, gs[:, 1, :], v_sb, v_sb[:, 7, :], 1.0, v_sb, v_sb[:, 0, :], ALU.add, ALU.mult)
    xv = xT.t.rearrange("(kc p) t -> p kc t", p=128)
    mv = mT.t.rearrange("(kc p) t -> p kc t", p=128)
    ov = oT.t.rearrange("(kc p) t -> p kc t", p=128)
    w2v = w2.t.rearrange("(mo p) n -> mo p n", p=128)
    tiles = opt.get("tiles", [(i * TT, TT, 0) for i in range(2048 // TT)] + [(2048, 128, 1)])
    x1r = Ring([x1, P.sb([128, 8, TT])])
    hr = Ring([h, P.sb([128, 8, TT])])
    sqr = Ring([sq, P.sb([128, 8, TT])])
    rsr = Ring([rs, P.sb([128, TT])])

    def front(tile):
        (t0, Tn, isctx) = tile
        x = xs.next()
        m = ms.next()
        x1_, h_ = x1r.next(), hr.next()
        P.dma(x[:, :, :Tn], xv[:, :, t0:t0 + Tn], reads=[xT], writes=[x])
        P.dma(m[:, :, :Tn], mv[:, :, t0:t0 + Tn], reads=[mT], writes=[m])
        g1 = 5 if isctx else 1
        sh = 6 if isctx else 2
        for oc in range(8):
            ps = pso.next()
            for kc in range(8):
                P.mm(ps, ps[:, :Tn], wo_sb, wo_sb[:, kc, oc * 128:(oc + 1) * 128], m, m[:, kc, :Tn], kc == 0, kc == 7)
            P.stt(x1_, x1_[:, oc, :Tn], ps, ps[:, :Tn], v_sb[:, g1, oc:oc + 1], x, x[:, oc, :Tn], ALU.mult, ALU.add,
                  extra_reads=[v_sb])
        rs_ = rsr.next()
        _rmsnorm_tile(P, x1_, Tn, ones, epsb, sqr.next(), pso.next(), rs_)
        for kc in range(8):
            P.stt(h_, h_[:, kc, :Tn], x1_, x1_[:, kc, :Tn], gs[:, isctx, kc:kc + 1], rs_, rs_[:, :Tn], ALU.mult, ALU.mult,
                  extra_reads=[gs])
            P.act(h_, h_[:, kc, :Tn], h_, h_[:, kc, :Tn], AF.Identity, bias=v_sb[:, sh, kc:kc + 1], extra_reads=[v_sb])
        return (x1_, h_)

    def back(tile, bufs, mid):
        (t0, Tn, isctx) = tile
        (x1_, h_) = bufs
        g2 = 8 if isctx else 4
        pend = None
        for mo in range(32):
            w1c = w1r.next()
            w2c = w2r.next()
            P.dma(w1c[:], w1[mo], reads=[w1], writes=[w1c])
            P.dma(w2c[:], w2v[mo], reads=[w2], writes=[w2c])
            ph = psh.next()
            for kc in range(8):
                P.mm(ph, ph[:, :Tn], w1c, w1c[:, kc, :], h_, h_[:, kc, :Tn], kc == 0, kc == 7)
            r = rl.next()
            P.act(r, r[:, :Tn], ph, ph[:, :Tn], AF.Relu)
            hd = hid.next()
            P.tt("pool" if mo % 2 == 0 else "dve", hd, hd[:, :Tn], r, r[:, :Tn], r, r[:, :Tn], ALU.mult)
            if pend is not None:
                (pmo, pw2, phd) = pend
                for oc in range(8):
                    P.mm(acc, acc[:, oc, :Tn], pw2, pw2[:, oc * 128:(oc + 1) * 128], phd, phd[:, :Tn],
                         pmo == 0 and oc % 2 == 0, False, skip=True)
            pend = (mo, w2c, hd)
            if mo == 15:
                mid()
        (pmo, pw2, phd) = pend
        for oc in range(8):
            P.mm(acc, acc[:, oc, :Tn], pw2, pw2[:, oc * 128:(oc + 1) * 128], phd, phd[:, :Tn], False, True, skip=True)
        o = osb.next()
        for oc in range(8):
            P.stt(o, o[:, oc, :Tn], acc, acc[:, oc, :Tn], v_sb[:, g2, oc:oc + 1], x1_, x1_[:, oc, :Tn], ALU.mult, ALU.add,
                  extra_reads=[v_sb])
        if final:
            rs_ = rsr.next()
            _rmsnorm_tile(P, o, Tn, ones, epsb, sqr.next(), pso.next(), rs_)
            for kc in range(8):
                P.stt(o, o[:, kc, :Tn], o, o[:, kc, :Tn], v_sb[:, 9, kc:kc + 1], rs_, rs_[:, :Tn], ALU.mult, ALU.mult,
                      extra_reads=[v_sb])
        P.dma(ov[:, :, t0:t0 + Tn], o[:, :, :Tn], reads=[o], writes=[oT])

    nxt = {"b": front(tiles[0])}
    for ti, tile in enumerate(tiles):
        cur_bufs = nxt["b"]

        def mid(ti=ti):
            if ti + 1 < len(tiles):
                nxt["b"] = front(tiles[ti + 1])

        back(tile, cur_bufs, mid)
    P.finish([oT])
    return nc


def fm(v):
    return np.ascontiguousarray(np.asarray(v, np.float32).reshape(8, 128).T)


def host_L0(inp):
    c8 = np.zeros((8, D), np.float32)
    c8[:4] = inp["c"]
    c8[4] = inp["c_ctx"]
    cT = np.ascontiguousarray(c8.T.reshape(8, 128, 8).transpose(1, 0, 2))
    maps = []
    for i in range(NCORES):
        l, hf = i // 2, i % 2
        maps.append({"cT": cT,
                     "aw": np.ascontiguousarray(inp["ada_w"][l][:, hf * 3072:(hf + 1) * 3072]),
                     "ab": np.ascontiguousarray(np.tile(inp["ada_b"][l][None, hf * 3072:(hf + 1) * 3072], (8, 1)))})
    res = _run("L0", build_L0, maps)
    mod = np.zeros((DEPTH, 8, 6 * D), np.float32)
    for i in range(NCORES):
        l, hf = i // 2, i % 2
        mod[l][:, hf * 3072:(hf + 1) * 3072] = res[i]["mod"]
    return mod.reshape(DEPTH, 8, 6, D)


def host_LA(inp, l, xTs, mod):
    maps = []
    for i in range(NCORES):
        b = i // 2
        vec = np.stack([fm(inp["norm1_g"][l]), fm(mod[l, b, 0]), fm(mod[l, b, 1]), fm(mod[l, 4, 0]), fm(mod[l, 4, 1])], axis=1)
        maps.append({"xT": xTs[i], "w": np.ascontiguousarray(inp["w_in"][l]), "vec": np.ascontiguousarray(vec)})
    return _run("LA", build_LA, maps)


def host_LD(inp, l, xTs, mTs, mod, final):
    w1 = np.ascontiguousarray(inp["mlp_w1"][l].reshape(8, 128, 32, 128).transpose(2, 1, 0, 3))
    maps = []
    for i in range(NCORES):
        b = i // 2
        vec = np.stack([fm(inp["norm2_g"][l]),
                        fm(mod[l, b, 2]), fm(mod[l, b, 3]), fm(mod[l, b, 4]), fm(mod[l, b, 5]),
                        fm(mod[l, 4, 2]), fm(mod[l, 4, 3]), fm(mod[l, 4, 4]), fm(mod[l, 4, 5]),
                        fm(inp["final_g"])], axis=1)
        maps.append({"xT": xTs[i], "mT": mTs[i], "wo": np.ascontiguousarray(inp["w_out"][l]), "w1": w1,
                     "w2": np.ascontiguousarray(inp["mlp_w2"][l]), "vec": np.ascontiguousarray(vec)})
    key = "LDf" if final else "LD"
    return _run(key, lambda: build_LD(final), maps)


def shard_tokens(lat, ctx):
    out = []
    for i in range(NCORES):
        b, hf = i // 2, i % 2
        out.append(np.ascontiguousarray(np.concatenate([lat[b, hf * 2048:(hf + 1) * 2048], ctx[b, hf * 128:(hf + 1) * 128]], 0).T))
    return out


def unshard_tokens(xTs):
    F = xTs[0].shape[0]
    lat = np.zeros((B, SEQ, F), np.float32)
    ctx = np.zeros((B, CTX, F), np.float32)
    for i in range(NCORES):
        b, hf = i // 2, i % 2
        lat[b, hf * 2048:(hf + 1) * 2048] = xTs[i][:, :2048].T
        ctx[b, hf * 128:(hf + 1) * 128] = xTs[i][:, 2048:].T
    return lat, ctx


def _bd(val):
    m = np.zeros((128, 128), np.float32)
    m[:64, :64] = val
    m[64:, 64:] = val
    return m


def _onesab():
    m = np.zeros((128, 2, 128), np.float32)
    m[:, 0, :64] = 1.0
    m[:, 1, 64:] = 1.0
    return m


def _rope_consts():
    t = np.arange(SEQ)
    row = (t // 64).astype(np.float32)
    col = (t % 64).astype(np.float32)
    inv = (np.float32(10000.0) ** (-np.arange(16, dtype=np.float32) / np.float32(16))).astype(np.float32)
    ang = np.concatenate([row[:, None] * inv, col[:, None] * inv], -1).astype(np.float32)
    cos = np.cos(ang).astype(np.float32).T
    sin = np.sin(ang).astype(np.float32).T
    cosf = np.ascontiguousarray(np.tile(cos, (4, 1)))
    sinf = np.ascontiguousarray(np.tile(sin, (4, 1)))
    Pm = np.zeros((128, 128), np.float32)
    for m in range(128):
        if m % 64 < 32:
            Pm[m + 32, m] = -1.0
        else:
            Pm[m - 32, m] = 1.0
    return cosf, sinf, Pm


def build_LC(P=None, bind=None, opt=None):
    P, nc = _begin(P, bind)
    opt = opt or {}
    qd = P.din("q", [128, TA])
    kd = P.din("k2", [128, TA])
    vd = P.din("vpad", [TA, 2, 128])
    gd = P.din("gains", [128, 2])
    cosd = P.din("cosf", [128, SEQ])
    sind = P.din("sinf", [128, SEQ])
    protd = P.din("prot", [128, 128])
    bdd = P.din("bd64", [128, 128])
    oabd = P.din("onesab", [128, 2, 128])
    od = P.dout("o", [128, TA])
    q = P.sb([128, TA])
    k = P.sb([128, TA])
    v = P.sb([128, 34, 2, 128])
    g = P.sb([128, 2])
    cosf = P.sb([128, SEQ])
    sinf = P.sb([128, SEQ])
    prot = P.sb([128, 128])
    bd = P.sb([128, 128])
    oab = P.sb([128, 2, 128])
    epsb = P.sb([128, 1])
    o = P.sb([128, TA])
    t1 = Ring([P.sb([128, 512]) for _ in range(2)])
    t2 = Ring([P.sb([128, 512]) for _ in range(2)])
    pt = Ring([P.sb([128, 512]) for _ in range(3)])
    pacc = [Ring([P.sb([128, 512]) for _ in range(2)]) for _ in range(2)]
    rec = P.sb([128, 512])
    pss = Ring([P.ps([128, 512]) for _ in range(3)])
    pso = Ring([P.ps([128, 512]) for _ in range(2)])
    psm = P.ps([128, 512])
    P.memset("pool", epsb, epsb[:], EPS)
    own = "own" in opt
    if own:
        P.dma(q[:, CTX:CTX + 2048], qd.t[:, bass.ds(opt["own"] * 2048 + CTX, 2048)], reads=[qd], writes=[q])
        cosq = P.sb([128, 2048])
        sinq = P.sb([128, 2048])
        P.dma(cosq[:], cosd.t[:, bass.ds(opt["own"] * 2048, 2048)], reads=[cosd], writes=[cosq])
        P.dma(sinq[:], sind.t[:, bass.ds(opt["own"] * 2048, 2048)], reads=[sind], writes=[sinq])
    else:
        P.dma(q[:], qd[:], reads=[qd], writes=[q])
    if "ksrc" in opt:
        P.dma(k[0:64, :], opt["ksrc"][:], reads=[opt["ksrc"]], writes=[k])
        P.dma(k[64:128, :], opt["ksrc"][:], reads=[opt["ksrc"]], writes=[k])
    else:
        P.dma(k[:], kd[:], reads=[kd], writes=[k], eng="pool")
    P.dma(g[:], gd[:], reads=[gd], writes=[g])
    P.dma(bd[:], bdd[:], reads=[bdd], writes=[bd])
    P.dma(prot[:], protd[:], reads=[protd], writes=[prot])
    P.dma(oab[:], oabd[:], reads=[oabd], writes=[oab])
    P.dma(cosf[:], cosd[:], reads=[cosd], writes=[cosf], eng="pool")
    P.dma(sinf[:], sind[:], reads=[sind], writes=[sinf])
    if "vsrc" in opt:
        vs = opt["vsrc"]
        P.memset("pool", v, v[:], 0.0)
        vsv = vs.t.rearrange("(c p) m -> p c m", p=128)
        P.dma(v[:, :, 0, 0:64], vsv, reads=[vs], writes=[v])
        P.dma(v[:, :, 1, 64:128], vsv, reads=[vs], writes=[v])
    else:
        P.dma(v[:], vd.t.rearrange("(c p) j m -> p c j m", p=128), reads=[vd], writes=[v], eng="pool")
    tiles = [(i * 512, 512) for i in range(8)] + [(4096, 256)]
    for (x, gi) in ((q, 0), (k, 1)):
        qown = own and gi == 0
        for (t0, Tn) in ([(CTX + i * 512, 512) for i in range(4)] if qown else tiles):
            a = t1.next()
            P.act(a, a[:, :Tn], x, x[:, t0:t0 + Tn], AF.Square)
            ps = pss.next()
            P.mm(ps, ps[:, :Tn], bd, bd[:], a, a[:, :Tn], True, True)
            b_ = t2.next()
            P.act(b_, b_[:, :Tn], ps, ps[:, :Tn], AF.Sqrt, bias=epsb[:, 0:1], extra_reads=[epsb])
            P.op("dve", lambda e: e.reciprocal(out=b_[:, :Tn], in_=b_[:, :Tn]), reads=[b_], writes=[b_])
            P.stt(x, x[:, t0:t0 + Tn], x, x[:, t0:t0 + Tn], g[:, gi:gi + 1], b_, b_[:, :Tn], ALU.mult, ALU.mult,
                  extra_reads=[g])
        (ctab, stab) = (cosq, sinq) if qown else (cosf, sinf)
        for i in range(4 if qown else 8):
            c0 = CTX + i * 512
            ps = pss.next()
            P.mm(ps, ps[:], prot, prot[:], x, x[:, c0:c0 + 512], True, True)
            a = t1.next()
            P.tt("pool", a, a[:], x, x[:, c0:c0 + 512], ctab, ctab[:, i * 512:(i + 1) * 512], ALU.mult)
            b_ = t2.next()
            P.tt("dve", b_, b_[:], ps, ps[:], stab, stab[:, i * 512:(i + 1) * 512], ALU.mult)
            P.tt("pool", x, x[:, c0:c0 + 512], a, a[:], b_, b_[:], ALU.add)

    def attend(q0, Tn, kcs):
        po = pso.next()
        pa = [pacc[0].next(), pacc[1].next()]
        items = [(ci, kc, hh) for ci, kc in enumerate(kcs) for hh in range(2)]
        pend = None

        def pv(item, p_, is_first, is_last):
            (ci, kc, hh) = item
            P.mm(po, po[:, :Tn], v, v[:, kc, hh, :], p_, p_[:, :Tn], is_first, is_last)
            if ci > 0:
                P.tt("dve" if hh == 0 else "pool", pa[hh], pa[hh][:, :Tn], pa[hh], pa[hh][:, :Tn], p_, p_[:, :Tn], ALU.add)

        for idx, item in enumerate(items):
            (ci, kc, hh) = item
            ps = pss.next()
            P.mm(ps, ps[:, :Tn], k, k[64 * hh:64 * hh + 64, kc * 128:(kc + 1) * 128],
                 q, q[64 * hh:64 * hh + 64, q0:q0 + Tn], True, True)
            p_ = pa[hh] if ci == 0 else pt.next()
            P.act(p_, p_[:, :Tn], ps, ps[:, :Tn], AF.Exp, scale=0.125)
            if pend is not None:
                pv(pend[0], pend[1], pend[2] == 0, False)
            pend = (item, p_, idx)
        pv(pend[0], pend[1], pend[2] == 0, True)
        P.mm(psm, psm[:, :Tn], oab, oab[:, 0, :], pa[0], pa[0][:, :Tn], True, False)
        P.mm(psm, psm[:, :Tn], oab, oab[:, 1, :], pa[1], pa[1][:, :Tn], False, True)
        P.op("dve", lambda e: e.reciprocal(out=rec[:, :Tn], in_=psm[:, :Tn]), reads=[psm], writes=[rec])
        P.tt("dve", o, o[:, q0:q0 + Tn], po, po[:, :Tn], rec, rec[:, :Tn], ALU.mult)

    if own:
        for qt in range(4):
            attend(CTX + qt * 512, 512, list(range(34)))
        P.dma(opt["osel"][:], o[:, CTX:CTX + 2048], reads=[o], writes=[opt["osel"]])
    else:
        attend(0, 256, [0, 1])
        for qt in range(8):
            attend(CTX + qt * 512, 512, list(range(34)))
        _store_mix(P, opt, od, o)
    P.finish([od])
    return nc


def host_PT(resA):
    Plat, Pctx = unshard_tokens([r["pT"] for r in resA])
    PT = np.concatenate([Pctx, Plat], 1)
    vl = np.zeros((B, SEQ, 384), np.float32)
    vc = np.zeros((B, CTX, 384), np.float32)
    for i in range(NCORES):
        b, hf = i // 2, i % 2
        vl[b, hf * 2048:(hf + 1) * 2048] = resA[i]["vtok"][:2048]
        vc[b, hf * 128:(hf + 1) * 128] = resA[i]["vtok"][2048:]
    VT = np.concatenate([vc, vl], 1)
    return PT, VT


_CONST = {}


def consts():
    if not _CONST:
        cosf, sinf, Pm = _rope_consts()
        _CONST.update(cosf=cosf, sinf=sinf, prot=Pm, bd64=_bd(1.0 / 64), onesab=_onesab())
    return _CONST


def host_LC(inp, l, PT, VT):
    C = consts()
    maps = []
    for i in range(NCORES):
        b, hp = i // 2, i % 2
        qT = np.ascontiguousarray(PT[b][:, 1728 + hp * 128:1728 + (hp + 1) * 128].T)
        kT = PT[b][:, 1984 + hp * 64:1984 + (hp + 1) * 64].T
        k2 = np.ascontiguousarray(np.concatenate([kT, kT], 0))
        vv = VT[b][:, 256 + hp * 64:256 + (hp + 1) * 64]
        vpad = np.zeros((TA, 2, 128), np.float32)
        vpad[:, 0, :64] = vv
        vpad[:, 1, 64:] = vv
        gains = np.stack([np.tile(inp["gqa_q_gain"][l], 2), np.tile(inp["gqa_k_gain"][l], 2)], 1).astype(np.float32)
        maps.append({"q": qT, "k2": k2, "vpad": vpad, "gains": np.ascontiguousarray(gains), "cosf": C["cosf"],
                     "sinf": C["sinf"], "prot": C["prot"], "bd64": C["bd64"], "onesab": C["onesab"]})
    res = _run("LC", build_LC, maps)
    return [r["o"] for r in res]


def _na_row(r):
    rs = min(max(r - 4, 0), 56)
    return rs, rs - r + 7


NA_CLASSES = [7, 6, 5, 4, 3, 2, 1, 0]


def build_LB(P=None, bind=None, opt=None):
    P, nc = _begin(P, bind)
    opt = opt or {}
    qd = P.din("q", [128, TA])
    kd = P.din("k", [128, TA])
    vd = P.din("vpad", [TA + 64, 2, 128])
    bd_ = P.din("bias8", [128, 2, 8, 4, 64])
    oabd = P.din("onesab", [128, 2, 128])
    od = P.dout("o", [128, TA])
    q = P.sb([128, TA])
    k = P.sb([128, TA])
    v0 = P.sb([128, 34, 128])
    v1 = P.sb([128, 34, 128])
    bias = P.sb([128, 8, 4, 2, 64])
    oab = P.sb([128, 2, 128])
    ones = P.sb([128, 128])
    o = P.sb([128, TA])
    qbd = Ring([P.sb([128, 128]) for _ in range(3)])
    tmp = Ring([P.sb([128, 512]) for _ in range(2)])
    psum_r = Ring([P.sb([128, 128]) for _ in range(3)])
    pt = Ring([P.sb([128, 6, 128]) for _ in range(3)])
    ptc = Ring([P.sb([128, 2, 256]) for _ in range(2)])
    rec = P.sb([128, 512])
    pssA = Ring([P.ps([128, 512]) for _ in range(2)])
    pssB = Ring([P.ps([128, 256]) for _ in range(2)])
    pso = Ring([P.ps([128, 512]) for _ in range(2)])
    psm = Ring([P.ps([128, 512]) for _ in range(2)])
    for t_ in qbd.items:
        P.memset("pool", t_, t_[:], 0.0)
    P.memset("pool", ones, ones[:], 1.0)
    P.dma(q[:], qd[:], reads=[qd], writes=[q])
    P.dma(k[:], kd[:], reads=[kd], writes=[k])
    for hh in range(2):
        P.dma(bias[:, :, :, hh, :], bd_[:, hh, :, :, :], reads=[bd_], writes=[bias])
    P.dma(oab[:], oabd[:], reads=[oabd], writes=[oab])
    if "vsrc" in opt:
        vs = opt["vsrc"]
        P.memset("pool", v1, v1[:, 33, :], 0.0)
        P.dma(v0[:], vs.t[0:TA].rearrange("(c p) m -> p c m", p=128), reads=[vs], writes=[v0])
        P.dma(v1[:, 0:33, :], vs.t[64:64 + 33 * 128].rearrange("(c p) m -> p c m", p=128), reads=[vs], writes=[v1])
    else:
        for (vt, lo) in ((v0, 0), (v1, 64)):
            src = vd.t[lo:lo + TA].rearrange("(c p) j m -> p c j m", p=128)
            P.dma(vt[:, :, 0:64], src[:, :, 0, 0:64], reads=[vd], writes=[vt])
            P.dma(vt[:, :, 64:128], src[:, :, 1, 64:128], reads=[vd], writes=[vt])
    po = pso.next()
    pm = psm.next()
    for hh in range(2):
        ps = pssA.next()
        for c in range(2):
            P.mm(ps, ps[:, c * 256:(c + 1) * 256], k, k[64 * hh:64 * hh + 64, c * 128:(c + 1) * 128],
                 q, q[64 * hh:64 * hh + 64, 0:256], True, True)
        p_ = ptc.next()
        P.act(p_, p_[:].rearrange("p a b -> p (a b)"), ps, ps[:], AF.Exp, scale=0.125)
        for c in range(2):
            P.mm(po, po[64 * hh:64 * hh + 64, 0:256], v0, v0[:, c, 64 * hh:64 * hh + 64], p_, p_[:, c, :], c == 0, c == 1)
        for c in range(2):
            P.mm(pm, pm[64 * hh:64 * hh + 64, 0:256], ones, ones[:, 0:64], p_, p_[:, c, :], c == 0, c == 1)
    P.op("dve", lambda e: e.reciprocal(out=rec[:, :256], in_=pm[:, :256]), reads=[pm], writes=[rec])
    P.tt("dve", o, o[:, 0:256], po, po[:, :256], rec, rec[:, :256], ALU.mult)

    def stage1(r):
        rs, off = _na_row(r)
        cl = 7 - off
        qb = qbd.next()
        for hh in range(2):
            pp = slice(64 * hh, 64 * hh + 64)
            P.copy("pool", qb, qb[pp, pp], q, q[pp, CTX + r * 64:CTX + (r + 1) * 64])
        psA = pssA.next()
        psB = pssB.next()
        for c in range(4):
            t0 = CTX + rs * 64 + 128 * c
            P.mm(psA, psA[:, c * 128:(c + 1) * 128], k, k[:, t0:t0 + 128], qb, qb[:], True, True)
        for c in range(2):
            P.mm(psB, psB[:, c * 128:(c + 1) * 128], k, k[:, c * 128:(c + 1) * 128], qb, qb[:], True, True)
        tm = tmp.next()
        P.stt(tm, tm[:], psA, psA[:], 0.125, bias, bias[:, cl, :, :, :].rearrange("p a b c -> p (a b c)"),
              ALU.mult, ALU.add)
        p_ = pt.next()
        P.act(p_, p_[:, 0:4, :].rearrange("p a b -> p (a b)"), tm, tm[:], AF.Exp)
        P.act(p_, p_[:, 4:6, :].rearrange("p a b -> p (a b)"), psB, psB[:], AF.Exp, scale=0.125)
        return p_

    cur = {}

    def stage2(r, p_):
        rs, off = _na_row(r)
        if r % 4 == 0:
            cur["po"] = pso.next()
            cur["pm"] = psm.next()
        po, pm = cur["po"], cur["pm"]
        col = (r % 4) * 128
        for c in range(6):
            if c < 4:
                s_ = 4 + rs + 2 * c
                vt, j = (v0, s_ // 2) if s_ % 2 == 0 else (v1, (s_ - 1) // 2)
            else:
                vt, j = v0, c - 4
            P.mm(po, po[:, col:col + 128], vt, vt[:, j, :], p_, p_[:, c, :], c == 0, c == 5)
        sm = psum_r.next()
        P.op("dve", lambda e: e.tensor_reduce(out=sm[:], in_=p_[:].rearrange("p c q -> p q c"),
                                              axis=AX.X, op=ALU.add), reads=[p_], writes=[sm])
        P.mm(pm, pm[:, col:col + 128], ones, ones[:], sm, sm[:], True, True)
        if r % 4 == 3:
            r0 = r - 3
            P.op("dve", lambda e: e.reciprocal(out=rec[:], in_=pm[:]), reads=[pm], writes=[rec])
            for hh in range(2):
                pp = slice(64 * hh, 64 * hh + 64)
                P.tt("dve", o, o[pp, CTX + r0 * 64:CTX + r0 * 64 + 256].rearrange("p (r q) -> p r q", r=4),
                     po, po[pp, :].rearrange("p (r h q) -> p r h q", r=4, h=2)[:, :, hh, :],
                     rec, rec[pp, :].rearrange("p (r h q) -> p r h q", r=4, h=2)[:, :, hh, :], ALU.mult)

    pend = None
    for r in range(64):
        p_ = stage1(r)
        if pend is not None:
            stage2(*pend)
        pend = (r, p_)
    stage2(*pend)
    _store_mix(P, opt, od, o)
    P.finish([od])
    return nc


def _na_bias_tables(rpb):
    wq = np.arange(64)
    wk = np.arange(64)
    col_start = np.clip(wq - 8, 0, 48)
    in_win = (wk[None, :] >= col_start[:, None]) & (wk[None, :] < col_start[:, None] + 16)
    col_off = np.clip(wk[None, :] - wq[:, None], -15, 15) + 15
    out = np.zeros((4, 8, 128, 4, 64), np.float32)
    for cl in range(8):
        off = 7 - cl
        for c in range(4):
            for half in range(2):
                ro = off + 2 * c + half
                tbl = rpb[:, ro][:, col_off]
                tbl = np.where(in_win[None], tbl, np.float32(-30000.0))
                out[:, cl, half * 64:(half + 1) * 64, c, :] = tbl.transpose(0, 2, 1)
    return out


def host_LB(inp, l, PT, VT):
    C = consts()
    bt = _na_bias_tables(inp["na_rpb"][l])
    maps = []
    for i in range(NCORES):
        b, hp = i // 2, i % 2
        qT = np.ascontiguousarray(PT[b][:, 960 + hp * 128:960 + (hp + 1) * 128].T)
        kT = np.ascontiguousarray(PT[b][:, 1216 + hp * 128:1216 + (hp + 1) * 128].T)
        vv = VT[b][:, hp * 128:(hp + 1) * 128]
        vpad = np.zeros((TA + 64, 2, 128), np.float32)
        vpad[:TA, 0, :64] = vv[:, :64]
        vpad[:TA, 1, 64:] = vv[:, 64:]
        bias8 = np.ascontiguousarray(bt[2 * hp:2 * hp + 2].transpose(2, 0, 1, 3, 4))
        maps.append({"q": qT, "k": kT, "vpad": vpad, "bias8": bias8, "onesab": C["onesab"]})
    res = _run("LB", build_LB, maps)
    return [r["o"] for r in res]


def build_LP(P=None, bind=None, opt=None):
    P, nc = _begin(P, bind)
    opt = opt or {}
    xd = P.din("x", [128, TA])
    icd = P.din("invcnt", [128, TA])
    seld = P.din("sel", [128, 4])
    pwd = P.din("pw", [128, 128])
    pscd = P.din("psc", [128, 1])
    od = P.dout("o", [128, TA])
    PADL = 8
    W = TA + 64
    xp = P.sb([128, W])
    s2 = P.sb([128, W])
    s4 = P.sb([128, W])
    s8 = P.sb([128, W])
    s16 = P.sb([128, W])
    acc = P.sb([128, TA])
    ic = P.sb([128, TA])
    sel = P.sb([128, 4])
    pw = P.sb([128, 128])
    psc = P.sb([128, 1])
    o = P.sb([128, TA])
    pss = Ring([P.ps([128, 512]) for _ in range(2)])
    P.dma(ic[:], icd[:], reads=[icd], writes=[ic], eng="pool")
    P.dma(sel[:], seld[:], reads=[seld], writes=[sel])
    P.dma(pw[:], pwd[:], reads=[pwd], writes=[pw])
    P.dma(psc[:], pscd[:], reads=[pscd], writes=[psc])
    segs = [(0, CTX, 0), (CTX, SEQ, CTX + 24)]
    P.memset("pool", xp, xp[:], 0.0)
    for (t0, L, base) in segs:
        P.dma(xp[:, base + PADL:base + PADL + L], xd[:, t0:t0 + L], reads=[xd], writes=[xp])
    n = W - 2
    P.tt("dve", s2, s2[:, 0:n], xp, xp[:, 0:n], xp, xp[:, 1:n + 1], ALU.add)
    P.tt("dve", s4, s4[:, 0:n - 2], s2, s2[:, 0:n - 2], s2, s2[:, 2:n], ALU.add)
    P.tt("dve", s8, s8[:, 0:n - 6], s4, s4[:, 0:n - 6], s4, s4[:, 4:n - 2], ALU.add)
    P.tt("dve", s16, s16[:, 0:n - 14], s8, s8[:, 0:n - 14], s8, s8[:, 8:n - 6], ALU.add)
    for (t0, L, base) in segs:
        for j, (sw, w) in enumerate(((s2, 2), (s4, 4), (s8, 8), (s16, 16))):
            u0 = base + PADL - w // 2
            if j == 0:
                P.ts("dve", acc, acc[:, t0:t0 + L], sw, sw[:, u0:u0 + L], sel[:, 0:1], None, ALU.mult, extra_reads=[sel])
            else:
                P.stt(acc, acc[:, t0:t0 + L], sw, sw[:, u0:u0 + L], sel[:, j:j + 1], acc, acc[:, t0:t0 + L],
                      ALU.mult, ALU.add, extra_reads=[sel])
        P.tt("dve", acc, acc[:, t0:t0 + L], acc, acc[:, t0:t0 + L], ic, ic[:, t0:t0 + L], ALU.mult)
        P.tt("dve", acc, acc[:, t0:t0 + L], acc, acc[:, t0:t0 + L], xp, xp[:, base + PADL:base + PADL + L], ALU.subtract)
    for (t0, Tn) in [(i * 512, 512) for i in range(8)] + [(4096, 256)]:
        ps = pss.next()
        P.mm(ps, ps[:, :Tn], pw, pw[:], acc, acc[:, t0:t0 + Tn], True, True)
        P.ts("dve", o, o[:, t0:t0 + Tn], ps, ps[:, :Tn], psc[:, 0:1], None, ALU.mult, extra_reads=[psc])
    _store_mix(P, opt, od, o)
    P.finish([od])
    return nc


def _pool_invcnt():
    out = np.zeros((4, TA), np.float32)
    for j, w in enumerate((2, 4, 8, 16)):
        for (t0, L) in ((0, CTX), (CTX, SEQ)):
            t = np.arange(L)
            lo = np.clip(t - w // 2, 0, L)
            hi = np.clip(t - w // 2 + w, 0, L)
            out[j, t0:t0 + L] = (np.float32(1.0) / (hi - lo).astype(np.float32))
    return out


def host_LP(inp, l, PT):
    ic4 = _pool_invcnt()
    maps = []
    for i in range(NCORES):
        b, hp = i // 2, i % 2
        xT = np.ascontiguousarray(PT[b][:, 2240 + hp * 128:2240 + (hp + 1) * 128].T)
        ic = np.ascontiguousarray(np.repeat(ic4[2 * hp:2 * hp + 2], 64, axis=0))
        sel = np.zeros((128, 4), np.float32)
        sel[:64, 2 * hp] = 1.0
        sel[64:, 2 * hp + 1] = 1.0
        pw = np.zeros((128, 128), np.float32)
        pw[:64, :64] = inp["pool_w"][l][2 * hp]
        pw[64:, 64:] = inp["pool_w"][l][2 * hp + 1]
        psc = np.ascontiguousarray(inp["pool_scale"][l][hp * 128:(hp + 1) * 128].reshape(128, 1))
        maps.append({"x": xT, "invcnt": ic, "sel": sel, "pw": pw, "psc": psc})
    res = _run("LP", build_LP, maps)
    return [r["o"] for r in res]


GN_EPS = 1e-5 * 64
NEG_E05 = -0.6065306597126334


def build_LR(P=None, bind=None, opt=None):
    P, nc = _begin(P, bind)
    opt = opt or {}
    rd = P.din("r", [128, TA])
    kd = P.din("k", [128, TA])
    vd = P.din("v", [128, TA])
    lrd = P.din("lr", [128, TA])
    gd = P.din("g", [64, TA])
    zwd = P.din("zw", [128, 4, 128])
    g2d = P.din("g2c", [64, 128])
    vecd = P.din("vec", [128, 10])
    mkd = P.din("masks", [128, 5, 128])
    bd1d = P.din("bd1", [128, 128])
    bdmd = P.din("bdm", [128, 128])
    rmd = P.din("rmask", [128, 256])
    od = P.dout("o", [128, TA])
    T = 256
    zw = P.sb([128, 4, 128])
    g2c = P.sb([64, 128])
    vec = P.sb([128, 12])
    mk = P.sb([128, 5, 128])
    bd1 = P.sb([128, 128])
    bdm = P.sb([128, 128])
    rmask = P.sb([128, T])
    gneps = P.sb([128, 1])
    y = P.sb([128, TA])
    o = P.sb([128, TA])
    for (sbt, dt_) in ((zw, zwd), (g2c, g2d), (mk, mkd), (bd1, bd1d), (bdm, bdmd), (rmask, rmd)):
        P.dma(sbt[:], dt_[:], reads=[dt_], writes=[sbt])
    P.dma(vec[:, 0:10], vecd[:], reads=[vecd], writes=[vec])
    P.memset("pool", gneps, gneps[:], GN_EPS)
    P.ts("dve", vec, vec[:, 10:11], vec, vec[:, 5:6], -1.0, 1.0, ALU.mult, ALU.add)
    P.ts("dve", vec, vec[:, 11:12], vec, vec[:, 5:6], -2.0, 2.0, ALU.mult, ALU.add)
    ident = Buf(mk.t[:, 4, :], "ident")

    def blk(n, shape=(128, T), k=2):
        return Ring([P.sb(list(shape), name="%s%d" % (n, i)) for i in range(k)])

    rb_r, kb_r, vb_r, lr_r = blk("rb"), blk("kb"), blk("vb"), blk("lrb")
    gb_r = blk("gb", (64, T))
    lrA_r, lw_r, a_r, kk_r, sq_r, rn_r, kap_r, nb_r, tmp_r, kmod_r = (blk(n) for n in
        ("lrA", "lw", "a", "kk", "sq", "rn", "kap", "nb", "tmp", "kmod"))
    gp_r, gm_r, en_r, ep_r, ekm_r = (blk(n) for n in ("gp", "gm", "en", "ep", "ekm"))
    def bdr(n):
        tiles = [P.sb([128, 128], name="%s%d" % (n, i)) for i in range(2)]
        for t_ in tiles:
            P.memset("pool", t_, t_[:], 0.0)
        return Ring(tiles)
    def bd4(n):
        tiles = [P.sb([128, 128], name="%s%d" % (n, i)) for i in range(4)]
        for t_ in tiles:
            P.memset("pool", t_, t_[:], 0.0)
        return tiles
    kt, nbt, kkt, rt, Kh, Bh, vB = (bd4(n) for n in ("kt", "nbt", "kkt", "rt", "Kh", "Bh", "vB"))
    def sq4(n):
        return [P.sb([128, 128], name="%s%d" % (n, i)) for i in range(4)]
    def sq128(n, k=2):
        return Ring([P.sb([128, 128], name="%s%d" % (n, i)) for i in range(k)])
    Nn, NTn, Akk, Ark, Arb, Vbd, ktb, Khb, Bhb, X1, WT = (sq4(n) for n in
        ("N", "NT", "Akk", "Ark", "Arb", "Vbd", "ktb", "Khb", "Bhb", "X1", "WT"))
    TTr = [sq128("TT%d_" % j, 3) for j in range(4)]
    Pwr = [sq128("Pw%d_" % j, 3) for j in range(4)]
    PTwr = [sq128("PTw%d_" % j, 3) for j in range(4)]
    U_r, H_r = sq128("U"), sq128("H")
    yt_r = Ring([P.sb([128, 64], name="yt%d" % i) for i in range(2)])
    pq = Ring([P.ps([128, 128], name="pq%d" % i) for i in range(6)])
    pbig = Ring([P.ps([128, T], name="pbig%d" % i) for i in range(2)])

    def mmq(lhsT, rhs):
        ps = pq.next()
        P.mm(ps, ps[:], lhsT, lhsT[:], rhs, rhs[:], True, True)
        return ps

    def transp(x):
        ps = pq.next()
        P.op("pe", lambda e: e.transpose(ps[:], x[:], ident[:]), reads=[x, ident], writes=[ps])
        return ps

    def load_block(bi, need_g=False):
        t0 = bi * T
        rb, kb, vb, lrb = rb_r.next(), kb_r.next(), vb_r.next(), lr_r.next()
        P.dma(rb[:], rd[:, t0:t0 + T], reads=[rd], writes=[rb])
        P.dma(kb[:], kd[:, t0:t0 + T], reads=[kd], writes=[kb], eng="pool")
        P.dma(vb[:], vd[:, t0:t0 + T], reads=[vd], writes=[vb])
        P.dma(lrb[:], lrd[:, t0:t0 + T], reads=[lrd], writes=[lrb], eng="pool")
        gb = None
        if need_g:
            gb = gb_r.next()
            P.dma(gb[:], gd[:, t0:t0 + T], reads=[gd], writes=[gb])
        return rb, kb, vb, lrb, gb

    def sigm_lowrank(lhs_idx, rhs, bias_col, out):
        ps = pbig.next()
        P.mm(ps, ps[:], zw, zw[:, lhs_idx, :], rhs, rhs[:], True, True)
        P.act(out, out[:], ps, ps[:], AF.Sigmoid, bias=vec[:, bias_col:bias_col + 1], extra_reads=[vec])

    for d in range(2):
        H = H_r.next()
        P.memset("pool", H, H[:], 0.0)
        if d == 0:
            msl, msu, mui = 0, 1, 2
            blocks = list(range(17))
        else:
            msl, msu, mui = 1, 0, 3
            blocks = [0] + list(range(16, 0, -1))
        def pre_block(bi):
            rb, kb, vb, lrb, _ = load_block(bi)
            lrA = lrA_r.next()
            P.act(lrA, lrA[0:64, :], lrb, lrb[0:64, :], AF.Tanh)
            P.copy("pool", lrA, lrA[64:128, :], lrb, lrb[64:128, :])
            lw = lw_r.next()
            sigm_lowrank(d, lrA, d, lw)
            P.ts("dve", lw, lw[:], lw, lw[:], NEG_E05, None, ALU.mult)
            a = a_r.next()
            sigm_lowrank(2 + d, lrA, 2 + d, a)
            kk = kk_r.next()
            P.ts("dve", kk, kk[:], kb, kb[:], vec[:, 4:5], None, ALU.mult, extra_reads=[vec])
            sq = sq_r.next()
            P.act(sq, sq[:], kk, kk[:], AF.Square)
            ps = pbig.next()
            P.mm(ps, ps[:], bd1, bd1[:], sq, sq[:], True, True)
            rn = rn_r.next()
            P.act(rn, rn[:], ps, ps[:], AF.Sqrt)
            P.ts("dve", rn, rn[:], rn, rn[:], 1e-12, None, ALU.max)
            P.op("dve", lambda e: e.reciprocal(out=rn[:], in_=rn[:]), reads=[rn], writes=[rn])
            kap = kap_r.next()
            P.tt("dve", kap, kap[:], kk, kk[:], rn, rn[:], ALU.mult)
            nb = nb_r.next()
            P.stt(nb, nb[:], kap, kap[:], -1.0, a, a[:], ALU.mult, ALU.mult)
            tmp = tmp_r.next()
            P.ts("dve", tmp, tmp[:], a, a[:], vec[:, 5:6], vec[:, 10:11], ALU.mult, ALU.add, extra_reads=[vec])
            kmod = kmod_r.next()
            P.tt("pool", kmod, kmod[:], kb, kb[:], tmp, tmp[:], ALU.mult)
            gp = gp_r.next()
            P.op("dve", lambda e: e.tensor_tensor_scan(out=gp[:], data0=rmask[:], data1=lw[:], initial=0.0,
                                                       op0=ALU.mult, op1=ALU.add),
                 reads=[rmask, lw], writes=[gp])
            if d == 1:
                g2_ = gm_r.next()
                for c in range(4):
                    P.ts("dve", g2_, g2_[:, c * 64:(c + 1) * 64], gp, gp[:, c * 64:(c + 1) * 64],
                         gp[:, c * 64 + 63:c * 64 + 64], -1.0, ALU.subtract, ALU.mult)
                gfull = gp_r.next()
                P.tt("dve", gfull, gfull[:], g2_, g2_[:], lw, lw[:], ALU.add)
            else:
                gfull = gp
            gm = gm_r.next()
            P.tt("pool", gm, gm[:], gfull, gfull[:], lw, lw[:], ALU.subtract)
            en, ep, ekm = en_r.next(), ep_r.next(), ekm_r.next()
            P.act(en, en[:], gfull, gfull[:], AF.Exp, scale=-1.0)
            P.act(ep, ep[:], gfull, gfull[:], AF.Exp)
            P.act(ekm, ekm[:], gm, gm[:], AF.Exp)
            return dict(bi=bi, rb=rb, vb=vb, kap=kap, nb=nb, kmod=kmod, en=en, ep=ep, ekm=ekm)

        def prep_block(cur):
            bi, rb, vb, kap, nb, kmod, en, ep, ekm = (cur[k_] for k_ in ('bi', 'rb', 'vb', 'kap', 'nb', 'kmod', 'en', 'ep', 'ekm'))
            chunks = [0, 1, 2, 3] if d == 0 else [3, 2, 1, 0]
            J = range(4)
            css = [slice(c * 64, (c + 1) * 64) for c in chunks]
            gcols = [(c * 64 + 63) if d == 0 else (c * 64) for c in chunks]
            for j in J:
                cs, gcol = css[j], gcols[j]
                for hf in range(2):
                    pp = slice(64 * hf, 64 * hf + 64)
                    P.tt("pool", kt[j], kt[j][pp, pp], kap, kap[pp, cs], ekm, ekm[pp, cs], ALU.mult)
                    P.tt("pool", nbt[j], nbt[j][pp, pp], nb, nb[pp, cs], en, en[pp, cs], ALU.mult)
                    P.tt("pool", kkt[j], kkt[j][pp, pp], kmod, kmod[pp, cs], en, en[pp, cs], ALU.mult)
                    P.tt("pool", rt[j], rt[j][pp, pp], rb, rb[pp, cs], ep, ep[pp, cs], ALU.mult)
                    P.stt(Kh[j], Kh[j][pp, pp], kmod, kmod[pp, cs], ep[pp, gcol:gcol + 1], en, en[pp, cs], ALU.mult, ALU.mult,
                          extra_reads=[ep])
                    P.stt(Bh[j], Bh[j][pp, pp], nb, nb[pp, cs], ep[pp, gcol:gcol + 1], en, en[pp, cs], ALU.mult, ALU.mult,
                          extra_reads=[ep])
                    P.copy("act", vB[j], vB[j][pp, pp], vb, vb[pp, cs])
            for j in J:
                ps = mmq(kt[j], nbt[j])
                P.tt("dve", Nn[j], Nn[j][:], ps, ps[:], mk, mk[:, msl, :], ALU.mult)
            for j in J:
                ps = mmq(nbt[j], kt[j])
                P.tt("dve", NTn[j], NTn[j][:], ps, ps[:], mk, mk[:, msu, :], ALU.mult)
            TT = [TTr[j].next() for j in J]
            for j in J:
                P.tt("pool", TT[j], TT[j][:], NTn[j], NTn[j][:], ident, ident[:], ALU.add)
            Pc = [Nn[j] for j in J]
            PTc = [NTn[j] for j in J]
            for i in range(5):
                P2 = [Pwr[j].next() for j in J]
                for j in J:
                    ps = mmq(PTc[j], Pc[j])
                    P.copy("act", P2[j], P2[j][:], ps, ps[:])
                if i < 4:
                    PT2 = [PTwr[j].next() for j in J]
                    for j in J:
                        ps2 = mmq(Pc[j], PTc[j])
                        P.copy("dve", PT2[j], PT2[j][:], ps2, ps2[:])
                    PTc = PT2
                Pc = P2
                TTn = [TTr[j].next() for j in J]
                for j in J:
                    ps = mmq(Pc[j], TT[j])
                    P.tt("dve", TTn[j], TTn[j][:], ps, ps[:], TT[j], TT[j][:], ALU.add)
                TT = TTn
            for j in J:
                ps = mmq(kkt[j], kt[j])
                P.tt("dve", Akk[j], Akk[j][:], ps, ps[:], mk, mk[:, msu, :], ALU.mult)
            for j in J:
                ps = mmq(kkt[j], rt[j])
                P.tt("dve", Ark[j], Ark[j][:], ps, ps[:], mk, mk[:, mui, :], ALU.mult)
            for j in J:
                ps = mmq(nbt[j], rt[j])
                P.tt("dve", Arb[j], Arb[j][:], ps, ps[:], mk, mk[:, mui, :], ALU.mult)
            for (srcs, dsts) in ((vB, Vbd), (kt, ktb), (Kh, Khb), (Bh, Bhb)):
                for j in J:
                    ps = transp(srcs[j])
                    P.copy("act", dsts[j], dsts[j][:], ps, ps[:])
            for j in J:
                ps = mmq(Akk[j], Vbd[j])
                P.copy("act", X1[j], X1[j][:], ps, ps[:])
            for j in J:
                ps = mmq(ktb[j], TT[j])
                P.copy("dve", WT[j], WT[j][:], ps, ps[:])
            return dict(chunks=chunks, gcols=gcols, TT=TT)

        def state_block(cur, pr, H):
            bi, rb, vb, kap, nb, kmod, en, ep, ekm = (cur[k_] for k_ in ('bi', 'rb', 'vb', 'kap', 'nb', 'kmod', 'en', 'ep', 'ekm'))
            chunks, gcols, TT = pr['chunks'], pr['gcols'], pr['TT']
            J = range(4)
            for j in J:
                c, gcol = chunks[j], gcols[j]
                U = U_r.next()
                ps = pq.next()
                P.mm(ps, ps[:], TT[j], TT[j][:], X1[j], X1[j][:], True, False)
                P.mm(ps, ps[:], WT[j], WT[j][:], H, H[:], False, True)
                P.copy("act", U, U[:], ps, ps[:])
                psy = pq.next()
                P.mm(psy, psy[:], H, H[:], rt[j], rt[j][:], True, False)
                P.mm(psy, psy[:], Vbd[j], Vbd[j][:], Ark[j], Ark[j][:], False, False)
                P.mm(psy, psy[:], U, U[:], Arb[j], Arb[j][:], False, True)
                psh = pq.next()
                P.mm(psh, psh[:], Khb[j], Khb[j][:], Vbd[j], Vbd[j][:], True, False)
                P.mm(psh, psh[:], Bhb[j], Bhb[j][:], U, U[:], False, True)
                Hn = H_r.next()
                P.stt(Hn, Hn[:], H, H[:], ep[:, gcol:gcol + 1], psh, psh[:], ALU.mult, ALU.add, extra_reads=[ep])
                H = Hn
                g0 = bi * T + c * 64
                yt = yt_r.next()
                if d == 0:
                    P.copy("act", yt, yt[:], psy, psy[:, 0:64])
                    P.tt("dve", y, y[:, g0:g0 + 64], psy, psy[:, 64:128], yt, yt[:], ALU.add)
                else:
                    P.tt("dve", yt, yt[:], psy, psy[:, 0:64], y, y[:, g0:g0 + 64], ALU.add)
                    P.tt("dve", y, y[:, g0:g0 + 64], psy, psy[:, 64:128], yt, yt[:], ALU.add)
            return H

        cur = pre_block(blocks[0])
        for idx in range(len(blocks)):
            pr = prep_block(cur)
            nxt = pre_block(blocks[idx + 1]) if idx + 1 < len(blocks) else None
            H = state_block(cur, pr, H)
            cur = nxt
    for bi in range(17):
        t0 = bi * T
        rb, kb, vb, lrb, gb = load_block(bi, need_g=True)
        af, ab = a_r.next(), tmp_r.next()
        sigm_lowrank(2, lrb, 2, af)
        sigm_lowrank(3, lrb, 3, ab)
        s = kk_r.next()
        P.tt("dve", s, s[:], af, af[:], ab, ab[:], ALU.add)
        P.ts("dve", s, s[:], s, s[:], vec[:, 5:6], vec[:, 11:12], ALU.mult, ALU.add, extra_reads=[vec])
        P.tt("dve", s, s[:], s, s[:], kb, kb[:], ALU.mult)
        P.stt(s, s[:], s, s[:], vec[:, 6:7], rb, rb[:], ALU.mult, ALU.mult, extra_reads=[vec])
        ps = pbig.next()
        P.mm(ps, ps[:], bd1, bd1[:], s, s[:], True, True)
        bon = kap_r.next()
        P.tt("dve", bon, bon[:], ps, ps[:], vb, vb[:], ALU.mult)
        ps = pbig.next()
        P.mm(ps, ps[:], bdm, bdm[:], y, y[:, t0:t0 + T], True, True)
        yc = nb_r.next()
        P.tt("dve", yc, yc[:], y, y[:, t0:t0 + T], ps, ps[:], ALU.subtract)
        sq = sq_r.next()
        P.act(sq, sq[:], yc, yc[:], AF.Square)
        ps = pbig.next()
        P.mm(ps, ps[:], bdm, bdm[:], sq, sq[:], True, True)
        rn = rn_r.next()
        P.act(rn, rn[:], ps, ps[:], AF.Sqrt, bias=gneps[:, 0:1], extra_reads=[gneps])
        P.op("dve", lambda e: e.reciprocal(out=rn[:], in_=rn[:]), reads=[rn], writes=[rn])
        P.tt("dve", yc, yc[:], yc, yc[:], rn, rn[:], ALU.mult)
        P.ts("dve", yc, yc[:], yc, yc[:], vec[:, 7:8], vec[:, 8:9], ALU.mult, ALU.add, extra_reads=[vec])
        P.tt("pool", yc, yc[:], yc, yc[:], bon, bon[:], ALU.add)
        sg = gb_r.next()
        P.act(sg, sg[:], gb, gb[:], AF.Sigmoid)
        ps = pbig.next()
        P.mm(ps, ps[:], g2c, g2c[:], sg, sg[:], True, True)
        P.tt("dve", o, o[:, t0:t0 + T], ps, ps[:], yc, yc[:], ALU.mult)
    _store_mix(P, opt, od, o)
    P.finish([od])
    return nc


def _lr_masks():
    tt_, ss_ = np.meshgrid(np.arange(128), np.arange(128), indexing="ij")
    same = (tt_ // 64) == (ss_ // 64)
    sl = (same & (ss_ < tt_)).astype(np.float32)
    su = np.ascontiguousarray(sl.T)
    ui = (same & (tt_ <= ss_)).astype(np.float32)
    li = np.ascontiguousarray(ui.T)
    return np.ascontiguousarray(np.stack([sl, su, ui, li, np.eye(128, dtype=np.float32)], 1))


def host_LR(inp, l, PT):
    masks = _lr_masks()
    rmask = np.ones((128, 256), np.float32)
    rmask[:, ::64] = 0.0
    maps = []
    for i in range(NCORES):
        b, hp = i // 2, i % 2
        hs = slice(hp * 128, (hp + 1) * 128)
        T_ = PT[b]
        zw = np.zeros((128, 4, 128), np.float32)
        zw[0:32, 0] = inp["rwkv_w2"][l][0][:, hs]
        zw[32:64, 1] = inp["rwkv_w2"][l][1][:, hs]
        zw[64:96, 2] = inp["rwkv_a2"][l][0][:, hs]
        zw[96:128, 3] = inp["rwkv_a2"][l][1][:, hs]
        vec = np.zeros((128, 10), np.float32)
        vec[:, 0] = inp["rwkv_w0"][l][0][hs]
        vec[:, 1] = inp["rwkv_w0"][l][1][hs]
        vec[:, 2] = inp["rwkv_a0"][l][0][hs]
        vec[:, 3] = inp["rwkv_a0"][l][1][hs]
        vec[:, 4] = inp["rwkv_k_k"][l][hs]
        vec[:, 5] = inp["rwkv_k_a"][l][hs]
        vec[:, 6] = inp["rwkv_r_k"][l].reshape(-1)[hs]
        vec[:, 7] = inp["rwkv_gn_w"][l][hs]
        vec[:, 8] = inp["rwkv_gn_b"][l][hs]
        maps.append({"r": np.ascontiguousarray(T_[:, 0 + hp * 128:0 + (hp + 1) * 128].T),
                     "k": np.ascontiguousarray(T_[:, 256 + hp * 128:256 + (hp + 1) * 128].T),
                     "v": np.ascontiguousarray(T_[:, 512 + hp * 128:512 + (hp + 1) * 128].T),
                     "lr": np.ascontiguousarray(T_[:, 768:896].T),
                     "g": np.ascontiguousarray(T_[:, 896:960].T),
                     "zw": zw, "g2c": np.ascontiguousarray(inp["rwkv_g2"][l][:, hs]), "vec": vec,
                     "masks": masks, "bd1": _bd(1.0), "bdm": _bd(1.0 / 64), "rmask": rmask})
    res = _run("LR", build_LR, maps)
    return [r["o"] for r in res]


def kernel_unfused(**inp):
    inp = {k: np.asarray(v, np.float32) for k, v in inp.items()}
    mod = host_L0(inp)
    xTs = shard_tokens(inp["x"], inp["ctx"])
    for l in range(DEPTH):
        resA = host_LA(inp, l, xTs, mod)
        PT, VT = host_PT(resA)
        parts = [host_LR(inp, l, PT), host_LB(inp, l, PT, VT), host_LC(inp, l, PT, VT), host_LP(inp, l, PT)]
        mix = np.zeros((B, TA, D), np.float32)
        for i in range(NCORES):
            b, hp = i // 2, i % 2
            for j in range(4):
                mix[b][:, j * 256 + hp * 128:j * 256 + (hp + 1) * 128] = parts[j][i].T
        mTs = shard_tokens(mix[:, CTX:], mix[:, :CTX])
        res = host_LD(inp, l, xTs, mTs, mod, l == DEPTH - 1)
        xTs = [r["oT"] for r in res]
    out, _ = unshard_tokens(xTs)
    return out


def emit_MOD(P, cT, ada_w, abf, n1g, n2g, fg, modS_d, vecA, vecD, depth):
    P.begin_phase({})
    c_sb = P.sb([128, 8, 2])
    sg = P.sb([128, 8, 2])
    ab = P.sb([128, 4, 6, 8])
    modS = P.sb([128, 4, 2, 6, 8])
    wr = Ring([P.sb([128, 8, 1024]) for _ in range(2)])
    pss = Ring([P.ps([128, 8, 2]) for _ in range(2)])
    P.dma(c_sb[:], cT[:], reads=[cT], writes=[c_sb])
    P.dma(ab[:], abf[:], reads=[abf], writes=[ab])
    P.act(sg, sg[:], c_sb, c_sb[:], AF.Sigmoid)
    P.tt("dve", sg, sg[:], sg, sg[:], c_sb, c_sb[:], ALU.mult)
    for l in range(depth):
        awv = ada_w.t[l].rearrange("(kc p) n -> p kc n", p=128)
        for j in range(6):
            wb = wr.next()
            P.dma(wb[:], awv[:, :, j * 1024:(j + 1) * 1024], reads=[ada_w], writes=[wb])
            ps = pss.next()
            for fc in range(8):
                for kc in range(8):
                    P.mm(ps, ps[:, fc, :], wb, wb[:, kc, fc * 128:(fc + 1) * 128], sg, sg[:, kc, :], kc == 0, kc == 7)
            for r in range(2):
                P.tt("dve", modS, modS[:, l, r, j, :], ps, ps[:, :, r], ab, ab[:, l, j, :], ALU.add)
    P.dma(modS_d[:], modS[:], reads=[modS], writes=[modS_d])
    for l in range(depth):
        P.dma(vecA[l, :, 0, :], n1g[l], reads=[n1g], writes=[vecA])
        P.dma(vecA[l, :, 1:3, :], modS_d[:, l, 0, 0:2, :], reads=[modS_d], writes=[vecA])
        P.dma(vecA[l, :, 3:5, :], modS_d[:, l, 1, 0:2, :], reads=[modS_d], writes=[vecA])
        P.dma(vecD[l, :, 0, :], n2g[l], reads=[n2g], writes=[vecD])
        P.dma(vecD[l, :, 1:5, :], modS_d[:, l, 0, 2:6, :], reads=[modS_d], writes=[vecD])
        P.dma(vecD[l, :, 5:9, :], modS_d[:, l, 1, 2:6, :], reads=[modS_d], writes=[vecD])
        P.dma(vecD[l, :, 9, :], fg[:], reads=[fg], writes=[vecD])
    P.end_phase()


def build_FUSED(depth=DEPTH, final=True):
    nc = bass.Bass("TRN2", target_bir_lowering=False)
    P = Prog(nc)
    P.standalone = False
    e = {}
    for name, shape in (("xT0", [1024, TA]), ("cT", [128, 8, 2]), ("ada_w", [DEPTH, 1024, 6144]), ("abf", [128, 4, 6, 8]),
                        ("n1g", [DEPTH, 128, 8]), ("n2g", [DEPTH, 128, 8]), ("fg", [128, 8]),
                        ("w_in", [DEPTH, 1024, D_IN]), ("w_out", [DEPTH, 1024, 1024]),
                        ("w1", [DEPTH, 32, 128, 8, 128]), ("w2", [DEPTH, 4096, 1024]),
                        ("zw", [DEPTH, 2, 128, 4, 128]), ("g2c", [DEPTH, 2, 64, 128]), ("rvec", [DEPTH, 2, 128, 10]),
                        ("masks", [128, 5, 128]), ("bd1", [128, 128]), ("bdm", [128, 128]), ("rmask", [128, 256]),
                        ("bias8", [DEPTH, 2, 128, 2, 8, 4, 64]), ("onesab", [128, 2, 128]), ("gains", [DEPTH, 128, 2]),
                        ("cosf", [128, SEQ]), ("sinf", [128, SEQ]), ("prot", [128, 128]), ("bd64", [128, 128]),
                        ("invcnt", [2, 128, TA]), ("sel", [2, 128, 4]), ("pw", [DEPTH, 2, 128, 128]),
                        ("psc", [DEPTH, 2, 128, 1])):
        e[name] = P.din(name, shape)
    out = P.dout("oT", [1024, 2048])
    par = nc.sync.partition_id() % 2
    xsel = P.scratch("xsel", [1024, 2048])
    msel = P.scratch("msel", [1024, 2048])
    xs = [P.scratch("xA", [1024, TA]), P.scratch("xB", [1024, TA])]
    pT = P.scratch("pTs", [D_IN, TA])
    vtok = P.scratch("vtoks", [TA, 384])
    mixT = P.scratch("mixTs", [1024, TA])
    modS_d = P.scratch("modS", [128, 4, 2, 6, 8])
    vecA = P.scratch("vecA", [DEPTH, 128, 5, 8])
    vecD = P.scratch("vecD", [DEPTH, 128, 10, 8])

    def V(ap, name="v"):
        return Buf(ap, name)

    emit_MOD(P, e["cT"], e["ada_w"], e["abf"], e["n1g"], e["n2g"], e["fg"], modS_d, vecA, vecD, depth)
    xcur = e["xT0"]
    tilesA = [(0, 256, 1)] + [(CTX + i * 512, 512, 0) for i in range(8)]
    tilesD = [(0, 256, 1)] + [(CTX + i * 256, 256, 0) for i in range(16)]
    dummy = e["bd1"]
    for l in range(depth):
        build_LA(P, {"xT": xcur, "w": V(e["w_in"].t[l]), "vec": V(vecA.t[l]), "pT": pT, "vtok": vtok},
                 {"ntok": TA, "tiles": tilesA})
        for hp in range(2):
            rows = lambda r0, n=128: V(pT.t[r0:r0 + n, :])
            build_LR(P, {"r": rows(hp * 128), "k": rows(256 + hp * 128), "v": rows(512 + hp * 128),
                         "lr": rows(768), "g": rows(896, 64), "zw": V(e["zw"].t[l, hp]), "g2c": V(e["g2c"].t[l, hp]),
                         "vec": V(e["rvec"].t[l, hp]), "masks": e["masks"], "bd1": e["bd1"], "bdm": e["bdm"],
                         "rmask": e["rmask"], "o": V(mixT.t[hp * 128:(hp + 1) * 128, :])})
            build_LB(P, {"q": rows(960 + hp * 128), "k": rows(1216 + hp * 128), "vpad": dummy,
                         "bias8": V(e["bias8"].t[l, hp]), "onesab": e["onesab"],
                         "o": V(mixT.t[256 + hp * 128:256 + (hp + 1) * 128, :])},
                     {"vsrc": V(vtok.t[:, hp * 128:(hp + 1) * 128])})
            optC = {"ksrc": V(pT.t[1984 + hp * 64:1984 + (hp + 1) * 64, :]),
                    "vsrc": V(vtok.t[:, 256 + hp * 64:256 + (hp + 1) * 64])}
            if l == depth - 1:
                optC.update(own=par, osel=V(msel.t[512 + hp * 128:512 + (hp + 1) * 128, :]))
            build_LC(P, {"q": rows(1728 + hp * 128), "k2": dummy, "vpad": dummy, "gains": V(e["gains"].t[l]),
                         "cosf": e["cosf"], "sinf": e["sinf"], "prot": e["prot"], "bd64": e["bd64"],
                         "onesab": e["onesab"], "o": V(mixT.t[512 + hp * 128:512 + (hp + 1) * 128, :])}, optC)
            build_LP(P, {"x": rows(2240 + hp * 128), "invcnt": V(e["invcnt"].t[hp]), "sel": V(e["sel"].t[hp]),
                         "pw": V(e["pw"].t[l, hp]), "psc": V(e["psc"].t[l, hp]),
                         "o": V(mixT.t[768 + hp * 128:768 + (hp + 1) * 128, :])})
        last = (l == depth - 1)
        if last:
            P.begin_phase({})
            off = par * 2048 + CTX
            for r0 in range(0, 1024, 256):
                P.dma(xsel.t[r0:r0 + 256, :], xcur.t[r0:r0 + 256, bass.ds(off, 2048)], reads=[xcur], writes=[xsel])
                if r0 != 512:
                    P.dma(msel.t[r0:r0 + 256, :], mixT.t[r0:r0 + 256, bass.ds(off, 2048)], reads=[mixT], writes=[msel])
            P.end_phase()
            build_LD(final, P, {"xT": xsel, "mT": msel, "wo": V(e["w_out"].t[l]), "w1": V(e["w1"].t[l]),
                                "w2": V(e["w2"].t[l]), "vec": V(vecD.t[l]), "oT": out},
                     {"ntok": 2048, "tiles": [(i * 256, 256, 0) for i in range(8)]})
        else:
            xnext = xs[l % 2]
            build_LD(False, P, {"xT": xcur, "mT": mixT, "wo": V(e["w_out"].t[l]), "w1": V(e["w1"].t[l]),
                                "w2": V(e["w2"].t[l]), "vec": V(vecD.t[l]), "oT": xnext},
                     {"ntok": TA, "tiles": tilesD})
            xcur = xnext
    P.es.close()
    return nc


def fused_inputs(inp):
    C = consts()
    f32 = np.float32
    ic4 = _pool_invcnt()
    invcnt = np.stack([np.repeat(ic4[2 * hp:2 * hp + 2], 64, axis=0) for hp in range(2)], 0).astype(f32)
    sel = np.zeros((2, 128, 4), f32)
    for hp in range(2):
        sel[hp, :64, 2 * hp] = 1.0
        sel[hp, 64:, 2 * hp + 1] = 1.0
    rmask = np.ones((128, 256), f32)
    rmask[:, ::64] = 0.0
    zw = np.zeros((DEPTH, 2, 128, 4, 128), f32)
    rvec = np.zeros((DEPTH, 2, 128, 10), f32)
    g2c = np.zeros((DEPTH, 2, 64, 128), f32)
    bias8 = np.zeros((DEPTH, 2, 128, 2, 8, 4, 64), f32)
    pw = np.zeros((DEPTH, 2, 128, 128), f32)
    psc = np.zeros((DEPTH, 2, 128, 1), f32)
    gains = np.zeros((DEPTH, 128, 2), f32)
    for l in range(DEPTH):
        bt = _na_bias_tables(inp["na_rpb"][l])
        gains[l, :, 0] = np.tile(inp["gqa_q_gain"][l], 2)
        gains[l, :, 1] = np.tile(inp["gqa_k_gain"][l], 2)
        for hp in range(2):
            hs = slice(hp * 128, (hp + 1) * 128)
            zw[l, hp, 0:32, 0] = inp["rwkv_w2"][l][0][:, hs]
            zw[l, hp, 32:64, 1] = inp["rwkv_w2"][l][1][:, hs]
            zw[l, hp, 64:96, 2] = inp["rwkv_a2"][l][0][:, hs]
            zw[l, hp, 96:128, 3] = inp["rwkv_a2"][l][1][:, hs]
            v = rvec[l, hp]
            v[:, 0] = inp["rwkv_w0"][l][0][hs]
            v[:, 1] = inp["rwkv_w0"][l][1][hs]
            v[:, 2] = inp["rwkv_a0"][l][0][hs]
            v[:, 3] = inp["rwkv_a0"][l][1][hs]
            v[:, 4] = inp["rwkv_k_k"][l][hs]
            v[:, 5] = inp["rwkv_k_a"][l][hs]
            v[:, 6] = inp["rwkv_r_k"][l].reshape(-1)[hs]
            v[:, 7] = inp["rwkv_gn_w"][l][hs]
            v[:, 8] = inp["rwkv_gn_b"][l][hs]
            g2c[l, hp] = inp["rwkv_g2"][l][:, hs]
            bias8[l, hp] = bt[2 * hp:2 * hp + 2].transpose(2, 0, 1, 3, 4)
            pw[l, hp, :64, :64] = inp["pool_w"][l][2 * hp]
            pw[l, hp, 64:, 64:] = inp["pool_w"][l][2 * hp + 1]
            psc[l, hp, :, 0] = inp["pool_scale"][l][hs]
    shared = {
        "ada_w": np.ascontiguousarray(inp["ada_w"]),
        "abf": np.ascontiguousarray(inp["ada_b"].reshape(DEPTH, 6, 8, 128).transpose(3, 0, 1, 2)),
        "n1g": np.ascontiguousarray(inp["norm1_g"].reshape(DEPTH, 8, 128).transpose(0, 2, 1)),
        "n2g": np.ascontiguousarray(inp["norm2_g"].reshape(DEPTH, 8, 128).transpose(0, 2, 1)),
        "fg": fm(inp["final_g"]),
        "w_in": np.ascontiguousarray(inp["w_in"]), "w_out": np.ascontiguousarray(inp["w_out"]),
        "w1": np.ascontiguousarray(inp["mlp_w1"].reshape(DEPTH, 8, 128, 32, 128).transpose(0, 3, 2, 1, 4)),
        "w2": np.ascontiguousarray(inp["mlp_w2"]),
        "zw": zw, "g2c": g2c, "rvec": rvec, "masks": _lr_masks(), "bd1": _bd(1.0), "bdm": _bd(1.0 / 64), "rmask": rmask,
        "bias8": bias8, "onesab": C["onesab"], "gains": gains, "cosf": C["cosf"], "sinf": C["sinf"], "prot": C["prot"],
        "bd64": C["bd64"], "invcnt": invcnt, "sel": sel, "pw": pw, "psc": psc,
    }
    maps = []
    for i in range(NCORES):
        b = i // 2
        m = dict(shared)
        m["xT0"] = np.ascontiguousarray(np.concatenate([inp["ctx"][b], inp["x"][b]], 0).T)
        c2 = np.stack([inp["c"][b], inp["c_ctx"]], 1)
        m["cT"] = np.ascontiguousarray(c2.reshape(8, 128, 2).transpose(1, 0, 2))
        maps.append(m)
    return maps


OWN_COLS = 1344
OWN_PCH = [(0, 128, 0), (128, 128, 128), (256, 128, 256), (384, 128, 384), (512, 64, 512),
           (576, 128, 576), (704, 128, 704), (832, 128, 832), (960, 64, 960), (1024, 128, 1024)]
OWN_VCH = [(1152, 128, 0), (1280, 64, 128)]


def _own_cols(hp):
    segs = [(hp * 128, 128), (256 + hp * 128, 128), (512 + hp * 128, 128), (768, 128), (896, 64),
            (960 + hp * 128, 128), (1216 + hp * 128, 128), (1728 + hp * 128, 128), (1984 + hp * 64, 64),
            (2240 + hp * 128, 128), (1472 + hp * 128, 128), (2112 + hp * 64, 64)]
    return np.concatenate([np.arange(a, a + n) for a, n in segs])


DBG = {}


def build_FUSED_paired(depth=DEPTH, final=True):
    nc = bass.Bass("TRN2", target_bir_lowering=False, num_devices=NCORES)
    P = Prog(nc)
    P.standalone = False
    e = {}
    for name, shape in (("x0", [2, 1024, NT]), ("cT", [128, 8, 2]), ("ada_w", [DEPTH, 1024, 6144]), ("abf", [128, 4, 6, 8]),
                        ("n1g", [DEPTH, 128, 8]), ("n2g", [DEPTH, 128, 8]), ("fg", [128, 8]),
                        ("w_in", [DEPTH, 1024, OWN_COLS]), ("w_out", [DEPTH, 1024, 1024]),
                        ("w1", [DEPTH, 32, 128, 8, 128]), ("w2", [DEPTH, 4096, 1024]),
                        ("zw", [DEPTH, 128, 4, 128]), ("g2c", [DEPTH, 64, 128]), ("rvec", [DEPTH, 128, 10]),
                        ("masks", [128, 5, 128]), ("bd1", [128, 128]), ("bdm", [128, 128]), ("rmask", [128, 256]),
                        ("bias8", [DEPTH, 128, 2, 8, 4, 64]), ("onesab", [128, 2, 128]), ("gains", [DEPTH, 128, 2]),
                        ("cosf", [128, SEQ]), ("sinf", [128, SEQ]), ("prot", [128, 128]), ("bd64", [128, 128]),
                        ("invcnt", [128, TA]), ("sel", [128, 4]), ("pw", [DEPTH, 128, 128]),
                        ("psc", [DEPTH, 128, 1])):
        e[name] = P.din(name, shape)
    out = P.dout("oT", [1024, NT])
    par = nc.sync.partition_id() % 2
    XS = [Buf(nc.dram_tensor("XS%d" % i, [2, 1024, NT], F32, kind="Internal", addr_space="Shared").ap(), "XS%d" % i)
          for i in range(2)]
    MS = Buf(nc.dram_tensor("MS", [2, 2, 512, NT], F32, kind="Internal", addr_space="Shared").ap(), "MS")
    pT = P.scratch("pTs", [1152, TA])
    vtok = P.scratch("vtoks", [TA, 192])
    mixL = P.scratch("mixL", [2, 512, NT])
    mloc = P.scratch("mloc", [1024, NT])
    xloc = P.scratch("xloc", [1024, NT])
    xout = P.scratch("xout", [1024, NT])
    modS_d = P.scratch("modS", [128, 4, 2, 6, 8])
    vecA = P.scratch("vecA", [DEPTH, 128, 5, 8])
    vecD = P.scratch("vecD", [DEPTH, 128, 10, 8])

    def V(ap, name="v"):
        return Buf(ap, name)

    emit_MOD(P, e["cT"], e["ada_w"], e["abf"], e["n1g"], e["n2g"], e["fg"], modS_d, vecA, vecD, depth)
    tilesA = []
    for h in range(2):
        tilesA.append((2048, 128, 1, h * 128, h))
        tilesA += [(i * 512, 512, 0, CTX + h * 2048 + i * 512, h) for i in range(4)]
    dummy = e["bd1"]
    rows = lambda r0, n=128: V(pT.t[r0:r0 + n, :])
    xs_cur = e["x0"]
    for l in range(depth):
        build_LA(P, {"xT": dummy, "w": V(e["w_in"].t[l]), "vec": V(vecA.t[l]), "pT": pT, "vtok": vtok},
                 {"ntok": TA, "tiles": tilesA, "wcols": OWN_COLS, "pch": OWN_PCH, "vch": OWN_VCH,
                  "xslots": [V(xs_cur.t[0]), V(xs_cur.t[1])]})
        mo = lambda r0: {"ms": mixL, "ms_row": r0}
        build_LR(P, {"r": rows(0), "k": rows(128), "v": rows(256), "lr": rows(384), "g": rows(512, 64),
                     "zw": V(e["zw"].t[l]), "g2c": V(e["g2c"].t[l]), "vec": V(e["rvec"].t[l]), "masks": e["masks"],
                     "bd1": e["bd1"], "bdm": e["bdm"], "rmask": e["rmask"], "o": dummy}, mo(0))
        build_LB(P, {"q": rows(576), "k": rows(704), "vpad": dummy, "bias8": V(e["bias8"].t[l]),
                     "onesab": e["onesab"], "o": dummy}, dict(mo(128), vsrc=V(vtok.t[:, 0:128])))
        build_LC(P, {"q": rows(832), "k2": dummy, "vpad": dummy, "gains": V(e["gains"].t[l]), "cosf": e["cosf"],
                     "sinf": e["sinf"], "prot": e["prot"], "bd64": e["bd64"], "onesab": e["onesab"], "o": dummy},
                 dict(mo(256), ksrc=rows(960, 64), vsrc=V(vtok.t[:, 128:192])))
        build_LP(P, {"x": rows(1024), "invcnt": e["invcnt"], "sel": e["sel"], "pw": V(e["pw"].t[l]),
                     "psc": V(e["psc"].t[l]), "o": dummy}, mo(384))
        if not (DBG.get("no_ex1_l1") and l >= 1):
            P.begin_phase({})
            for h in range(2):
                P.dma(MS.t[h, bass.ds(par, 1), :, :], mixL.t[h][None, :, :], reads=[mixL], writes=[MS])
            P.end_phase()
        if not (DBG.get("no_bar1_l1") and l >= 1):
            nc.all_core_barrier()
        P.begin_phase({})
        P.dma(mloc.t[None, :, :], MS.t.rearrange("h s r t -> h (s r) t")[bass.ds(par, 1), :, :], reads=[MS], writes=[mloc])
        P.dma(xloc.t[None, :, :], xs_cur.t[bass.ds(par, 1), :, :], reads=[xs_cur], writes=[xloc])
        P.end_phase()
        last = (l == depth - 1)
        build_LD(final and last, P, {"xT": xloc, "mT": mloc, "wo": V(e["w_out"].t[l]), "w1": V(e["w1"].t[l]),
                                     "w2": V(e["w2"].t[l]), "vec": V(vecD.t[l]), "oT": (out if last else xout)},
                 {"ntok": NT})
        if not last:
            if not DBG.get("no_xs_write"):
                P.begin_phase({})
                P.dma(XS[l % 2].t[bass.ds(par, 1), :, :], xout.t[None, :, :], reads=[xout], writes=[XS[l % 2]])
                P.end_phase()
            if not DBG.get("no_bar2"):
                nc.all_core_barrier()
            if not DBG.get("no_xs_read"):
                xs_cur = XS[l % 2]
    P.es.close()
    return nc


def fused_inputs_paired(inp):
    C = consts()
    f32 = np.float32
    ic4 = _pool_invcnt()
    rmask = np.ones((128, 256), f32)
    rmask[:, ::64] = 0.0
    common = {
        "ada_w": np.ascontiguousarray(inp["ada_w"]),
        "abf": np.ascontiguousarray(inp["ada_b"].reshape(DEPTH, 6, 8, 128).transpose(3, 0, 1, 2)),
        "n1g": np.ascontiguousarray(inp["norm1_g"].reshape(DEPTH, 8, 128).transpose(0, 2, 1)),
        "n2g": np.ascontiguousarray(inp["norm2_g"].reshape(DEPTH, 8, 128).transpose(0, 2, 1)),
        "fg": fm(inp["final_g"]),
        "w_out": np.ascontiguousarray(inp["w_out"].reshape(DEPTH, 4, 2, 128, D).transpose(0, 2, 1, 3, 4).reshape(DEPTH, D, D)),
        "w1": np.ascontiguousarray(inp["mlp_w1"].reshape(DEPTH, 8, 128, 32, 128).transpose(0, 3, 2, 1, 4)),
        "w2": np.ascontiguousarray(inp["mlp_w2"]),
        "masks": _lr_masks(), "bd1": _bd(1.0), "bdm": _bd(1.0 / 64), "rmask": rmask,
        "onesab": C["onesab"], "cosf": C["cosf"], "sinf": C["sinf"], "prot": C["prot"], "bd64": C["bd64"],
    }
    gains = np.zeros((DEPTH, 128, 2), f32)
    for l in range(DEPTH):
        gains[l, :, 0] = np.tile(inp["gqa_q_gain"][l], 2)
        gains[l, :, 1] = np.tile(inp["gqa_k_gain"][l], 2)
    common["gains"] = gains
    bts = [_na_bias_tables(inp["na_rpb"][l]) for l in range(DEPTH)]
    per_hp = []
    for hp in range(2):
        hs = slice(hp * 128, (hp + 1) * 128)
        zw = np.zeros((DEPTH, 128, 4, 128), f32)
        rvec = np.zeros((DEPTH, 128, 10), f32)
        g2c = np.zeros((DEPTH, 64, 128), f32)
        bias8 = np.zeros((DEPTH, 128, 2, 8, 4, 64), f32)
        pw = np.zeros((DEPTH, 128, 128), f32)
        psc = np.zeros((DEPTH, 128, 1), f32)
        for l in range(DEPTH):
            zw[l, 0:32, 0] = inp["rwkv_w2"][l][0][:, hs]
            zw[l, 32:64, 1] = inp["rwkv_w2"][l][1][:, hs]
            zw[l, 64:96, 2] = inp["rwkv_a2"][l][0][:, hs]
            zw[l, 96:128, 3] = inp["rwkv_a2"][l][1][:, hs]
            v = rvec[l]
            v[:, 0] = inp["rwkv_w0"][l][0][hs]
            v[:, 1] = inp["rwkv_w0"][l][1][hs]
            v[:, 2] = inp["rwkv_a0"][l][0][hs]
            v[:, 3] = inp["rwkv_a0"][l][1][hs]
            v[:, 4] = inp["rwkv_k_k"][l][hs]
            v[:, 5] = inp["rwkv_k_a"][l][hs]
            v[:, 6] = inp["rwkv_r_k"][l].reshape(-1)[hs]
            v[:, 7] = inp["rwkv_gn_w"][l][hs]
            v[:, 8] = inp["rwkv_gn_b"][l][hs]
            g2c[l] = inp["rwkv_g2"][l][:, hs]
            bias8[l] = bts[l][2 * hp:2 * hp + 2].transpose(2, 0, 1, 3, 4)
            pw[l, :64, :64] = inp["pool_w"][l][2 * hp]
            pw[l, 64:, 64:] = inp["pool_w"][l][2 * hp + 1]
            psc[l, :, 0] = inp["pool_scale"][l][hs]
        sel = np.zeros((128, 4), f32)
        sel[:64, 2 * hp] = 1.0
        sel[64:, 2 * hp + 1] = 1.0
        per_hp.append({"zw": zw, "rvec": rvec, "g2c": g2c, "bias8": bias8, "pw": pw, "psc": psc, "sel": sel,
                       "invcnt": np.ascontiguousarray(np.repeat(ic4[2 * hp:2 * hp + 2], 64, axis=0)),
                       "w_in": np.ascontiguousarray(inp["w_in"][:, :, _own_cols(hp)])})
    maps = []
    for i in range(NCORES):
        b, hp = i // 2, i % 2
        m = dict(common)
        m.update(per_hp[hp])
        x0 = np.zeros((2, D, NT), f32)
        for h in range(2):
            x0[h, :, :2048] = inp["x"][b, h * 2048:(h + 1) * 2048].T
            x0[h, :, 2048:] = inp["ctx"][b, h * 128:(h + 1) * 128].T
        m["x0"] = x0
        c2 = np.stack([inp["c"][b], inp["c_ctx"]], 1)
        m["cT"] = np.ascontiguousarray(c2.reshape(8, 128, 2).transpose(1, 0, 2))
        maps.append(m)
    return maps


def kernel(**inp):
    inp = {k: np.asarray(v, np.float32) for k, v in inp.items()}
    maps = fused_inputs(inp)
    res = _run("FUSED", build_FUSED, maps)
    out = np.zeros((B, SEQ, D), np.float32)
    for i in range(NCORES):
        b, hf = i // 2, i % 2
        out[b, hf * 2048:(hf + 1) * 2048] = res[i]["oT"].T
    return out
```

```python
import contextlib
import numpy as np
import concourse.bass as bass
import concourse.mybir as mybir
from concourse.bass_utils import run_bass_kernel_spmd

F32 = mybir.dt.float32
AF = mybir.ActivationFunctionType
ALU = mybir.AluOpType
AX = mybir.AxisListType

D = 1024
B = 4
SEQ = 4096
DEPTH = 4
CTX = 256
NT = 2176
TA = 4352
D_IN = 2496
EPS = 1e-6
NCORES = 8


class Buf:
    __slots__ = ("t", "lw", "rd", "name")

    def __init__(self, t, name=""):
        self.t = t
        self.lw = None
        self.rd = {}
        self.name = name

    def __getitem__(self, idx):
        return self.t[idx]


class Prog:
    NDMA = 16

    def __init__(self, nc):
        self.nc = nc
        self.es = contextlib.ExitStack()
        self.E = {"pe": nc.tensor, "dve": nc.vector, "act": nc.scalar, "pool": nc.gpsimd, "sp": nc.sync}
        self.sem = {}
        self.cnt = {}
        for e in self.E:
            self.sem[e] = self.es.enter_context(nc.semaphore("c_" + e))
            self.cnt[e] = 0
        self.dsem = []
        for i in range(self.NDMA):
            k = "d%d" % i
            self.sem[k] = self.es.enter_context(nc.semaphore(k))
            self.cnt[k] = 0
            self.dsem.append(k)
        self.dnext = 0
        self.seen = {e: {} for e in self.E}
        self.nbuf = 0
        self.ninst = 0
        self.dq = 0
        self.pes = None
        self.bind = {}
        self.standalone = True

    def begin_phase(self, bind):
        self.pes = contextlib.ExitStack()
        self.bind = dict(bind or {})

    def barrier(self):
        for e, E in self.E.items():
            for k, c in self.cnt.items():
                if k == e or c == 0:
                    continue
                if self.seen[e].get(k, 0) < c:
                    E.wait_ge(self.sem[k], c)
                    self.seen[e][k] = c

    def core_barrier(self):
        self.barrier()
        self.nc.all_core_barrier()

    def end_phase(self):
        self.barrier()
        self.pes.close()
        self.pes = None
        self.bind = {}

    def _stack(self):
        return self.pes if self.pes is not None else self.es

    def sb(self, shape, dt=F32, name=None):
        self.nbuf += 1
        name = ("sb%d" % self.nbuf) if name is None else ("%s_%d" % (name, self.nbuf))
        return Buf(self._stack().enter_context(self.nc.sbuf_tensor(name, list(shape), dt)), name)

    def ps(self, shape, dt=F32, name=None):
        self.nbuf += 1
        name = ("ps%d" % self.nbuf) if name is None else ("%s_%d" % (name, self.nbuf))
        return Buf(self._stack().enter_context(self.nc.psum_tensor(name, list(shape), dt)), name)

    def din(self, name, shape, dt=F32):
        if name in self.bind:
            return self.bind[name]
        return Buf(self.nc.dram_tensor(name, list(shape), dt, kind="ExternalInput").ap(), name)

    def dout(self, name, shape, dt=F32):
        if name in self.bind:
            return self.bind[name]
        return Buf(self.nc.dram_tensor(name, list(shape), dt, kind="ExternalOutput").ap(), name)

    def _waits(self, eng, reads, writes):
        w = {}

        def need(tok):
            if tok is None:
                return
            k, v = tok
            if k == eng and eng == "pe":
                return
            if w.get(k, 0) < v:
                w[k] = v

        for b in reads:
            need(b.lw)
        for b in writes:
            need(b.lw)
            for k, v in b.rd.items():
                need((k, v))
        E = self.E[eng]
        for k, v in w.items():
            if self.seen[eng].get(k, 0) < v:
                E.wait_ge(self.sem[k], v)
                self.seen[eng][k] = v

    def op(self, eng, fn, reads=(), writes=()):
        self._waits(eng, reads, writes)
        inst = fn(self.E[eng])
        self.cnt[eng] += 1
        inst.then_inc(self.sem[eng], 1)
        c = self.cnt[eng]
        for b in reads:
            b.rd[eng] = c
        for b in writes:
            b.lw = (eng, c)
            b.rd = {}
        self.ninst += 1
        return inst

    def dma(self, out, in_, reads=(), writes=(), eng=None):
        eng = "sp"
        k = self.dsem[self.dnext]
        self.dnext = (self.dnext + 1) % self.NDMA
        E = self.E[eng]
        if self.cnt[k] > 0 and self.seen[eng].get(k, 0) < self.cnt[k]:
            E.wait_ge(self.sem[k], self.cnt[k])
            self.seen[eng][k] = self.cnt[k]
        self._waits(eng, reads, writes)
        inst = E.dma_start(out=out, in_=in_)
        self.cnt[k] += 16
        inst.then_inc(self.sem[k], 16)
        c = self.cnt[k]
        for b in reads:
            b.rd[k] = c
        for b in writes:
            b.lw = (k, c)
            b.rd = {}
        self.ninst += 1

    def finish(self, bufs):
        if not self.standalone:
            self.end_phase()
            return
        self._waits("sp", bufs, ())
        self.es.close()

    def scratch(self, name, shape, dt=F32):
        return Buf(self.nc.dram_tensor(name, list(shape), dt, kind="Internal").ap(), name)

    def mm(self, ps, out_ap, lhsT_b, lhsT_ap, rhs_b, rhs_ap, start, stop, skip=False):
        self.op("pe", lambda e: e.matmul(out_ap, lhsT=lhsT_ap, rhs=rhs_ap, start=start, stop=stop,
                                         skip_group_check=skip),
                reads=[lhsT_b, rhs_b], writes=[ps])

    def act(self, out_b, out_ap, in_b, in_ap, func, bias=None, scale=1.0, extra_reads=()):
        kw = {}
        if bias is not None:
            kw["bias"] = bias
        self.op("act", lambda e: e.activation(out=out_ap, in_=in_ap, func=func, scale=scale, **kw),
                reads=[in_b] + list(extra_reads), writes=[out_b])

    def tt(self, eng, out_b, out_ap, a_b, a_ap, b_b, b_ap, op):
        self.op(eng, lambda e: e.tensor_tensor(out=out_ap, in0=a_ap, in1=b_ap, op=op),
                reads=[a_b, b_b], writes=[out_b])

    def ts(self, eng, out_b, out_ap, a_b, a_ap, s1, s2, op0, op1=None, extra_reads=()):
        if op1 is None:
            self.op(eng, lambda e: e.tensor_scalar(out=out_ap, in0=a_ap, scalar1=s1, scalar2=None, op0=op0),
                    reads=[a_b] + list(extra_reads), writes=[out_b])
        else:
            self.op(eng, lambda e: e.tensor_scalar(out=out_ap, in0=a_ap, scalar1=s1, scalar2=s2, op0=op0, op1=op1),
                    reads=[a_b] + list(extra_reads), writes=[out_b])

    def stt(self, out_b, out_ap, a_b, a_ap, scalar, b_b, b_ap, op0, op1, extra_reads=()):
        self.op("dve", lambda e: e.scalar_tensor_tensor(out=out_ap, in0=a_ap, scalar=scalar, in1=b_ap, op0=op0, op1=op1),
                reads=[a_b, b_b] + list(extra_reads), writes=[out_b])

    def copy(self, eng, out_b, out_ap, in_b, in_ap):
        if eng == "act":
            self.op("act", lambda e: e.activation(out=out_ap, in_=in_ap, func=AF.Copy), reads=[in_b], writes=[out_b])
        else:
            self.op(eng, lambda e: e.tensor_copy(out=out_ap, in_=in_ap), reads=[in_b], writes=[out_b])

    def memset(self, eng, b, ap, val):
        self.op(eng, lambda e: e.memset(ap, val), reads=[], writes=[b])


def _begin(P, bind):
    if P is None:
        nc = bass.Bass("TRN2", target_bir_lowering=False)
        return Prog(nc), nc
    P.standalone = False
    P.begin_phase(bind)
    return P, P.nc


def _store_mix(P, opt, od, o):
    if "ms" not in opt:
        P.dma(od[:], o[:], reads=[o], writes=[od])
        return
    ML, r0 = opt["ms"], opt["ms_row"]
    for h in range(2):
        P.dma(ML.t[h, r0:r0 + 128, 0:2048], o[:, CTX + h * 2048:CTX + (h + 1) * 2048], reads=[o], writes=[ML])
        P.dma(ML.t[h, r0:r0 + 128, 2048:2176], o[:, h * 128:(h + 1) * 128], reads=[o], writes=[ML])


_PROGS = {}


def _run(key, builder, in_maps):
    if key not in _PROGS:
        _PROGS[key] = builder()
    nc = _PROGS[key]
    res = run_bass_kernel_spmd(nc, in_maps, core_ids=list(range(NCORES)))
    return res.results


class Ring:
    def __init__(self, items):
        self.items = items
        self.i = 0

    def next(self):
        x = self.items[self.i]
        self.i = (self.i + 1) % len(self.items)
        return x


def build_L0(P=None, bind=None, opt=None):
    P, nc = _begin(P, bind)
    opt = opt or {}
    cT = P.din("cT", [128, 8, 8])
    aw = P.din("aw", [1024, 3072])
    ab = P.din("ab", [8, 3072])
    out = P.dout("mod", [8, 3072])
    c_sb = P.sb([128, 8, 8])
    sg = P.sb([128, 8, 8])
    w_sb = P.sb([128, 8, 3072])
    b_sb = P.sb([8, 3072])
    o_sb = P.sb([8, 3072])
    P.dma(c_sb[:], cT[:], reads=[cT], writes=[c_sb])
    P.dma(b_sb[:], ab[:], reads=[ab], writes=[b_sb])
    awv = aw.t.rearrange("(kc p) n -> p kc n", p=128)
    for kc in range(8):
        P.dma(w_sb[:, kc, :], awv[:, kc, :], reads=[aw], writes=[w_sb], eng=("sp" if kc % 2 == 0 else "pool"))
    P.act(sg, sg[:], c_sb, c_sb[:], AF.Sigmoid)
    P.tt("dve", sg, sg[:], sg, sg[:], c_sb, c_sb[:], ALU.mult)
    pss = [P.ps([8, 512]) for _ in range(2)]
    for j in range(6):
        ps = pss[j % 2]
        for kc in range(8):
            P.mm(ps, ps[:], sg, sg[:, kc, :], w_sb, w_sb[:, kc, j * 512:(j + 1) * 512], kc == 0, kc == 7)
        P.tt("dve", o_sb, o_sb[:, j * 512:(j + 1) * 512], ps, ps[:], b_sb, b_sb[:, j * 512:(j + 1) * 512], ALU.add)
    P.dma(out[:], o_sb[:], reads=[o_sb], writes=[out])
    P.finish([out])
    return nc


def _rmsnorm_tile(P, x, Tn, ones, epsb, sq, ps_stat, rs):
    for kc in range(8):
        P.act(sq, sq[:, kc, :Tn], x, x[:, kc, :Tn], AF.Square)
    for kc in range(8):
        P.mm(ps_stat, ps_stat[:, :Tn], ones, ones[:], sq, sq[:, kc, :Tn], kc == 0, kc == 7)
    P.act(rs, rs[:, :Tn], ps_stat, ps_stat[:, :Tn], AF.Sqrt, bias=epsb[:, 0:1], extra_reads=[epsb])
    P.op("dve", lambda e: e.reciprocal(out=rs[:, :Tn], in_=rs[:, :Tn]), reads=[rs], writes=[rs])


PCH = [(0, 128), (128, 128), (256, 128), (384, 128), (512, 128), (640, 128),
       (768, 128), (896, 64),
       (960, 128), (1088, 128), (1216, 128), (1344, 128),
       (1728, 128), (1856, 128), (1984, 128),
       (2240, 128), (2368, 128)]
VB0, VC0 = 1472, 2112


def build_LA(P=None, bind=None, opt=None):
    P, nc = _begin(P, bind)
    opt = opt or {}
    NTK = opt.get("ntok", NT)
    WC = opt.get("wcols", D_IN)
    pch = opt.get("pch", [(r0, nr, r0) for (r0, nr) in PCH])
    vch = opt.get("vch", [(VB0, 256, 0), (VC0, 128, 256)])
    VW = sum(n for (_, n, _) in vch)
    xT = P.din("xT", [1024, NTK])
    w = P.din("w", [1024, WC])
    vec = P.din("vec", [128, 5, 8])
    pT = P.dout("pT", [D_IN, NTK])
    vtok = P.dout("vtok", [NTK, VW])
    w_sb = P.sb([128, 8, WC])
    v_sb = P.sb([128, 5, 8])
    gs = P.sb([128, 2, 8])
    ones = P.sb([128, 128])
    epsb = P.sb([128, 1])
    xs = Ring([P.sb([128, 8, 512]) for _ in range(2)])
    sq = P.sb([128, 8, 512])
    h = P.sb([128, 8, 512])
    rs = P.sb([128, 512])
    stg = Ring([P.sb([128, 512]) for _ in range(4)])
    ps_stat = P.ps([128, 512])
    pso = Ring([P.ps([128, 512]) for _ in range(4)])
    P.memset("pool", ones, ones[:], 1.0 / D)
    P.memset("pool", epsb, epsb[:], EPS)
    P.dma(v_sb[:], vec[:], reads=[vec], writes=[v_sb])
    wv = w.t.rearrange("(kc p) n -> p kc n", p=128)
    for kc in range(8):
        P.dma(w_sb[:, kc, :], wv[:, kc, :], reads=[w], writes=[w_sb], eng=("sp" if kc % 2 == 0 else "pool"))
    P.stt(gs, gs[:, 0, :], v_sb, v_sb[:, 2, :], 1.0, v_sb, v_sb[:, 0, :], ALU.add, ALU.mult)
    P.stt(gs, gs[:, 1, :], v_sb, v_sb[:, 4, :], 1.0, v_sb, v_sb[:, 0, :], ALU.add, ALU.mult)
    if "xslots" in opt:
        xvs = [b_.t.rearrange("(kc p) t -> p kc t", p=128) for b_ in opt["xslots"]]
        xbs = opt["xslots"]
    else:
        xvs = [xT.t.rearrange("(kc p) t -> p kc t", p=128)]
        xbs = [xT]
    tiles = opt.get("tiles", [(i * 512, 512, 0) for i in range(4)] + [(2048, 128, 1)])
    ei = 0
    hr = Ring([h, P.sb([128, 8, 512])])
    eic = [0]

    def front(tl):
        (s0, Tn, isctx) = tl[:3]
        sl = tl[4] if len(tl) > 4 else 0
        x = xs.next()
        h_ = hr.next()
        P.dma(x[:, :, :Tn], xvs[sl][:, :, s0:s0 + Tn], reads=[xbs[sl]], writes=[x])
        _rmsnorm_tile(P, x, Tn, ones, epsb, sq, ps_stat, rs)
        sh = 3 if isctx else 1
        for kc in range(8):
            P.stt(h_, h_[:, kc, :Tn], x, x[:, kc, :Tn], gs[:, isctx, kc:kc + 1], rs, rs[:, :Tn], ALU.mult, ALU.mult,
                  extra_reads=[gs])
            P.act(h_, h_[:, kc, :Tn], h_, h_[:, kc, :Tn], AF.Identity, bias=v_sb[:, sh, kc:kc + 1], extra_reads=[v_sb])
        return h_

    def back(tl, h_, mid):
        (s0, Tn, isctx) = tl[:3]
        t0 = tl[3] if len(tl) > 3 else s0
        for ci, (wc0, nr, r0) in enumerate(pch):
            ps = pso.next()
            for kc in range(8):
                P.mm(ps, ps[:nr, :Tn], w_sb, w_sb[:, kc, wc0:wc0 + nr], h_, h_[:, kc, :Tn], kc == 0, kc == 7)
            st = stg.next()
            P.copy("act" if eic[0] % 2 == 0 else "dve", st, st[:nr, :Tn], ps, ps[:nr, :Tn])
            eic[0] += 1
            P.dma(pT[r0:r0 + nr, t0:t0 + Tn], st[:nr, :Tn], reads=[st], writes=[pT])
            if ci == 5:
                mid()
        for sub in range(Tn // 128):
            ps = pso.next()
            for (wc0, ncol, oc0) in vch:
                for kc in range(8):
                    P.mm(ps, ps[:, oc0:oc0 + ncol], h_, h_[:, kc, sub * 128:(sub + 1) * 128], w_sb, w_sb[:, kc, wc0:wc0 + ncol],
                         kc == 0, kc == 7)
            st = stg.next()
            P.copy("act" if eic[0] % 2 == 0 else "dve", st, st[:, :VW], ps, ps[:, :VW])
            eic[0] += 1
            P.dma(vtok[t0 + sub * 128:t0 + (sub + 1) * 128, :], st[:, :VW], reads=[st], writes=[vtok])

    nxt = {"h": front(tiles[0])}
    for ti, tl in enumerate(tiles):
        cur_h = nxt["h"]

        def mid(ti=ti):
            if ti + 1 < len(tiles):
                nxt["h"] = front(tiles[ti + 1])

        back(tl, cur_h, mid)
    P.finish([pT, vtok])
    return nc


def build_LD(final, P=None, bind=None, opt=None):
    P, nc = _begin(P, bind)
    opt = opt or {}
    NTK = opt.get("ntok", NT)
    xT = P.din("xT", [1024, NTK])
    mT = P.din("mT", [1024, NTK])
    wo = P.din("wo", [1024, 1024])
    w1 = P.din("w1", [32, 128, 8, 128])
    w2 = P.din("w2", [4096, 1024])
    vec = P.din("vec", [128, 10, 8])
    oT = P.dout("oT", [1024, NTK])
    TT = 256
    wo_sb = P.sb([128, 8, 1024])
    v_sb = P.sb([128, 10, 8])
    gs = P.sb([128, 2, 8])
    ones = P.sb([128, 128])
    epsb = P.sb([128, 1])
    xs = Ring([P.sb([128, 8, TT]) for _ in range(2)])
    ms = Ring([P.sb([128, 8, TT]) for _ in range(2)])
    x1 = P.sb([128, 8, TT])
    sq = P.sb([128, 8, TT])
    h = P.sb([128, 8, TT])
    rs = P.sb([128, TT])
    w1r = Ring([P.sb([128, 8, 128]) for _ in range(3)])
    w2r = Ring([P.sb([128, 1024]) for _ in range(3)])
    rl = Ring([P.sb([128, TT]) for _ in range(2)])
    hid = Ring([P.sb([128, TT]) for _ in range(3)])
    osb = Ring([P.sb([128, 8, TT]) for _ in range(2)])
    pso = Ring([P.ps([128, TT]) for _ in range(2)])
    psh = Ring([P.ps([128, TT]) for _ in range(2)])
    acc = P.ps([128, 8, TT])
    P.memset("pool", ones, ones[:], 1.0 / D)
    P.memset("pool", epsb, epsb[:], EPS)
    P.dma(v_sb[:], vec[:], reads=[vec], writes=[v_sb])
    wov = wo.t.rearrange("(kc p) n -> p kc n", p=128)
    for kc in range(8):
        P.dma(wo_sb[:, kc, :], wov[:, kc, :], reads=[wo], writes=[wo_sb], eng=("sp" if kc % 2 == 0 else "pool"))
    P.stt(gs, gs[:, 0, :], v_sb, v_sb[:, 3, :], 1.0, v_sb, v_sb[:, 0, :], ALU.add, ALU.mult)
    P.stt(gs, gs[:, 1, :], v_sb, v_sb[:, 7, :], 1.0, v_sb, v_sb[:, 0, :], ALU.add, ALU.mult)
    xv = xT.t.rearrange("(kc p) t -> p kc t", p=128)
    mv = mT.t.rearrange("(kc p) t -> p kc t", p=128)
    ov = oT.t.rearrange("(kc p) t -> p kc t", p=128)
    w2v = w2.t.rearrange("(mo p) n -> mo p n", p=128)
    tiles = opt.get("tiles", [(i * TT, TT, 0) for i in range(2048 // TT)] + [(2048, 128, 1)])
    x1r = Ring([x1, P.sb([128, 8, TT])])
    hr = Ring([h, P.sb([128, 8, TT])])
    sqr = Ring([sq, P.sb([128, 8, TT])])
    rsr = Ring([rs, P.sb([128, TT])])

    def loads(tile):
        (t0, Tn, isctx) = tile
        x = xs.next()
        m = ms.next()
        P.dma(x[:, :, :Tn], xv[:, :, t0:t0 + Tn], reads=[xT], writes=[x])
        P.dma(m[:, :, :Tn], mv[:, :, t0:t0 + Tn], reads=[mT], writes=[m])
        return (x, m)

    def front(tile, ld):
        (t0, Tn, isctx) = tile
        (x, m) = ld
        x1_, h_ = x1r.next(), hr.next()
        g1 = 5 if isctx else 1
        sh = 6 if isctx else 2
        for oc in range(8):
            ps = pso.next()
            for kc in range(8):
                P.mm(ps, ps[:, :Tn], wo_sb, wo_sb[:, kc, oc * 128:(oc + 1) * 128], m, m[:, kc, :Tn], kc == 0, kc == 7)
            P.stt(x1_, x1_[:, oc, :Tn], ps, ps[:, :Tn], v_sb[:, g1, oc:oc + 1], x, x[:, oc, :Tn], ALU.mult, ALU.add,
                  extra_reads=[v_sb])
        rs_ = rsr.next()
        _rmsnorm_tile(P, x1_, Tn, ones, epsb, sqr.next(), pso.next(), rs_)
        for kc in range(8):
            P.stt(h_, h_[:, kc, :Tn], x1_, x1_[:, kc, :Tn], gs[:, isctx, kc:kc + 1], rs_, rs_[:, :Tn], ALU.mult, ALU.mult,
                  extra_reads=[gs])
            P.act(h_, h_[:, kc, :Tn], h_, h_[:, kc, :Tn], AF.Identity, bias=v_sb[:, sh, kc:kc + 1], extra_reads=[v_sb])
        return (x1_, h_)

    def back(tile, bufs, mid):
        (t0, Tn, isctx) = tile
        (x1_, h_) = bufs
        g2 = 8 if isctx else 4
        pend = None
        for mo in range(32):
            w1c = w1r.next()
            w2c = w2r.next()
            P.dma(w1c[:], w1[mo], reads=[w1], writes=[w1c])
            P.dma(w2c[:], w2v[mo], reads=[w2], writes=[w2c])
            ph = psh.next()
            for kc in range(8):
                P.mm(ph, ph[:, :Tn], w1c, w1c[:, kc, :], h_, h_[:, kc, :Tn], kc == 0, kc == 7)
            r = rl.next()
            P.act(r, r[:, :Tn], ph, ph[:, :Tn], AF.Relu)
            hd = hid.next()
            P.tt("pool" if mo % 2 == 0 else "dve", hd, hd[:, :Tn], r, r[:, :Tn], r, r[:, :Tn], ALU.mult)
            if pend is not None:
                (pmo, pw2, phd) = pend
                for oc in range(8):
                    P.mm(acc, acc[:, oc, :Tn], pw2, pw2[:, oc * 128:(oc + 1) * 128], phd, phd[:, :Tn],
                         pmo == 0 and oc % 2 == 0, False, skip=True)
            pend = (mo, w2c, hd)
            if mo == 15:
                mid()
        (pmo, pw2, phd) = pend
        for oc in range(8):
            P.mm(acc, acc[:, oc, :Tn], pw2, pw2[:, oc * 128:(oc + 1) * 128], phd, phd[:, :Tn], False, True, skip=True)
        o = osb.next()
        for oc in range(8):
            P.stt(o, o[:, oc, :Tn], acc, acc[:, oc, :Tn], v_sb[:, g2, oc:oc + 1], x1_, x1_[:, oc, :Tn], ALU.mult, ALU.add,
                  extra_reads=[v_sb])
        if final:
            rs_ = rsr.next()
            _rmsnorm_tile(P, o, Tn, ones, epsb, sqr.next(), pso.next(), rs_)
            for kc in range(8):
                P.stt(o, o[:, kc, :Tn], o, o[:, kc, :Tn], v_sb[:, 9, kc:kc + 1], rs_, rs_[:, :Tn], ALU.mult, ALU.mult,
                      extra_reads=[v_sb])
        P.dma(ov[:, :, t0:t0 + Tn], o[:, :, :Tn], reads=[o], writes=[oT])

    nxt = {"b": front(tiles[0], loads(tiles[0]))}
    for ti, tile in enumerate(tiles):
        cur_bufs = nxt["b"]
        ld_next = loads(tiles[ti + 1]) if ti + 1 < len(tiles) else None

        def mid(ti=ti, ld_next=ld_next):
            if ti + 1 < len(tiles):
                nxt["b"] = front(tiles[ti + 1], ld_next)

        back(tile, cur_bufs, mid)
    P.finish([oT])
    return nc


def fm(v):
    return np.ascontiguousarray(np.asarray(v, np.float32).reshape(8, 128).T)


def host_L0(inp):
    c8 = np.zeros((8, D), np.float32)
    c8[:4] = inp["c"]
    c8[4] = inp["c_ctx"]
    cT = np.ascontiguousarray(c8.T.reshape(8, 128, 8).transpose(1, 0, 2))
    maps = []
    for i in range(NCORES):
        l, hf = i // 2, i % 2
        maps.append({"cT": cT,
                     "aw": np.ascontiguousarray(inp["ada_w"][l][:, hf * 3072:(hf + 1) * 3072]),
                     "ab": np.ascontiguousarray(np.tile(inp["ada_b"][l][None, hf * 3072:(hf + 1) * 3072], (8, 1)))})
    res = _run("L0", build_L0, maps)
    mod = np.zeros((DEPTH, 8, 6 * D), np.float32)
    for i in range(NCORES):
        l, hf = i // 2, i % 2
        mod[l][:, hf * 3072:(hf + 1) * 3072] = res[i]["mod"]
    return mod.reshape(DEPTH, 8, 6, D)


def host_LA(inp, l, xTs, mod):
    maps = []
    for i in range(NCORES):
        b = i // 2
        vec = np.stack([fm(inp["norm1_g"][l]), fm(mod[l, b, 0]), fm(mod[l, b, 1]), fm(mod[l, 4, 0]), fm(mod[l, 4, 1])], axis=1)
        maps.append({"xT": xTs[i], "w": np.ascontiguousarray(inp["w_in"][l]), "vec": np.ascontiguousarray(vec)})
    return _run("LA", build_LA, maps)


def host_LD(inp, l, xTs, mTs, mod, final):
    w1 = np.ascontiguousarray(inp["mlp_w1"][l].reshape(8, 128, 32, 128).transpose(2, 1, 0, 3))
    maps = []
    for i in range(NCORES):
        b = i // 2
        vec = np.stack([fm(inp["norm2_g"][l]),
                        fm(mod[l, b, 2]), fm(mod[l, b, 3]), fm(mod[l, b, 4]), fm(mod[l, b, 5]),
                        fm(mod[l, 4, 2]), fm(mod[l, 4, 3]), fm(mod[l, 4, 4]), fm(mod[l, 4, 5]),
                        fm(inp["final_g"])], axis=1)
        maps.append({"xT": xTs[i], "mT": mTs[i], "wo": np.ascontiguousarray(inp["w_out"][l]), "w1": w1,
                     "w2": np.ascontiguousarray(inp["mlp_w2"][l]), "vec": np.ascontiguousarray(vec)})
    key = "LDf" if final else "LD"
    return _run(key, lambda: build_LD(final), maps)


def shard_tokens(lat, ctx):
    out = []
    for i in range(NCORES):
        b, hf = i // 2, i % 2
        out.append(np.ascontiguousarray(np.concatenate([lat[b, hf * 2048:(hf + 1) * 2048], ctx[b, hf * 128:(hf + 1) * 128]], 0).T))
    return out


def unshard_tokens(xTs):
    F = xTs[0].shape[0]
    lat = np.zeros((B, SEQ, F), np.float32)
    ctx = np.zeros((B, CTX, F), np.float32)
    for i in range(NCORES):
        b, hf = i // 2, i % 2
        lat[b, hf * 2048:(hf + 1) * 2048] = xTs[i][:, :2048].T
        ctx[b, hf * 128:(hf + 1) * 128] = xTs[i][:, 2048:].T
    return lat, ctx


def _bd(val):
    m = np.zeros((128, 128), np.float32)
    m[:64, :64] = val
    m[64:, 64:] = val
    return m


def _onesab():
    m = np.zeros((128, 2, 128), np.float32)
    m[:, 0, :64] = 1.0
    m[:, 1, 64:] = 1.0
    return m


def _rope_consts():
    t = np.arange(SEQ)
    row = (t // 64).astype(np.float32)
    col = (t % 64).astype(np.float32)
    inv = (np.float32(10000.0) ** (-np.arange(16, dtype=np.float32) / np.float32(16))).astype(np.float32)
    ang = np.concatenate([row[:, None] * inv, col[:, None] * inv], -1).astype(np.float32)
    cos = np.cos(ang).astype(np.float32).T
    sin = np.sin(ang).astype(np.float32).T
    cosf = np.ascontiguousarray(np.tile(cos, (4, 1)))
    sinf = np.ascontiguousarray(np.tile(sin, (4, 1)))
    Pm = np.zeros((128, 128), np.float32)
    for m in range(128):
        if m % 64 < 32:
            Pm[m + 32, m] = -1.0
        else:
            Pm[m - 32, m] = 1.0
    return cosf, sinf, Pm


def build_LC(P=None, bind=None, opt=None):
    P, nc = _begin(P, bind)
    opt = opt or {}
    qd = P.din("q", [128, TA])
    kd = P.din("k2", [128, TA])
    vd = P.din("vpad", [TA, 2, 128])
    gd = P.din("gains", [128, 2])
    cosd = P.din("cosf", [128, SEQ])
    sind = P.din("sinf", [128, SEQ])
    protd = P.din("prot", [128, 128])
    bdd = P.din("bd64", [128, 128])
    oabd = P.din("onesab", [128, 2, 128])
    od = P.dout("o", [128, TA])
    q = P.sb([128, TA])
    k = P.sb([128, TA])
    v = P.sb([128, 34, 2, 128])
    g = P.sb([128, 2])
    cosf = P.sb([128, SEQ])
    sinf = P.sb([128, SEQ])
    prot = P.sb([128, 128])
    bd = P.sb([128, 128])
    oab = P.sb([128, 2, 128])
    epsb = P.sb([128, 1])
    o = P.sb([128, TA])
    t1 = Ring([P.sb([128, 512]) for _ in range(2)])
    t2 = Ring([P.sb([128, 512]) for _ in range(2)])
    pt = Ring([P.sb([128, 512]) for _ in range(3)])
    pacc = [Ring([P.sb([128, 512]) for _ in range(2)]) for _ in range(2)]
    rec = P.sb([128, 512])
    pss = Ring([P.ps([128, 512]) for _ in range(3)])
    pso = Ring([P.ps([128, 512]) for _ in range(2)])
    psm = P.ps([128, 512])
    P.memset("pool", epsb, epsb[:], EPS)
    own = "own" in opt
    if own:
        P.dma(q[:, CTX:CTX + 2048], qd.t[:, bass.ds(opt["own"] * 2048 + CTX, 2048)], reads=[qd], writes=[q])
        cosq = P.sb([128, 2048])
        sinq = P.sb([128, 2048])
        P.dma(cosq[:], cosd.t[:, bass.ds(opt["own"] * 2048, 2048)], reads=[cosd], writes=[cosq])
        P.dma(sinq[:], sind.t[:, bass.ds(opt["own"] * 2048, 2048)], reads=[sind], writes=[sinq])
    else:
        P.dma(q[:], qd[:], reads=[qd], writes=[q])
    if "ksrc" in opt:
        P.dma(k[0:64, :], opt["ksrc"][:], reads=[opt["ksrc"]], writes=[k])
        P.dma(k[64:128, :], opt["ksrc"][:], reads=[opt["ksrc"]], writes=[k])
    else:
        P.dma(k[:], kd[:], reads=[kd], writes=[k], eng="pool")
    P.dma(g[:], gd[:], reads=[gd], writes=[g])
    P.dma(bd[:], bdd[:], reads=[bdd], writes=[bd])
    P.dma(prot[:], protd[:], reads=[protd], writes=[prot])
    P.dma(oab[:], oabd[:], reads=[oabd], writes=[oab])
    P.dma(cosf[:], cosd[:], reads=[cosd], writes=[cosf], eng="pool")
    P.dma(sinf[:], sind[:], reads=[sind], writes=[sinf])
    if "vsrc" in opt:
        vs = opt["vsrc"]
        P.memset("pool", v, v[:], 0.0)
        vsv = vs.t.rearrange("(c p) m -> p c m", p=128)
        P.dma(v[:, :, 0, 0:64], vsv, reads=[vs], writes=[v])
        P.dma(v[:, :, 1, 64:128], vsv, reads=[vs], writes=[v])
    else:
        P.dma(v[:], vd.t.rearrange("(c p) j m -> p c j m", p=128), reads=[vd], writes=[v], eng="pool")
    tiles = [(i * 512, 512) for i in range(8)] + [(4096, 256)]
    for (x, gi) in ((q, 0), (k, 1)):
        qown = own and gi == 0
        for (t0, Tn) in ([(CTX + i * 512, 512) for i in range(4)] if qown else tiles):
            a = t1.next()
            P.act(a, a[:, :Tn], x, x[:, t0:t0 + Tn], AF.Square)
            ps = pss.next()
            P.mm(ps, ps[:, :Tn], bd, bd[:], a, a[:, :Tn], True, True)
            b_ = t2.next()
            P.act(b_, b_[:, :Tn], ps, ps[:, :Tn], AF.Sqrt, bias=epsb[:, 0:1], extra_reads=[epsb])
            P.op("dve", lambda e: e.reciprocal(out=b_[:, :Tn], in_=b_[:, :Tn]), reads=[b_], writes=[b_])
            P.stt(x, x[:, t0:t0 + Tn], x, x[:, t0:t0 + Tn], g[:, gi:gi + 1], b_, b_[:, :Tn], ALU.mult, ALU.mult,
                  extra_reads=[g])
        (ctab, stab) = (cosq, sinq) if qown else (cosf, sinf)
        for i in range(4 if qown else 8):
            c0 = CTX + i * 512
            ps = pss.next()
            P.mm(ps, ps[:], prot, prot[:], x, x[:, c0:c0 + 512], True, True)
            a = t1.next()
            P.tt("pool", a, a[:], x, x[:, c0:c0 + 512], ctab, ctab[:, i * 512:(i + 1) * 512], ALU.mult)
            b_ = t2.next()
            P.tt("dve", b_, b_[:], ps, ps[:], stab, stab[:, i * 512:(i + 1) * 512], ALU.mult)
            P.tt("pool", x, x[:, c0:c0 + 512], a, a[:], b_, b_[:], ALU.add)

    def attend(q0, Tn, kcs):
        po = pso.next()
        pa = [pacc[0].next(), pacc[1].next()]
        items = [(ci, kc, hh) for ci, kc in enumerate(kcs) for hh in range(2)]
        pend = None

        def pv(item, p_, is_first, is_last):
            (ci, kc, hh) = item
            P.mm(po, po[:, :Tn], v, v[:, kc, hh, :], p_, p_[:, :Tn], is_first, is_last)
            if ci > 0:
                P.tt("dve" if hh == 0 else "pool", pa[hh], pa[hh][:, :Tn], pa[hh], pa[hh][:, :Tn], p_, p_[:, :Tn], ALU.add)

        for idx, item in enumerate(items):
            (ci, kc, hh) = item
            ps = pss.next()
            P.mm(ps, ps[:, :Tn], k, k[64 * hh:64 * hh + 64, kc * 128:(kc + 1) * 128],
                 q, q[64 * hh:64 * hh + 64, q0:q0 + Tn], True, True)
            p_ = pa[hh] if ci == 0 else pt.next()
            P.act(p_, p_[:, :Tn], ps, ps[:, :Tn], AF.Exp, scale=0.125)
            if pend is not None:
                pv(pend[0], pend[1], pend[2] == 0, False)
            pend = (item, p_, idx)
        pv(pend[0], pend[1], pend[2] == 0, True)
        P.mm(psm, psm[:, :Tn], oab, oab[:, 0, :], pa[0], pa[0][:, :Tn], True, False)
        P.mm(psm, psm[:, :Tn], oab, oab[:, 1, :], pa[1], pa[1][:, :Tn], False, True)
        P.op("dve", lambda e: e.reciprocal(out=rec[:, :Tn], in_=psm[:, :Tn]), reads=[psm], writes=[rec])
        P.tt("dve", o, o[:, q0:q0 + Tn], po, po[:, :Tn], rec, rec[:, :Tn], ALU.mult)

    if own:
        for qt in range(4):
            attend(CTX + qt * 512, 512, list(range(34)))
        P.dma(opt["osel"][:], o[:, CTX:CTX + 2048], reads=[o], writes=[opt["osel"]])
    else:
        attend(0, 256, [0, 1])
        for qt in range(8):
            attend(CTX + qt * 512, 512, list(range(34)))
        _store_mix(P, opt, od, o)
    P.finish([od])
    return nc


def host_PT(resA):
    Plat, Pctx = unshard_tokens([r["pT"] for r in resA])
    PT = np.concatenate([Pctx, Plat], 1)
    vl = np.zeros((B, SEQ, 384), np.float32)
    vc = np.zeros((B, CTX, 384), np.float32)
    for i in range(NCORES):
        b, hf = i // 2, i % 2
        vl[b, hf * 2048:(hf + 1) * 2048] = resA[i]["vtok"][:2048]
        vc[b, hf * 128:(hf + 1) * 128] = resA[i]["vtok"][2048:]
    VT = np.concatenate([vc, vl], 1)
    return PT, VT


_CONST = {}


def consts():
    if not _CONST:
        cosf, sinf, Pm = _rope_consts()
        _CONST.update(cosf=cosf, sinf=sinf, prot=Pm, bd64=_bd(1.0 / 64), onesab=_onesab())
    return _CONST


def host_LC(inp, l, PT, VT):
    C = consts()
    maps = []
    for i in range(NCORES):
        b, hp = i // 2, i % 2
        qT = np.ascontiguousarray(PT[b][:, 1728 + hp * 128:1728 + (hp + 1) * 128].T)
        kT = PT[b][:, 1984 + hp * 64:1984 + (hp + 1) * 64].T
        k2 = np.ascontiguousarray(np.concatenate([kT, kT], 0))
        vv = VT[b][:, 256 + hp * 64:256 + (hp + 1) * 64]
        vpad = np.zeros((TA, 2, 128), np.float32)
        vpad[:, 0, :64] = vv
        vpad[:, 1, 64:] = vv
        gains = np.stack([np.tile(inp["gqa_q_gain"][l], 2), np.tile(inp["gqa_k_gain"][l], 2)], 1).astype(np.float32)
        maps.append({"q": qT, "k2": k2, "vpad": vpad, "gains": np.ascontiguousarray(gains), "cosf": C["cosf"],
                     "sinf": C["sinf"], "prot": C["prot"], "bd64": C["bd64"], "onesab": C["onesab"]})
    res = _run("LC", build_LC, maps)
    return [r["o"] for r in res]


def _na_row(r):
    rs = min(max(r - 4, 0), 56)
    return rs, rs - r + 7


NA_CLASSES = [7, 6, 5, 4, 3, 2, 1, 0]


def build_LB(P=None, bind=None, opt=None):
    P, nc = _begin(P, bind)
    opt = opt or {}
    qd = P.din("q", [128, TA])
    kd = P.din("k", [128, TA])
    vd = P.din("vpad", [TA + 64, 2, 128])
    bd_ = P.din("bias8", [128, 2, 8, 4, 64])
    oabd = P.din("onesab", [128, 2, 128])
    od = P.dout("o", [128, TA])
    q = P.sb([128, TA])
    k = P.sb([128, TA])
    v0 = P.sb([128, 34, 128])
    v1 = P.sb([128, 34, 128])
    bias = P.sb([128, 8, 4, 2, 64])
    oab = P.sb([128, 2, 128])
    ones = P.sb([128, 128])
    o = P.sb([128, TA])
    qbd = Ring([P.sb([128, 128]) for _ in range(3)])
    tmp = Ring([P.sb([128, 512]) for _ in range(2)])
    psum_r = Ring([P.sb([128, 128]) for _ in range(3)])
    pt = Ring([P.sb([128, 6, 128]) for _ in range(3)])
    ptc = Ring([P.sb([128, 2, 256]) for _ in range(2)])
    rec = P.sb([128, 512])
    pssA = Ring([P.ps([128, 512]) for _ in range(2)])
    pssB = Ring([P.ps([128, 256]) for _ in range(2)])
    pso = Ring([P.ps([128, 512]) for _ in range(2)])
    psm = Ring([P.ps([128, 512]) for _ in range(2)])
    for t_ in qbd.items:
        P.memset("pool", t_, t_[:], 0.0)
    P.memset("pool", ones, ones[:], 1.0)
    P.dma(q[:], qd[:], reads=[qd], writes=[q])
    P.dma(k[:], kd[:], reads=[kd], writes=[k])
    for hh in range(2):
        P.dma(bias[:, :, :, hh, :], bd_[:, hh, :, :, :], reads=[bd_], writes=[bias])
    P.dma(oab[:], oabd[:], reads=[oabd], writes=[oab])
    if "vsrc" in opt:
        vs = opt["vsrc"]
        P.memset("pool", v1, v1[:, 33, :], 0.0)
        P.dma(v0[:], vs.t[0:TA].rearrange("(c p) m -> p c m", p=128), reads=[vs], writes=[v0])
        P.dma(v1[:, 0:33, :], vs.t[64:64 + 33 * 128].rearrange("(c p) m -> p c m", p=128), reads=[vs], writes=[v1])
    else:
        for (vt, lo) in ((v0, 0), (v1, 64)):
            src = vd.t[lo:lo + TA].rearrange("(c p) j m -> p c j m", p=128)
            P.dma(vt[:, :, 0:64], src[:, :, 0, 0:64], reads=[vd], writes=[vt])
            P.dma(vt[:, :, 64:128], src[:, :, 1, 64:128], reads=[vd], writes=[vt])
    po = pso.next()
    pm = psm.next()
    for hh in range(2):
        ps = pssA.next()
        for c in range(2):
            P.mm(ps, ps[:, c * 256:(c + 1) * 256], k, k[64 * hh:64 * hh + 64, c * 128:(c + 1) * 128],
                 q, q[64 * hh:64 * hh + 64, 0:256], True, True)
        p_ = ptc.next()
        P.act(p_, p_[:].rearrange("p a b -> p (a b)"), ps, ps[:], AF.Exp, scale=0.125)
        for c in range(2):
            P.mm(po, po[64 * hh:64 * hh + 64, 0:256], v0, v0[:, c, 64 * hh:64 * hh + 64], p_, p_[:, c, :], c == 0, c == 1)
        for c in range(2):
            P.mm(pm, pm[64 * hh:64 * hh + 64, 0:256], ones, ones[:, 0:64], p_, p_[:, c, :], c == 0, c == 1)
    P.op("dve", lambda e: e.reciprocal(out=rec[:, :256], in_=pm[:, :256]), reads=[pm], writes=[rec])
    P.tt("dve", o, o[:, 0:256], po, po[:, :256], rec, rec[:, :256], ALU.mult)

    def stage1(r):
        rs, off = _na_row(r)
        cl = 7 - off
        qb = qbd.next()
        for hh in range(2):
            pp = slice(64 * hh, 64 * hh + 64)
            P.copy("pool", qb, qb[pp, pp], q, q[pp, CTX + r * 64:CTX + (r + 1) * 64])
        psA = pssA.next()
        psB = pssB.next()
        for c in range(4):
            t0 = CTX + rs * 64 + 128 * c
            P.mm(psA, psA[:, c * 128:(c + 1) * 128], k, k[:, t0:t0 + 128], qb, qb[:], True, True)
        for c in range(2):
            P.mm(psB, psB[:, c * 128:(c + 1) * 128], k, k[:, c * 128:(c + 1) * 128], qb, qb[:], True, True)
        tm = tmp.next()
        P.stt(tm, tm[:], psA, psA[:], 0.125, bias, bias[:, cl, :, :, :].rearrange("p a b c -> p (a b c)"),
              ALU.mult, ALU.add)
        p_ = pt.next()
        P.act(p_, p_[:, 0:4, :].rearrange("p a b -> p (a b)"), tm, tm[:], AF.Exp)
        P.act(p_, p_[:, 4:6, :].rearrange("p a b -> p (a b)"), psB, psB[:], AF.Exp, scale=0.125)
        return p_

    cur = {}

    def stage2(r, p_):
        rs, off = _na_row(r)
        if r % 4 == 0:
            cur["po"] = pso.next()
            cur["pm"] = psm.next()
        po, pm = cur["po"], cur["pm"]
        col = (r % 4) * 128
        for c in range(6):
            if c < 4:
                s_ = 4 + rs + 2 * c
                vt, j = (v0, s_ // 2) if s_ % 2 == 0 else (v1, (s_ - 1) // 2)
            else:
                vt, j = v0, c - 4
            P.mm(po, po[:, col:col + 128], vt, vt[:, j, :], p_, p_[:, c, :], c == 0, c == 5)
        sm = psum_r.next()
        P.op("dve", lambda e: e.tensor_reduce(out=sm[:], in_=p_[:].rearrange("p c q -> p q c"),
                                              axis=AX.X, op=ALU.add), reads=[p_], writes=[sm])
        P.mm(pm, pm[:, col:col + 128], ones, ones[:], sm, sm[:], True, True)
        if r % 4 == 3:
            r0 = r - 3
            P.op("dve", lambda e: e.reciprocal(out=rec[:], in_=pm[:]), reads=[pm], writes=[rec])
            for hh in range(2):
                pp = slice(64 * hh, 64 * hh + 64)
                P.tt("dve", o, o[pp, CTX + r0 * 64:CTX + r0 * 64 + 256].rearrange("p (r q) -> p r q", r=4),
                     po, po[pp, :].rearrange("p (r h q) -> p r h q", r=4, h=2)[:, :, hh, :],
                     rec, rec[pp, :].rearrange("p (r h q) -> p r h q", r=4, h=2)[:, :, hh, :], ALU.mult)

    pend = None
    for r in range(64):
        p_ = stage1(r)
        if pend is not None:
            stage2(*pend)
        pend = (r, p_)
    stage2(*pend)
    _store_mix(P, opt, od, o)
    P.finish([od])
    return nc


def _na_bias_tables(rpb):
    wq = np.arange(64)
    wk = np.arange(64)
    col_start = np.clip(wq - 8, 0, 48)
    in_win = (wk[None, :] >= col_start[:, None]) & (wk[None, :] < col_start[:, None] + 16)
    col_off = np.clip(wk[None, :] - wq[:, None], -15, 15) + 15
    out = np.zeros((4, 8, 128, 4, 64), np.float32)
    for cl in range(8):
        off = 7 - cl
        for c in range(4):
            for half in range(2):
                ro = off + 2 * c + half
                tbl = rpb[:, ro][:, col_off]
                tbl = np.where(in_win[None], tbl, np.float32(-30000.0))
                out[:, cl, half * 64:(half + 1) * 64, c, :] = tbl.transpose(0, 2, 1)
    return out


def host_LB(inp, l, PT, VT):
    C = consts()
    bt = _na_bias_tables(inp["na_rpb"][l])
    maps = []
    for i in range(NCORES):
        b, hp = i // 2, i % 2
        qT = np.ascontiguousarray(PT[b][:, 960 + hp * 128:960 + (hp + 1) * 128].T)
        kT = np.ascontiguousarray(PT[b][:, 1216 + hp * 128:1216 + (hp + 1) * 128].T)
        vv = VT[b][:, hp * 128:(hp + 1) * 128]
        vpad = np.zeros((TA + 64, 2, 128), np.float32)
        vpad[:TA, 0, :64] = vv[:, :64]
        vpad[:TA, 1, 64:] = vv[:, 64:]
        bias8 = np.ascontiguousarray(bt[2 * hp:2 * hp + 2].transpose(2, 0, 1, 3, 4))
        maps.append({"q": qT, "k": kT, "vpad": vpad, "bias8": bias8, "onesab": C["onesab"]})
    res = _run("LB", build_LB, maps)
    return [r["o"] for r in res]


def build_LP(P=None, bind=None, opt=None):
    P, nc = _begin(P, bind)
    opt = opt or {}
    xd = P.din("x", [128, TA])
    icd = P.din("invcnt", [128, TA])
    seld = P.din("sel", [128, 4])
    pwd = P.din("pw", [128, 128])
    pscd = P.din("psc", [128, 1])
    od = P.dout("o", [128, TA])
    PADL = 8
    W = TA + 64
    xp = P.sb([128, W])
    s2 = P.sb([128, W])
    s4 = P.sb([128, W])
    s8 = P.sb([128, W])
    s16 = P.sb([128, W])
    acc = P.sb([128, TA])
    ic = P.sb([128, TA])
    sel = P.sb([128, 4])
    pw = P.sb([128, 128])
    psc = P.sb([128, 1])
    o = P.sb([128, TA])
    pss = Ring([P.ps([128, 512]) for _ in range(2)])
    P.dma(ic[:], icd[:], reads=[icd], writes=[ic], eng="pool")
    P.dma(sel[:], seld[:], reads=[seld], writes=[sel])
    P.dma(pw[:], pwd[:], reads=[pwd], writes=[pw])
    P.dma(psc[:], pscd[:], reads=[pscd], writes=[psc])
    segs = [(0, CTX, 0), (CTX, SEQ, CTX + 24)]
    P.memset("pool", xp, xp[:], 0.0)
    for (t0, L, base) in segs:
        P.dma(xp[:, base + PADL:base + PADL + L], xd[:, t0:t0 + L], reads=[xd], writes=[xp])
    n = W - 2
    P.tt("dve", s2, s2[:, 0:n], xp, xp[:, 0:n], xp, xp[:, 1:n + 1], ALU.add)
    P.tt("dve", s4, s4[:, 0:n - 2], s2, s2[:, 0:n - 2], s2, s2[:, 2:n], ALU.add)
    P.tt("dve", s8, s8[:, 0:n - 6], s4, s4[:, 0:n - 6], s4, s4[:, 4:n - 2], ALU.add)
    P.tt("dve", s16, s16[:, 0:n - 14], s8, s8[:, 0:n - 14], s8, s8[:, 8:n - 6], ALU.add)
    for (t0, L, base) in segs:
        for j, (sw, w) in enumerate(((s2, 2), (s4, 4), (s8, 8), (s16, 16))):
            u0 = base + PADL - w // 2
            if j == 0:
                P.ts("dve", acc, acc[:, t0:t0 + L], sw, sw[:, u0:u0 + L], sel[:, 0:1], None, ALU.mult, extra_reads=[sel])
            else:
                P.stt(acc, acc[:, t0:t0 + L], sw, sw[:, u0:u0 + L], sel[:, j:j + 1], acc, acc[:, t0:t0 + L],
                      ALU.mult, ALU.add, extra_reads=[sel])
        P.tt("dve", acc, acc[:, t0:t0 + L], acc, acc[:, t0:t0 + L], ic, ic[:, t0:t0 + L], ALU.mult)
        P.tt("dve", acc, acc[:, t0:t0 + L], acc, acc[:, t0:t0 + L], xp, xp[:, base + PADL:base + PADL + L], ALU.subtract)
    for (t0, Tn) in [(i * 512, 512) for i in range(8)] + [(4096, 256)]:
        ps = pss.next()
        P.mm(ps, ps[:, :Tn], pw, pw[:], acc, acc[:, t0:t0 + Tn], True, True)
        P.ts("dve", o, o[:, t0:t0 + Tn], ps, ps[:, :Tn], psc[:, 0:1], None, ALU.mult, extra_reads=[psc])
    _store_mix(P, opt, od, o)
    P.finish([od])
    return nc


def _pool_invcnt():
    out = np.zeros((4, TA), np.float32)
    for j, w in enumerate((2, 4, 8, 16)):
        for (t0, L) in ((0, CTX), (CTX, SEQ)):
            t = np.arange(L)
            lo = np.clip(t - w // 2, 0, L)
            hi = np.clip(t - w // 2 + w, 0, L)
            out[j, t0:t0 + L] = (np.float32(1.0) / (hi - lo).astype(np.float32))
    return out


def host_LP(inp, l, PT):
    ic4 = _pool_invcnt()
    maps = []
    for i in range(NCORES):
        b, hp = i // 2, i % 2
        xT = np.ascontiguousarray(PT[b][:, 2240 + hp * 128:2240 + (hp + 1) * 128].T)
        ic = np.ascontiguousarray(np.repeat(ic4[2 * hp:2 * hp + 2], 64, axis=0))
        sel = np.zeros((128, 4), np.float32)
        sel[:64, 2 * hp] = 1.0
        sel[64:, 2 * hp + 1] = 1.0
        pw = np.zeros((128, 128), np.float32)
        pw[:64, :64] = inp["pool_w"][l][2 * hp]
        pw[64:, 64:] = inp["pool_w"][l][2 * hp + 1]
        psc = np.ascontiguousarray(inp["pool_scale"][l][hp * 128:(hp + 1) * 128].reshape(128, 1))
        maps.append({"x": xT, "invcnt": ic, "sel": sel, "pw": pw, "psc": psc})
    res = _run("LP", build_LP, maps)
    return [r["o"] for r in res]


GN_EPS = 1e-5 * 64
NEG_E05 = -0.6065306597126334


def build_LR(P=None, bind=None, opt=None):
    P, nc = _begin(P, bind)
    opt = opt or {}
    rd = P.din("r", [128, TA])
    kd = P.din("k", [128, TA])
    vd = P.din("v", [128, TA])
    lrd = P.din("lr", [128, TA])
    gd = P.din("g", [64, TA])
    zwd = P.din("zw", [128, 4, 128])
    g2d = P.din("g2c", [64, 128])
    vecd = P.din("vec", [128, 10])
    mkd = P.din("masks", [128, 5, 128])
    bd1d = P.din("bd1", [128, 128])
    bdmd = P.din("bdm", [128, 128])
    rmd = P.din("rmask", [128, 256])
    od = P.dout("o", [128, TA])
    T = 256
    zw = P.sb([128, 4, 128])
    g2c = P.sb([64, 128])
    vec = P.sb([128, 12])
    mk = P.sb([128, 5, 128])
    bd1 = P.sb([128, 128])
    bdm = P.sb([128, 128])
    rmask = P.sb([128, T])
    gneps = P.sb([128, 1])
    y = P.sb([128, TA])
    o = P.sb([128, TA])
    for (sbt, dt_) in ((zw, zwd), (g2c, g2d), (mk, mkd), (bd1, bd1d), (bdm, bdmd), (rmask, rmd)):
        P.dma(sbt[:], dt_[:], reads=[dt_], writes=[sbt])
    P.dma(vec[:, 0:10], vecd[:], reads=[vecd], writes=[vec])
    P.memset("pool", gneps, gneps[:], GN_EPS)
    P.ts("dve", vec, vec[:, 10:11], vec, vec[:, 5:6], -1.0, 1.0, ALU.mult, ALU.add)
    P.ts("dve", vec, vec[:, 11:12], vec, vec[:, 5:6], -2.0, 2.0, ALU.mult, ALU.add)
    ident = Buf(mk.t[:, 4, :], "ident")

    def blk(n, shape=(128, T), k=2):
        return Ring([P.sb(list(shape), name="%s%d" % (n, i)) for i in range(k)])

    rb_r, kb_r, vb_r, lr_r = blk("rb"), blk("kb"), blk("vb"), blk("lrb")
    gb_r = blk("gb", (64, T))
    lrA_r, lw_r, a_r, kk_r, sq_r, rn_r, kap_r, nb_r, tmp_r, kmod_r = (blk(n) for n in
        ("lrA", "lw", "a", "kk", "sq", "rn", "kap", "nb", "tmp", "kmod"))
    gp_r, gm_r, en_r, ep_r, ekm_r = (blk(n) for n in ("gp", "gm", "en", "ep", "ekm"))
    def bdr(n):
        tiles = [P.sb([128, 128], name="%s%d" % (n, i)) for i in range(2)]
        for t_ in tiles:
            P.memset("pool", t_, t_[:], 0.0)
        return Ring(tiles)
    def bd4(n):
        tiles = [P.sb([128, 128], name="%s%d" % (n, i)) for i in range(4)]
        for t_ in tiles:
            P.memset("pool", t_, t_[:], 0.0)
        return tiles
    kt, nbt, kkt, rt, Kh, Bh, vB = (bd4(n) for n in ("kt", "nbt", "kkt", "rt", "Kh", "Bh", "vB"))
    def sq4(n):
        return [P.sb([128, 128], name="%s%d" % (n, i)) for i in range(4)]
    def sq128(n, k=2):
        return Ring([P.sb([128, 128], name="%s%d" % (n, i)) for i in range(k)])
    Nn, NTn, Akk, Ark, Arb, Vbd, ktb, Khb, Bhb, X1, WT = (sq4(n) for n in
        ("N", "NT", "Akk", "Ark", "Arb", "Vbd", "ktb", "Khb", "Bhb", "X1", "WT"))
    TTr = [sq128("TT%d_" % j, 3) for j in range(4)]
    Pwr = [sq128("Pw%d_" % j, 3) for j in range(4)]
    PTwr = [sq128("PTw%d_" % j, 3) for j in range(4)]
    U_r, H_r = sq128("U"), sq128("H")
    yt_r = Ring([P.sb([128, 64], name="yt%d" % i) for i in range(2)])
    pq = Ring([P.ps([128, 128], name="pq%d" % i) for i in range(6)])
    pbig = Ring([P.ps([128, T], name="pbig%d" % i) for i in range(2)])

    def mmq(lhsT, rhs):
        ps = pq.next()
        P.mm(ps, ps[:], lhsT, lhsT[:], rhs, rhs[:], True, True)
        return ps

    def transp(x):
        ps = pq.next()
        P.op("pe", lambda e: e.transpose(ps[:], x[:], ident[:]), reads=[x, ident], writes=[ps])
        return ps

    def load_block(bi, need_g=False):
        t0 = bi * T
        rb, kb, vb, lrb = rb_r.next(), kb_r.next(), vb_r.next(), lr_r.next()
        P.dma(rb[:], rd[:, t0:t0 + T], reads=[rd], writes=[rb])
        P.dma(kb[:], kd[:, t0:t0 + T], reads=[kd], writes=[kb], eng="pool")
        P.dma(vb[:], vd[:, t0:t0 + T], reads=[vd], writes=[vb])
        P.dma(lrb[:], lrd[:, t0:t0 + T], reads=[lrd], writes=[lrb], eng="pool")
        gb = None
        if need_g:
            gb = gb_r.next()
            P.dma(gb[:], gd[:, t0:t0 + T], reads=[gd], writes=[gb])
        return rb, kb, vb, lrb, gb

    def sigm_lowrank(lhs_idx, rhs, bias_col, out):
        ps = pbig.next()
        P.mm(ps, ps[:], zw, zw[:, lhs_idx, :], rhs, rhs[:], True, True)
        P.act(out, out[:], ps, ps[:], AF.Sigmoid, bias=vec[:, bias_col:bias_col + 1], extra_reads=[vec])

    for d in range(2):
        H = H_r.next()
        P.memset("pool", H, H[:], 0.0)
        if d == 0:
            msl, msu, mui = 0, 1, 2
            blocks = list(range(17))
        else:
            msl, msu, mui = 1, 0, 3
            blocks = [0] + list(range(16, 0, -1))
        def pre_block(bi):
            rb, kb, vb, lrb, _ = load_block(bi)
            lrA = lrA_r.next()
            P.act(lrA, lrA[0:64, :], lrb, lrb[0:64, :], AF.Tanh)
            P.copy("pool", lrA, lrA[64:128, :], lrb, lrb[64:128, :])
            lw = lw_r.next()
            sigm_lowrank(d, lrA, d, lw)
            P.ts("dve", lw, lw[:], lw, lw[:], NEG_E05, None, ALU.mult)
            a = a_r.next()
            sigm_lowrank(2 + d, lrA, 2 + d, a)
            kk = kk_r.next()
            P.ts("dve", kk, kk[:], kb, kb[:], vec[:, 4:5], None, ALU.mult, extra_reads=[vec])
            sq = sq_r.next()
            P.act(sq, sq[:], kk, kk[:], AF.Square)
            ps = pbig.next()
            P.mm(ps, ps[:], bd1, bd1[:], sq, sq[:], True, True)
            rn = rn_r.next()
            P.act(rn, rn[:], ps, ps[:], AF.Sqrt)
            P.ts("dve", rn, rn[:], rn, rn[:], 1e-12, None, ALU.max)
            P.op("dve", lambda e: e.reciprocal(out=rn[:], in_=rn[:]), reads=[rn], writes=[rn])
            kap = kap_r.next()
            P.tt("dve", kap, kap[:], kk, kk[:], rn, rn[:], ALU.mult)
            nb = nb_r.next()
            P.stt(nb, nb[:], kap, kap[:], -1.0, a, a[:], ALU.mult, ALU.mult)
            tmp = tmp_r.next()
            P.ts("dve", tmp, tmp[:], a, a[:], vec[:, 5:6], vec[:, 10:11], ALU.mult, ALU.add, extra_reads=[vec])
            kmod = kmod_r.next()
            P.tt("pool", kmod, kmod[:], kb, kb[:], tmp, tmp[:], ALU.mult)
            gp = gp_r.next()
            P.op("dve", lambda e: e.tensor_tensor_scan(out=gp[:], data0=rmask[:], data1=lw[:], initial=0.0,
                                                       op0=ALU.mult, op1=ALU.add),
                 reads=[rmask, lw], writes=[gp])
            if d == 1:
                g2_ = gm_r.next()
                for c in range(4):
                    P.ts("dve", g2_, g2_[:, c * 64:(c + 1) * 64], gp, gp[:, c * 64:(c + 1) * 64],
                         gp[:, c * 64 + 63:c * 64 + 64], -1.0, ALU.subtract, ALU.mult)
                gfull = gp_r.next()
                P.tt("dve", gfull, gfull[:], g2_, g2_[:], lw, lw[:], ALU.add)
            else:
                gfull = gp
            gm = gm_r.next()
            P.tt("pool", gm, gm[:], gfull, gfull[:], lw, lw[:], ALU.subtract)
            en, ep, ekm = en_r.next(), ep_r.next(), ekm_r.next()
            P.act(en, en[:], gfull, gfull[:], AF.Exp, scale=-1.0)
            P.act(ep, ep[:], gfull, gfull[:], AF.Exp)
            P.act(ekm, ekm[:], gm, gm[:], AF.Exp)
            return dict(bi=bi, rb=rb, vb=vb, kap=kap, nb=nb, kmod=kmod, en=en, ep=ep, ekm=ekm)

        def prep_block(cur):
            bi, rb, vb, kap, nb, kmod, en, ep, ekm = (cur[k_] for k_ in ('bi', 'rb', 'vb', 'kap', 'nb', 'kmod', 'en', 'ep', 'ekm'))
            chunks = [0, 1, 2, 3] if d == 0 else [3, 2, 1, 0]
            J = range(4)
            css = [slice(c * 64, (c + 1) * 64) for c in chunks]
            gcols = [(c * 64 + 63) if d == 0 else (c * 64) for c in chunks]
            for j in J:
                cs, gcol = css[j], gcols[j]
                for hf in range(2):
                    pp = slice(64 * hf, 64 * hf + 64)
                    P.tt("pool", kt[j], kt[j][pp, pp], kap, kap[pp, cs], ekm, ekm[pp, cs], ALU.mult)
                    P.tt("pool", nbt[j], nbt[j][pp, pp], nb, nb[pp, cs], en, en[pp, cs], ALU.mult)
                    P.tt("pool", kkt[j], kkt[j][pp, pp], kmod, kmod[pp, cs], en, en[pp, cs], ALU.mult)
                    P.tt("pool", rt[j], rt[j][pp, pp], rb, rb[pp, cs], ep, ep[pp, cs], ALU.mult)
                    P.stt(Kh[j], Kh[j][pp, pp], kmod, kmod[pp, cs], ep[pp, gcol:gcol + 1], en, en[pp, cs], ALU.mult, ALU.mult,
                          extra_reads=[ep])
                    P.stt(Bh[j], Bh[j][pp, pp], nb, nb[pp, cs], ep[pp, gcol:gcol + 1], en, en[pp, cs], ALU.mult, ALU.mult,
                          extra_reads=[ep])
                    P.copy("act", vB[j], vB[j][pp, pp], vb, vb[pp, cs])
            for j in J:
                ps = mmq(kt[j], nbt[j])
                P.tt("dve", Nn[j], Nn[j][:], ps, ps[:], mk, mk[:, msl, :], ALU.mult)
            for j in J:
                ps = mmq(nbt[j], kt[j])
                P.tt("dve", NTn[j], NTn[j][:], ps, ps[:], mk, mk[:, msu, :], ALU.mult)
            TT = [TTr[j].next() for j in J]
            for j in J:
                P.tt("pool", TT[j], TT[j][:], NTn[j], NTn[j][:], ident, ident[:], ALU.add)
            Pc = [Nn[j] for j in J]
            PTc = [NTn[j] for j in J]
            for i in range(5):
                P2 = [Pwr[j].next() for j in J]
                for j in J:
                    ps = mmq(PTc[j], Pc[j])
                    P.copy("act", P2[j], P2[j][:], ps, ps[:])
                if i < 4:
                    PT2 = [PTwr[j].next() for j in J]
                    for j in J:
                        ps2 = mmq(Pc[j], PTc[j])
                        P.copy("dve", PT2[j], PT2[j][:], ps2, ps2[:])
                    PTc = PT2
                Pc = P2
                TTn = [TTr[j].next() for j in J]
                for j in J:
                    ps = mmq(Pc[j], TT[j])
                    P.tt("dve", TTn[j], TTn[j][:], ps, ps[:], TT[j], TT[j][:], ALU.add)
                TT = TTn
            for j in J:
                ps = mmq(kkt[j], kt[j])
                P.tt("dve", Akk[j], Akk[j][:], ps, ps[:], mk, mk[:, msu, :], ALU.mult)
            for j in J:
                ps = mmq(kkt[j], rt[j])
                P.tt("dve", Ark[j], Ark[j][:], ps, ps[:], mk, mk[:, mui, :], ALU.mult)
            for j in J:
                ps = mmq(nbt[j], rt[j])
                P.tt("dve", Arb[j], Arb[j][:], ps, ps[:], mk, mk[:, mui, :], ALU.mult)
            for (srcs, dsts) in ((vB, Vbd), (kt, ktb), (Kh, Khb), (Bh, Bhb)):
                for j in J:
                    ps = transp(srcs[j])
                    P.copy("act", dsts[j], dsts[j][:], ps, ps[:])
            for j in J:
                ps = mmq(Akk[j], Vbd[j])
                P.copy("act", X1[j], X1[j][:], ps, ps[:])
            for j in J:
                ps = mmq(ktb[j], TT[j])
                P.copy("dve", WT[j], WT[j][:], ps, ps[:])
            return dict(chunks=chunks, gcols=gcols, TT=TT)

        def state_block(cur, pr, H):
            bi, rb, vb, kap, nb, kmod, en, ep, ekm = (cur[k_] for k_ in ('bi', 'rb', 'vb', 'kap', 'nb', 'kmod', 'en', 'ep', 'ekm'))
            chunks, gcols, TT = pr['chunks'], pr['gcols'], pr['TT']
            J = range(4)
            for j in J:
                c, gcol = chunks[j], gcols[j]
                U = U_r.next()
                ps = pq.next()
                P.mm(ps, ps[:], TT[j], TT[j][:], X1[j], X1[j][:], True, False)
                P.mm(ps, ps[:], WT[j], WT[j][:], H, H[:], False, True)
                P.copy("act", U, U[:], ps, ps[:])
                psy = pq.next()
                P.mm(psy, psy[:], H, H[:], rt[j], rt[j][:], True, False)
                P.mm(psy, psy[:], Vbd[j], Vbd[j][:], Ark[j], Ark[j][:], False, False)
                P.mm(psy, psy[:], U, U[:], Arb[j], Arb[j][:], False, True)
                psh = pq.next()
                P.mm(psh, psh[:], Khb[j], Khb[j][:], Vbd[j], Vbd[j][:], True, False)
                P.mm(psh, psh[:], Bhb[j], Bhb[j][:], U, U[:], False, True)
                Hn = H_r.next()
                P.stt(Hn, Hn[:], H, H[:], ep[:, gcol:gcol + 1], psh, psh[:], ALU.mult, ALU.add, extra_reads=[ep])
                H = Hn
                g0 = bi * T + c * 64
                yt = yt_r.next()
                if d == 0:
                    P.copy("act", yt, yt[:], psy, psy[:, 0:64])
                    P.tt("dve", y, y[:, g0:g0 + 64], psy, psy[:, 64:128], yt, yt[:], ALU.add)
                else:
                    P.tt("dve", yt, yt[:], psy, psy[:, 0:64], y, y[:, g0:g0 + 64], ALU.add)
                    P.tt("dve", y, y[:, g0:g0 + 64], psy, psy[:, 64:128], yt, yt[:], ALU.add)
            return H

        cur = pre_block(blocks[0])
        for idx in range(len(blocks)):
            pr = prep_block(cur)
            nxt = pre_block(blocks[idx + 1]) if idx + 1 < len(blocks) else None
            H = state_block(cur, pr, H)
            cur = nxt
    for bi in range(17):
        t0 = bi * T
        rb, kb, vb, lrb, gb = load_block(bi, need_g=True)
        af, ab = a_r.next(), tmp_r.next()
        sigm_lowrank(2, lrb, 2, af)
        sigm_lowrank(3, lrb, 3, ab)
        s = kk_r.next()
        P.tt("dve", s, s[:], af, af[:], ab, ab[:], ALU.add)
        P.ts("dve", s, s[:], s, s[:], vec[:, 5:6], vec[:, 11:12], ALU.mult, ALU.add, extra_reads=[vec])
        P.tt("dve", s, s[:], s, s[:], kb, kb[:], ALU.mult)
        P.stt(s, s[:], s, s[:], vec[:, 6:7], rb, rb[:], ALU.mult, ALU.mult, extra_reads=[vec])
        ps = pbig.next()
        P.mm(ps, ps[:], bd1, bd1[:], s, s[:], True, True)
        bon = kap_r.next()
        P.tt("dve", bon, bon[:], ps, ps[:], vb, vb[:], ALU.mult)
        ps = pbig.next()
        P.mm(ps, ps[:], bdm, bdm[:], y, y[:, t0:t0 + T], True, True)
        yc = nb_r.next()
        P.tt("dve", yc, yc[:], y, y[:, t0:t0 + T], ps, ps[:], ALU.subtract)
        sq = sq_r.next()
        P.act(sq, sq[:], yc, yc[:], AF.Square)
        ps = pbig.next()
        P.mm(ps, ps[:], bdm, bdm[:], sq, sq[:], True, True)
        rn = rn_r.next()
        P.act(rn, rn[:], ps, ps[:], AF.Sqrt, bias=gneps[:, 0:1], extra_reads=[gneps])
        P.op("dve", lambda e: e.reciprocal(out=rn[:], in_=rn[:]), reads=[rn], writes=[rn])
        P.tt("dve", yc, yc[:], yc, yc[:], rn, rn[:], ALU.mult)
        P.ts("dve", yc, yc[:], yc, yc[:], vec[:, 7:8], vec[:, 8:9], ALU.mult, ALU.add, extra_reads=[vec])
        P.tt("pool", yc, yc[:], yc, yc[:], bon, bon[:], ALU.add)
        sg = gb_r.next()
        P.act(sg, sg[:], gb, gb[:], AF.Sigmoid)
        ps = pbig.next()
        P.mm(ps, ps[:], g2c, g2c[:], sg, sg[:], True, True)
        P.tt("dve", o, o[:, t0:t0 + T], ps, ps[:], yc, yc[:], ALU.mult)
    _store_mix(P, opt, od, o)
    P.finish([od])
    return nc


def _lr_masks():
    tt_, ss_ = np.meshgrid(np.arange(128), np.arange(128), indexing="ij")
    same = (tt_ // 64) == (ss_ // 64)
    sl = (same & (ss_ < tt_)).astype(np.float32)
    su = np.ascontiguousarray(sl.T)
    ui = (same & (tt_ <= ss_)).astype(np.float32)
    li = np.ascontiguousarray(ui.T)
    return np.ascontiguousarray(np.stack([sl, su, ui, li, np.eye(128, dtype=np.float32)], 1))


def host_LR(inp, l, PT):
    masks = _lr_masks()
    rmask = np.ones((128, 256), np.float32)
    rmask[:, ::64] = 0.0
    maps = []
    for i in range(NCORES):
        b, hp = i // 2, i % 2
        hs = slice(hp * 128, (hp + 1) * 128)
        T_ = PT[b]
        zw = np.zeros((128, 4, 128), np.float32)
        zw[0:32, 0] = inp["rwkv_w2"][l][0][:, hs]
        zw[32:64, 1] = inp["rwkv_w2"][l][1][:, hs]
        zw[64:96, 2] = inp["rwkv_a2"][l][0][:, hs]
        zw[96:128, 3] = inp["rwkv_a2"][l][1][:, hs]
        vec = np.zeros((128, 10), np.float32)
        vec[:, 0] = inp["rwkv_w0"][l][0][hs]
        vec[:, 1] = inp["rwkv_w0"][l][1][hs]
        vec[:, 2] = inp["rwkv_a0"][l][0][hs]
        vec[:, 3] = inp["rwkv_a0"][l][1][hs]
        vec[:, 4] = inp["rwkv_k_k"][l][hs]
        vec[:, 5] = inp["rwkv_k_a"][l][hs]
        vec[:, 6] = inp["rwkv_r_k"][l].reshape(-1)[hs]
        vec[:, 7] = inp["rwkv_gn_w"][l][hs]
        vec[:, 8] = inp["rwkv_gn_b"][l][hs]
        maps.append({"r": np.ascontiguousarray(T_[:, 0 + hp * 128:0 + (hp + 1) * 128].T),
                     "k": np.ascontiguousarray(T_[:, 256 + hp * 128:256 + (hp + 1) * 128].T),
                     "v": np.ascontiguousarray(T_[:, 512 + hp * 128:512 + (hp + 1) * 128].T),
                     "lr": np.ascontiguousarray(T_[:, 768:896].T),
                     "g": np.ascontiguousarray(T_[:, 896:960].T),
                     "zw": zw, "g2c": np.ascontiguousarray(inp["rwkv_g2"][l][:, hs]), "vec": vec,
                     "masks": masks, "bd1": _bd(1.0), "bdm": _bd(1.0 / 64), "rmask": rmask})
    res = _run("LR", build_LR, maps)
    return [r["o"] for r in res]


def kernel_unfused(**inp):
    inp = {k: np.asarray(v, np.float32) for k, v in inp.items()}
    mod = host_L0(inp)
    xTs = shard_tokens(inp["x"], inp["ctx"])
    for l in range(DEPTH):
        resA = host_LA(inp, l, xTs, mod)
        PT, VT = host_PT(resA)
        parts = [host_LR(inp, l, PT), host_LB(inp, l, PT, VT), host_LC(inp, l, PT, VT), host_LP(inp, l, PT)]
        mix = np.zeros((B, TA, D), np.float32)
        for i in range(NCORES):
            b, hp = i // 2, i % 2
            for j in range(4):
                mix[b][:, j * 256 + hp * 128:j * 256 + (hp + 1) * 128] = parts[j][i].T
        mTs = shard_tokens(mix[:, CTX:], mix[:, :CTX])
        res = host_LD(inp, l, xTs, mTs, mod, l == DEPTH - 1)
        xTs = [r["oT"] for r in res]
    out, _ = unshard_tokens(xTs)
    return out


def emit_MOD(P, cT, ada_w, abf, n1g, n2g, fg, modS_d, vecA, vecD, depth):
    P.begin_phase({})
    c_sb = P.sb([128, 8, 2])
    sg = P.sb([128, 8, 2])
    ab = P.sb([128, 4, 6, 8])
    modS = P.sb([128, 4, 2, 6, 8])
    wr = Ring([P.sb([128, 8, 1024]) for _ in range(2)])
    pss = Ring([P.ps([128, 8, 2]) for _ in range(2)])
    P.dma(c_sb[:], cT[:], reads=[cT], writes=[c_sb])
    P.dma(ab[:], abf[:], reads=[abf], writes=[ab])
    P.act(sg, sg[:], c_sb, c_sb[:], AF.Sigmoid)
    P.tt("dve", sg, sg[:], sg, sg[:], c_sb, c_sb[:], ALU.mult)
    for l in range(depth):
        awv = ada_w.t[l].rearrange("(kc p) n -> p kc n", p=128)
        for j in range(6):
            wb = wr.next()
            P.dma(wb[:], awv[:, :, j * 1024:(j + 1) * 1024], reads=[ada_w], writes=[wb])
            ps = pss.next()
            for fc in range(8):
                for kc in range(8):
                    P.mm(ps, ps[:, fc, :], wb, wb[:, kc, fc * 128:(fc + 1) * 128], sg, sg[:, kc, :], kc == 0, kc == 7)
            for r in range(2):
                P.tt("dve", modS, modS[:, l, r, j, :], ps, ps[:, :, r], ab, ab[:, l, j, :], ALU.add)
    P.dma(modS_d[:], modS[:], reads=[modS], writes=[modS_d])
    for l in range(depth):
        P.dma(vecA[l, :, 0, :], n1g[l], reads=[n1g], writes=[vecA])
        P.dma(vecA[l, :, 1:3, :], modS_d[:, l, 0, 0:2, :], reads=[modS_d], writes=[vecA])
        P.dma(vecA[l, :, 3:5, :], modS_d[:, l, 1, 0:2, :], reads=[modS_d], writes=[vecA])
        P.dma(vecD[l, :, 0, :], n2g[l], reads=[n2g], writes=[vecD])
        P.dma(vecD[l, :, 1:5, :], modS_d[:, l, 0, 2:6, :], reads=[modS_d], writes=[vecD])
        P.dma(vecD[l, :, 5:9, :], modS_d[:, l, 1, 2:6, :], reads=[modS_d], writes=[vecD])
        P.dma(vecD[l, :, 9, :], fg[:], reads=[fg], writes=[vecD])
    P.end_phase()


def build_FUSED(depth=DEPTH, final=True):
    nc = bass.Bass("TRN2", target_bir_lowering=False)
    P = Prog(nc)
    P.standalone = False
    e = {}
    for name, shape in (("xT0", [1024, TA]), ("cT", [128, 8, 2]), ("ada_w", [DEPTH, 1024, 6144]), ("abf", [128, 4, 6, 8]),
                        ("n1g", [DEPTH, 128, 8]), ("n2g", [DEPTH, 128, 8]), ("fg", [128, 8]),
                        ("w_in", [DEPTH, 1024, D_IN]), ("w_out", [DEPTH, 1024, 1024]),
                        ("w1", [DEPTH, 32, 128, 8, 128]), ("w2", [DEPTH, 4096, 1024]),
                        ("zw", [DEPTH, 2, 128, 4, 128]), ("g2c", [DEPTH, 2, 64, 128]), ("rvec", [DEPTH, 2, 128, 10]),
                        ("masks", [128, 5, 128]), ("bd1", [128, 128]), ("bdm", [128, 128]), ("rmask", [128, 256]),
                        ("bias8", [DEPTH, 2, 128, 2, 8, 4, 64]), ("onesab", [128, 2, 128]), ("gains", [DEPTH, 128, 2]),
                        ("cosf", [128, SEQ]), ("sinf", [128, SEQ]), ("prot", [128, 128]), ("bd64", [128, 128]),
                        ("invcnt", [2, 128, TA]), ("sel", [2, 128, 4]), ("pw", [DEPTH, 2, 128, 128]),
                        ("psc", [DEPTH, 2, 128, 1])):
        e[name] = P.din(name, shape)
    out = P.dout("oT", [1024, 2048])
    par = nc.sync.partition_id() % 2
    xsel = P.scratch("xsel", [1024, 2048])
    msel = P.scratch("msel", [1024, 2048])
    xs = [P.scratch("xA", [1024, TA]), P.scratch("xB", [1024, TA])]
    pT = P.scratch("pTs", [D_IN, TA])
    vtok = P.scratch("vtoks", [TA, 384])
    mixT = P.scratch("mixTs", [1024, TA])
    modS_d = P.scratch("modS", [128, 4, 2, 6, 8])
    vecA = P.scratch("vecA", [DEPTH, 128, 5, 8])
    vecD = P.scratch("vecD", [DEPTH, 128, 10, 8])

    def V(ap, name="v"):
        return Buf(ap, name)

    emit_MOD(P, e["cT"], e["ada_w"], e["abf"], e["n1g"], e["n2g"], e["fg"], modS_d, vecA, vecD, depth)
    xcur = e["xT0"]
    tilesA = [(0, 256, 1)] + [(CTX + i * 512, 512, 0) for i in range(8)]
    tilesD = [(0, 256, 1)] + [(CTX + i * 256, 256, 0) for i in range(16)]
    dummy = e["bd1"]
    for l in range(depth):
        build_LA(P, {"xT": xcur, "w": V(e["w_in"].t[l]), "vec": V(vecA.t[l]), "pT": pT, "vtok": vtok},
                 {"ntok": TA, "tiles": tilesA})
        for hp in range(2):
            rows = lambda r0, n=128: V(pT.t[r0:r0 + n, :])
            build_LR(P, {"r": rows(hp * 128), "k": rows(256 + hp * 128), "v": rows(512 + hp * 128),
                         "lr": rows(768), "g": rows(896, 64), "zw": V(e["zw"].t[l, hp]), "g2c": V(e["g2c"].t[l, hp]),
                         "vec": V(e["rvec"].t[l, hp]), "masks": e["masks"], "bd1": e["bd1"], "bdm": e["bdm"],
                         "rmask": e["rmask"], "o": V(mixT.t[hp * 128:(hp + 1) * 128, :])})
            build_LB(P, {"q": rows(960 + hp * 128), "k": rows(1216 + hp * 128), "vpad": dummy,
                         "bias8": V(e["bias8"].t[l, hp]), "onesab": e["onesab"],
                         "o": V(mixT.t[256 + hp * 128:256 + (hp + 1) * 128, :])},
                     {"vsrc": V(vtok.t[:, hp * 128:(hp + 1) * 128])})
            optC = {"ksrc": V(pT.t[1984 + hp * 64:1984 + (hp + 1) * 64, :]),
                    "vsrc": V(vtok.t[:, 256 + hp * 64:256 + (hp + 1) * 64])}
            if l == depth - 1:
                optC.update(own=par, osel=V(msel.t[512 + hp * 128:512 + (hp + 1) * 128, :]))
            build_LC(P, {"q": rows(1728 + hp * 128), "k2": dummy, "vpad": dummy, "gains": V(e["gains"].t[l]),
                         "cosf": e["cosf"], "sinf": e["sinf"], "prot": e["prot"], "bd64": e["bd64"],
                         "onesab": e["onesab"], "o": V(mixT.t[512 + hp * 128:512 + (hp + 1) * 128, :])}, optC)
            build_LP(P, {"x": rows(2240 + hp * 128), "invcnt": V(e["invcnt"].t[hp]), "sel": V(e["sel"].t[hp]),
                         "pw": V(e["pw"].t[l, hp]), "psc": V(e["psc"].t[l, hp]),
                         "o": V(mixT.t[768 + hp * 128:768 + (hp + 1) * 128, :])})
        last = (l == depth - 1)
        if last:
            P.begin_phase({})
            off = par * 2048 + CTX
            for r0 in range(0, 1024, 256):
                P.dma(xsel.t[r0:r0 + 256, :], xcur.t[r0:r0 + 256, bass.ds(off, 2048)], reads=[xcur], writes=[xsel])
                if r0 != 512:
                    P.dma(msel.t[r0:r0 + 256, :], mixT.t[r0:r0 + 256, bass.ds(off, 2048)], reads=[mixT], writes=[msel])
            P.end_phase()
            build_LD(final, P, {"xT": xsel, "mT": msel, "wo": V(e["w_out"].t[l]), "w1": V(e["w1"].t[l]),
                                "w2": V(e["w2"].t[l]), "vec": V(vecD.t[l]), "oT": out},
                     {"ntok": 2048, "tiles": [(i * 256, 256, 0) for i in range(8)]})
        else:
            xnext = xs[l % 2]
            build_LD(False, P, {"xT": xcur, "mT": mixT, "wo": V(e["w_out"].t[l]), "w1": V(e["w1"].t[l]),
                                "w2": V(e["w2"].t[l]), "vec": V(vecD.t[l]), "oT": xnext},
                     {"ntok": TA, "tiles": tilesD})
            xcur = xnext
    P.es.close()
    return nc


def fused_inputs(inp):
    C = consts()
    f32 = np.float32
    ic4 = _pool_invcnt()
    invcnt = np.stack([np.repeat(ic4[2 * hp:2 * hp + 2], 64, axis=0) for hp in range(2)], 0).astype(f32)
    sel = np.zeros((2, 128, 4), f32)
    for hp in range(2):
        sel[hp, :64, 2 * hp] = 1.0
        sel[hp, 64:, 2 * hp + 1] = 1.0
    rmask = np.ones((128, 256), f32)
    rmask[:, ::64] = 0.0
    zw = np.zeros((DEPTH, 2, 128, 4, 128), f32)
    rvec = np.zeros((DEPTH, 2, 128, 10), f32)
    g2c = np.zeros((DEPTH, 2, 64, 128), f32)
    bias8 = np.zeros((DEPTH, 2, 128, 2, 8, 4, 64), f32)
    pw = np.zeros((DEPTH, 2, 128, 128), f32)
    psc = np.zeros((DEPTH, 2, 128, 1), f32)
    gains = np.zeros((DEPTH, 128, 2), f32)
    for l in range(DEPTH):
        bt = _na_bias_tables(inp["na_rpb"][l])
        gains[l, :, 0] = np.tile(inp["gqa_q_gain"][l], 2)
        gains[l, :, 1] = np.tile(inp["gqa_k_gain"][l], 2)
        for hp in range(2):
            hs = slice(hp * 128, (hp + 1) * 128)
            zw[l, hp, 0:32, 0] = inp["rwkv_w2"][l][0][:, hs]
            zw[l, hp, 32:64, 1] = inp["rwkv_w2"][l][1][:, hs]
            zw[l, hp, 64:96, 2] = inp["rwkv_a2"][l][0][:, hs]
            zw[l, hp, 96:128, 3] = inp["rwkv_a2"][l][1][:, hs]
            v = rvec[l, hp]
            v[:, 0] = inp["rwkv_w0"][l][0][hs]
            v[:, 1] = inp["rwkv_w0"][l][1][hs]
            v[:, 2] = inp["rwkv_a0"][l][0][hs]
            v[:, 3] = inp["rwkv_a0"][l][1][hs]
            v[:, 4] = inp["rwkv_k_k"][l][hs]
            v[:, 5] = inp["rwkv_k_a"][l][hs]
            v[:, 6] = inp["rwkv_r_k"][l].reshape(-1)[hs]
            v[:, 7] = inp["rwkv_gn_w"][l][hs]
            v[:, 8] = inp["rwkv_gn_b"][l][hs]
            g2c[l, hp] = inp["rwkv_g2"][l][:, hs]
            bias8[l, hp] = bt[2 * hp:2 * hp + 2].transpose(2, 0, 1, 3, 4)
            pw[l, hp, :64, :64] = inp["pool_w"][l][2 * hp]
            pw[l, hp, 64:, 64:] = inp["pool_w"][l][2 * hp + 1]
            psc[l, hp, :, 0] = inp["pool_scale"][l][hs]
    shared = {
        "ada_w": np.ascontiguousarray(inp["ada_w"]),
        "abf": np.ascontiguousarray(inp["ada_b"].reshape(DEPTH, 6, 8, 128).transpose(3, 0, 1, 2)),
        "n1g": np.ascontiguousarray(inp["norm1_g"].reshape(DEPTH, 8, 128).transpose(0, 2, 1)),
        "n2g": np.ascontiguousarray(inp["norm2_g"].reshape(DEPTH, 8, 128).transpose(0, 2, 1)),
        "fg": fm(inp["final_g"]),
        "w_in": np.ascontiguousarray(inp["w_in"]), "w_out": np.ascontiguousarray(inp["w_out"]),
        "w1": np.ascontiguousarray(inp["mlp_w1"].reshape(DEPTH, 8, 128, 32, 128).transpose(0, 3, 2, 1, 4)),
        "w2": np.ascontiguousarray(inp["mlp_w2"]),
        "zw": zw, "g2c": g2c, "rvec": rvec, "masks": _lr_masks(), "bd1": _bd(1.0), "bdm": _bd(1.0 / 64), "rmask": rmask,
        "bias8": bias8, "onesab": C["onesab"], "gains": gains, "cosf": C["cosf"], "sinf": C["sinf"], "prot": C["prot"],
        "bd64": C["bd64"], "invcnt": invcnt, "sel": sel, "pw": pw, "psc": psc,
    }
    maps = []
    for i in range(NCORES):
        b = i // 2
        m = dict(shared)
        m["xT0"] = np.ascontiguousarray(np.concatenate([inp["ctx"][b], inp["x"][b]], 0).T)
        c2 = np.stack([inp["c"][b], inp["c_ctx"]], 1)
        m["cT"] = np.ascontiguousarray(c2.reshape(8, 128, 2).transpose(1, 0, 2))
        maps.append(m)
    return maps


OWN_COLS = 1344
OWN_PCH = [(0, 128, 0), (128, 128, 128), (256, 128, 256), (384, 128, 384), (512, 64, 512),
           (576, 128, 576), (704, 128, 704), (832, 128, 832), (960, 64, 960), (1024, 128, 1024)]
OWN_VCH = [(1152, 128, 0), (1280, 64, 128)]


def _own_cols(hp):
    segs = [(hp * 128, 128), (256 + hp * 128, 128), (512 + hp * 128, 128), (768, 128), (896, 64),
            (960 + hp * 128, 128), (1216 + hp * 128, 128), (1728 + hp * 128, 128), (1984 + hp * 64, 64),
            (2240 + hp * 128, 128), (1472 + hp * 128, 128), (2112 + hp * 64, 64)]
    return np.concatenate([np.arange(a, a + n) for a, n in segs])


DBG = {}


def build_FUSED_paired(depth=DEPTH, final=True):
    nc = bass.Bass("TRN2", target_bir_lowering=False, num_devices=NCORES)
    P = Prog(nc)
    P.standalone = False
    e = {}
    for name, shape in (("x0", [2, 1024, NT]), ("cT", [128, 8, 2]), ("ada_w", [DEPTH, 1024, 6144]), ("abf", [128, 4, 6, 8]),
                        ("n1g", [DEPTH, 128, 8]), ("n2g", [DEPTH, 128, 8]), ("fg", [128, 8]),
                        ("w_in", [DEPTH, 1024, OWN_COLS]), ("w_out", [DEPTH, 1024, 1024]),
                        ("w1", [DEPTH, 32, 128, 8, 128]), ("w2", [DEPTH, 4096, 1024]),
                        ("zw", [DEPTH, 128, 4, 128]), ("g2c", [DEPTH, 64, 128]), ("rvec", [DEPTH, 128, 10]),
                        ("masks", [128, 5, 128]), ("bd1", [128, 128]), ("bdm", [128, 128]), ("rmask", [128, 256]),
                        ("bias8", [DEPTH, 128, 2, 8, 4, 64]), ("onesab", [128, 2, 128]), ("gains", [DEPTH, 128, 2]),
                        ("cosf", [128, SEQ]), ("sinf", [128, SEQ]), ("prot", [128, 128]), ("bd64", [128, 128]),
                        ("invcnt", [128, TA]), ("sel", [128, 4]), ("pw", [DEPTH, 128, 128]),
                        ("psc", [DEPTH, 128, 1])):
        e[name] = P.din(name, shape)
    out = P.dout("oT", [1024, NT])
    par = nc.sync.partition_id() % 2
    XS = [Buf(nc.dram_tensor("XS%d" % i, [2, 1024, NT], F32, kind="Internal", addr_space="Shared").ap(), "XS%d" % i)
          for i in range(2)]
    MS = Buf(nc.dram_tensor("MS", [2, 2, 512, NT], F32, kind="Internal", addr_space="Shared").ap(), "MS")
    pT = P.scratch("pTs", [1152, TA])
    vtok = P.scratch("vtoks", [TA, 192])
    mixL = P.scratch("mixL", [2, 512, NT])
    mloc = P.scratch("mloc", [1024, NT])
    xloc = P.scratch("xloc", [1024, NT])
    xout = P.scratch("xout", [1024, NT])
    modS_d = P.scratch("modS", [128, 4, 2, 6, 8])
    vecA = P.scratch("vecA", [DEPTH, 128, 5, 8])
    vecD = P.scratch("vecD", [DEPTH, 128, 10, 8])

    def V(ap, name="v"):
        return Buf(ap, name)

    emit_MOD(P, e["cT"], e["ada_w"], e["abf"], e["n1g"], e["n2g"], e["fg"], modS_d, vecA, vecD, depth)
    tilesA = []
    for h in range(2):
        tilesA.append((2048, 128, 1, h * 128, h))
        tilesA += [(i * 512, 512, 0, CTX + h * 2048 + i * 512, h) for i in range(4)]
    dummy = e["bd1"]
    rows = lambda r0, n=128: V(pT.t[r0:r0 + n, :])
    xs_cur = e["x0"]
    for l in range(depth):
        build_LA(P, {"xT": dummy, "w": V(e["w_in"].t[l]), "vec": V(vecA.t[l]), "pT": pT, "vtok": vtok},
                 {"ntok": TA, "tiles": tilesA, "wcols": OWN_COLS, "pch": OWN_PCH, "vch": OWN_VCH,
                  "xslots": [V(xs_cur.t[0]), V(xs_cur.t[1])]})
        mo = lambda r0: {"ms": mixL, "ms_row": r0}
        build_LR(P, {"r": rows(0), "k": rows(128), "v": rows(256), "lr": rows(384), "g": rows(512, 64),
                     "zw": V(e["zw"].t[l]), "g2c": V(e["g2c"].t[l]), "vec": V(e["rvec"].t[l]), "masks": e["masks"],
                     "bd1": e["bd1"], "bdm": e["bdm"], "rmask": e["rmask"], "o": dummy}, mo(0))
        build_LB(P, {"q": rows(576), "k": rows(704), "vpad": dummy, "bias8": V(e["bias8"].t[l]),
                     "onesab": e["onesab"], "o": dummy}, dict(mo(128), vsrc=V(vtok.t[:, 0:128])))
        build_LC(P, {"q": rows(832), "k2": dummy, "vpad": dummy, "gains": V(e["gains"].t[l]), "cosf": e["cosf"],
                     "sinf": e["sinf"], "prot": e["prot"], "bd64": e["bd64"], "onesab": e["onesab"], "o": dummy},
                 dict(mo(256), ksrc=rows(960, 64), vsrc=V(vtok.t[:, 128:192])))
        build_LP(P, {"x": rows(1024), "invcnt": e["invcnt"], "sel": e["sel"], "pw": V(e["pw"].t[l]),
                     "psc": V(e["psc"].t[l]), "o": dummy}, mo(384))
        if not (DBG.get("no_ex1_l1") and l >= 1):
            P.begin_phase({})
            for h in range(2):
                P.dma(MS.t[h, bass.ds(par, 1), :, :], mixL.t[h][None, :, :], reads=[mixL], writes=[MS])
            P.end_phase()
        if not (DBG.get("no_bar1_l1") and l >= 1):
            nc.all_core_barrier()
        P.begin_phase({})
        P.dma(mloc.t[None, :, :], MS.t.rearrange("h s r t -> h (s r) t")[bass.ds(par, 1), :, :], reads=[MS], writes=[mloc])
        P.dma(xloc.t[None, :, :], xs_cur.t[bass.ds(par, 1), :, :], reads=[xs_cur], writes=[xloc])
        P.end_phase()
        last = (l == depth - 1)
        build_LD(final and last, P, {"xT": xloc, "mT": mloc, "wo": V(e["w_out"].t[l]), "w1": V(e["w1"].t[l]),
                                     "w2": V(e["w2"].t[l]), "vec": V(vecD.t[l]), "oT": (out if last else xout)},
                 {"ntok": NT})
        if not last:
            if not DBG.get("no_xs_write"):
                P.begin_phase({})
                P.dma(XS[l % 2].t[bass.ds(par, 1), :, :], xout.t[None, :, :], reads=[xout], writes=[XS[l % 2]])
                P.end_phase()
            if not DBG.get("no_bar2"):
                nc.all_core_barrier()
            if not DBG.get("no_xs_read"):
                xs_cur = XS[l % 2]
    P.es.close()
    return nc


def fused_inputs_paired(inp):
    C = consts()
    f32 = np.float32
    ic4 = _pool_invcnt()
    rmask = np.ones((128, 256), f32)
    rmask[:, ::64] = 0.0
    common = {
        "ada_w": np.ascontiguousarray(inp["ada_w"]),
        "abf": np.ascontiguousarray(inp["ada_b"].reshape(DEPTH, 6, 8, 128).transpose(3, 0, 1, 2)),
        "n1g": np.ascontiguousarray(inp["norm1_g"].reshape(DEPTH, 8, 128).transpose(0, 2, 1)),
        "n2g": np.ascontiguousarray(inp["norm2_g"].reshape(DEPTH, 8, 128).transpose(0, 2, 1)),
        "fg": fm(inp["final_g"]),
        "w_out": np.ascontiguousarray(inp["w_out"].reshape(DEPTH, 4, 2, 128, D).transpose(0, 2, 1, 3, 4).reshape(DEPTH, D, D)),
        "w1": np.ascontiguousarray(inp["mlp_w1"].reshape(DEPTH, 8, 128, 32, 128).transpose(0, 3, 2, 1, 4)),
        "w2": np.ascontiguousarray(inp["mlp_w2"]),
        "masks": _lr_masks(), "bd1": _bd(1.0), "bdm": _bd(1.0 / 64), "rmask": rmask,
        "onesab": C["onesab"], "cosf": C["cosf"], "sinf": C["sinf"], "prot": C["prot"], "bd64": C["bd64"],
    }
    gains = np.zeros((DEPTH, 128, 2), f32)
    for l in range(DEPTH):
        gains[l, :, 0] = np.tile(inp["gqa_q_gain"][l], 2)
        gains[l, :, 1] = np.tile(inp["gqa_k_gain"][l], 2)
    common["gains"] = gains
    bts = [_na_bias_tables(inp["na_rpb"][l]) for l in range(DEPTH)]
    per_hp = []
    for hp in range(2):
        hs = slice(hp * 128, (hp + 1) * 128)
        zw = np.zeros((DEPTH, 128, 4, 128), f32)
        rvec = np.zeros((DEPTH, 128, 10), f32)
        g2c = np.zeros((DEPTH, 64, 128), f32)
        bias8 = np.zeros((DEPTH, 128, 2, 8, 4, 64), f32)
        pw = np.zeros((DEPTH, 128, 128), f32)
        psc = np.zeros((DEPTH, 128, 1), f32)
        for l in range(DEPTH):
            zw[l, 0:32, 0] = inp["rwkv_w2"][l][0][:, hs]
            zw[l, 32:64, 1] = inp["rwkv_w2"][l][1][:, hs]
            zw[l, 64:96, 2] = inp["rwkv_a2"][l][0][:, hs]
            zw[l, 96:128, 3] = inp["rwkv_a2"][l][1][:, hs]
            v = rvec[l]
            v[:, 0] = inp["rwkv_w0"][l][0][hs]
            v[:, 1] = inp["rwkv_w0"][l][1][hs]
            v[:, 2] = inp["rwkv_a0"][l][0][hs]
            v[:, 3] = inp["rwkv_a0"][l][1][hs]
            v[:, 4] = inp["rwkv_k_k"][l][hs]
            v[:, 5] = inp["rwkv_k_a"][l][hs]
            v[:, 6] = inp["rwkv_r_k"][l].reshape(-1)[hs]
            v[:, 7] = inp["rwkv_gn_w"][l][hs]
            v[:, 8] = inp["rwkv_gn_b"][l][hs]
            g2c[l] = inp["rwkv_g2"][l][:, hs]
            bias8[l] = bts[l][2 * hp:2 * hp + 2].transpose(2, 0, 1, 3, 4)
            pw[l, :64, :64] = inp["pool_w"][l][2 * hp]
            pw[l, 64:, 64:] = inp["pool_w"][l][2 * hp + 1]
            psc[l, :, 0] = inp["pool_scale"][l][hs]
        sel = np.zeros((128, 4), f32)
        sel[:64, 2 * hp] = 1.0
        sel[64:, 2 * hp + 1] = 1.0
        per_hp.append({"zw": zw, "rvec": rvec, "g2c": g2c, "bias8": bias8, "pw": pw, "psc": psc, "sel": sel,
                       "invcnt": np.ascontiguousarray(np.repeat(ic4[2 * hp:2 * hp + 2], 64, axis=0)),
                       "w_in": np.ascontiguousarray(inp["w_in"][:, :, _own_cols(hp)])})
    maps = []
    for i in range(NCORES):
        b, hp = i // 2, i % 2
        m = dict(common)
        m.update(per_hp[hp])
        x0 = np.zeros((2, D, NT), f32)
        for h in range(2):
            x0[h, :, :2048] = inp["x"][b, h * 2048:(h + 1) * 2048].T
            x0[h, :, 2048:] = inp["ctx"][b, h * 128:(h + 1) * 128].T
        m["x0"] = x0
        c2 = np.stack([inp["c"][b], inp["c_ctx"]], 1)
        m["cT"] = np.ascontiguousarray(c2.reshape(8, 128, 2).transpose(1, 0, 2))
        maps.append(m)
    return maps


def kernel(**inp):
    inp = {k: np.asarray(v, np.float32) for k, v in inp.items()}
    maps = fused_inputs(inp)
    res = _run("FUSED", build_FUSED, maps)
    out = np.zeros((B, SEQ, D), np.float32)
    for i in range(NCORES):
        b, hf = i // 2, i % 2
        out[b, hf * 2048:(hf + 1) * 2048] = res[i]["oT"].T
    return out
```
